# Optimizing a Trainium2 kernel written in Bass

```python
import math
import jax, jax.numpy as jnp
from jax import lax
import numpy as np

D_MODEL = 1024
BATCH = 8
SEQ = 8192
DEPTH = 2
DEC_BATCH = 16
DEC_SEQ = 16
PAST_LEN = 4096

CHUNK = 64
N_A = DEPTH // 2
N_B = DEPTH - N_A
CONV_W = 3
D_CONV = D_MODEL
N_HEADS = 16
KV_HEADS = 4
GROUP = N_HEADS // KV_HEADS
HEAD_DIM = 64
ATT_WIDTH = N_HEADS * HEAD_DIM
WINDOW = 128
WIN_CHUNKS = WINDOW // CHUNK
ROT_DIM = HEAD_DIM // 4
ROPE_THETA = 500000.0
EPS = 1e-6
NEG = -1e30

kernel_name = "yoco_shortconv_swa_sink_stream_step"


def rms_norm(x, g):
    xf = x.astype(jnp.float32)
    y = xf * lax.rsqrt(jnp.mean(xf * xf, axis=-1, keepdims=True) + EPS)
    return (y * g.astype(jnp.float32)).astype(x.dtype)


def adaln(c, w, b, n):
    m = jax.nn.silu(c) @ w + b
    return jnp.split(m[:, None, :], n, axis=-1)


def rotary(x, pos):
    half = ROT_DIM // 2
    inv = 1.0 / (ROPE_THETA ** (jnp.arange(0, ROT_DIM, 2, dtype=jnp.float32) / ROT_DIM))
    ang = pos.astype(jnp.float32)[:, None] * inv[None, :]
    cos = jnp.cos(ang)[None, :, None, :]
    sin = jnp.sin(ang)[None, :, None, :]
    xr = x[..., :ROT_DIM].astype(jnp.float32)
    x1, x2 = xr[..., :half], xr[..., half:]
    rot = jnp.concatenate([x1 * cos - x2 * sin, x2 * cos + x1 * sin], axis=-1)
    return jnp.concatenate([rot.astype(x.dtype), x[..., ROT_DIM:]], axis=-1)


def short_conv_mixer(h, conv_cache, w_in, conv_w, w_out):
    seq = h.shape[1]
    b_gate, c_gate, u, z = jnp.split(h @ w_in, 4, axis=-1)
    vp = jnp.concatenate([conv_cache, c_gate * u], axis=1)
    y = vp[:, 0:seq] * conv_w[0]
    for tap in range(1, CONV_W):
        y = y + vp[:, tap:tap + seq] * conv_w[tap]
    out = (b_gate * y * jax.nn.silu(z)) @ w_out
    return out, vp[:, -(CONV_W - 1):]


def shared_kv(x, c, pos, g_kv, w_mod_kv, b_mod_kv, w_kv):
    bsz, seq, _ = x.shape
    shift, scale = adaln(c, w_mod_kv, b_mod_kv, 2)
    h = rms_norm(x, g_kv) * (1.0 + scale) + shift
    k, v = jnp.split(h @ w_kv, 2, axis=-1)
    k = rotary(k.reshape(bsz, seq, KV_HEADS, HEAD_DIM), pos)
    return k, v.reshape(bsz, seq, KV_HEADS, HEAD_DIM)


def sink_attention(q, k, v, sink, valid):
    s = jnp.einsum('bnqhgd,bnkhd->bnhgqk', q, k).astype(jnp.float32) * (HEAD_DIM ** -0.5)
    if valid is not None:
        s = jnp.where(valid, s, NEG)
    sk = sink.astype(jnp.float32).reshape(1, 1, KV_HEADS, GROUP, 1, 1)
    m = jnp.maximum(jnp.max(s, axis=-1, keepdims=True), sk)
    p = jnp.exp(s - m)
    p = p / (jnp.sum(p, axis=-1, keepdims=True) + jnp.exp(sk - m))
    return jnp.einsum('bnhgqk,bnkhd->bnqhgd', p.astype(v.dtype), v)


def swa_prompt(q, k, v, sink):
    bsz, seq = q.shape[0], q.shape[1]
    nc = seq // CHUNK
    padw = ((0, 0), (WINDOW, 0), (0, 0), (0, 0))
    kp = jnp.pad(k, padw).reshape(bsz, nc + WIN_CHUNKS, CHUNK, KV_HEADS, HEAD_DIM)
    vp = jnp.pad(v, padw).reshape(bsz, nc + WIN_CHUNKS, CHUNK, KV_HEADS, HEAD_DIM)
    kb = jnp.concatenate([kp[:, j:j + nc] for j in range(WIN_CHUNKS + 1)], axis=2)
    vb = jnp.concatenate([vp[:, j:j + nc] for j in range(WIN_CHUNKS + 1)], axis=2)
    kpos = (jnp.arange(nc)[:, None] - WIN_CHUNKS) * CHUNK + jnp.arange((WIN_CHUNKS + 1) * CHUNK)[None, :]
    valid = (kpos >= 0)[None, :, None, None, None, :]
    qb = q.reshape(bsz, nc, CHUNK, KV_HEADS, GROUP, HEAD_DIM)
    o = sink_attention(qb, kb, vb, sink, valid)
    return o.reshape(bsz, seq, ATT_WIDTH)


def swa_step(q, k_all, v_all, sink):
    bsz, seq = q.shape[0], q.shape[1]
    qb = q.reshape(bsz, 1, seq, KV_HEADS, GROUP, HEAD_DIM)
    o = sink_attention(qb, k_all[:, None], v_all[:, None], sink, None)
    return o.reshape(bsz, seq, ATT_WIDTH)


def trunk(x, c, pos, conv_states, k_past, v_past, g_a, w_mod_a, b_mod_a, w_in_a, conv_w_a, w_out_a,
          g_kv, w_mod_kv, b_mod_kv, w_kv, g_b, w_mod_b, b_mod_b, w_qz_b, w_o_b, sinks_b, g_final):
    bsz, seq, _ = x.shape
    new_conv = []
    k = v = k_all = v_all = None
    for layer in range(DEPTH):
        if layer < N_A:
            shift, scale, gate = adaln(c, w_mod_a[layer], b_mod_a[layer], 3)
            h = rms_norm(x, g_a[layer]) * (1.0 + scale) + shift
            out, st = short_conv_mixer(h, conv_states[layer], w_in_a[layer], conv_w_a[layer], w_out_a[layer])
            x = x + gate * out
            new_conv.append(st)
            if layer == N_A - 1:
                k, v = shared_kv(x, c, pos, g_kv, w_mod_kv, b_mod_kv, w_kv)
                if k_past is None:
                    k_all, v_all = k, v
                else:
                    k_all = jnp.concatenate([k_past, k], axis=1)
                    v_all = jnp.concatenate([v_past, v], axis=1)
        else:
            j = layer - N_A
            shift, scale, gate = adaln(c, w_mod_b[j], b_mod_b[j], 3)
            h = rms_norm(x, g_b[j]) * (1.0 + scale) + shift
            q, z = jnp.split(h @ w_qz_b[j], 2, axis=-1)
            q = rotary(q.reshape(bsz, seq, N_HEADS, HEAD_DIM), pos)
            if k_past is None:
                o = swa_prompt(q, k, v, sinks_b[j])
            else:
                o = swa_step(q, k_all, v_all, sinks_b[j])
            x = x + gate * ((o * jax.nn.silu(z)) @ w_o_b[j])
    return rms_norm(x, g_final), jnp.stack(new_conv, axis=0), k_all[:, -WINDOW:], v_all[:, -WINDOW:]


def setup_inputs(seed: int = 0) -> dict:
    key = jax.random.key(seed)
    ks = jax.random.split(key, 24)
    f32 = jnp.float32
    D = D_MODEL
    sd = D ** -0.5
    nrm = lambda k, shape, s: jax.random.normal(k, shape, f32) * s
    return {
        'x_prompt': nrm(ks[0], (BATCH, SEQ, D), 1.0),
        'x_sample': nrm(ks[1], (DEC_BATCH, DEC_SEQ, D), 1.0),
        'c_prompt': nrm(ks[2], (BATCH, D), 1.0),
        'c_sample': nrm(ks[3], (DEC_BATCH, D), 1.0),
        'state_conv': nrm(ks[4], (N_A, DEC_BATCH, CONV_W - 1, D_CONV), 1.0),
        'cache_k': nrm(ks[5], (DEC_BATCH, WINDOW, KV_HEADS, HEAD_DIM), 1.0),
        'cache_v': nrm(ks[6], (DEC_BATCH, WINDOW, KV_HEADS, HEAD_DIM), 1.0),
        'g_a': 1.0 + nrm(ks[7], (N_A, D), 0.02),
        'w_mod_a': nrm(ks[8], (N_A, D, 3 * D), 0.5 * sd),
        'b_mod_a': nrm(ks[9], (N_A, 3 * D), 0.02),
        'w_in_a': nrm(ks[10], (N_A, D, 4 * D_CONV), sd),
        'conv_w_a': nrm(ks[11], (N_A, CONV_W, D_CONV), CONV_W ** -0.5),
        'w_out_a': nrm(ks[12], (N_A, D_CONV, D), D_CONV ** -0.5),
        'g_kv': 1.0 + nrm(ks[13], (D,), 0.02),
        'w_mod_kv': nrm(ks[14], (D, 2 * D), 0.5 * sd),
        'b_mod_kv': nrm(ks[15], (2 * D,), 0.02),
        'w_kv': nrm(ks[16], (D, 2 * KV_HEADS * HEAD_DIM), sd),
        'g_b': 1.0 + nrm(ks[17], (N_B, D), 0.02),
        'w_mod_b': nrm(ks[18], (N_B, D, 3 * D), 0.5 * sd),
        'b_mod_b': nrm(ks[19], (N_B, 3 * D), 0.02),
        'w_qz_b': nrm(ks[20], (N_B, D, 2 * ATT_WIDTH), sd),
        'w_o_b': nrm(ks[21], (N_B, ATT_WIDTH, D), ATT_WIDTH ** -0.5),
        'sinks_b': nrm(ks[22], (N_B, N_HEADS), 0.5),
        'g_final': 1.0 + nrm(ks[23], (D,), 0.02),
    }


def reference(x_prompt, x_sample, c_prompt, c_sample, state_conv, cache_k, cache_v,
              g_a, w_mod_a, b_mod_a, w_in_a, conv_w_a, w_out_a,
              g_kv, w_mod_kv, b_mod_kv, w_kv,
              g_b, w_mod_b, b_mod_b, w_qz_b, w_o_b, sinks_b, g_final):
    seq = x_prompt.shape[1]
    dec_seq = x_sample.shape[1]
    pos_prompt = jnp.arange(seq)
    pos_sample = PAST_LEN + jnp.arange(dec_seq)
    zero_conv = jnp.zeros((N_A, x_prompt.shape[0], CONV_W - 1, D_CONV), x_prompt.dtype)
    y_prompt, conv_p, k_p, v_p = trunk(
        x_prompt, c_prompt, pos_prompt, zero_conv, None, None,
        g_a, w_mod_a, b_mod_a, w_in_a, conv_w_a, w_out_a, g_kv, w_mod_kv, b_mod_kv, w_kv,
        g_b, w_mod_b, b_mod_b, w_qz_b, w_o_b, sinks_b, g_final)
    y_sample, conv_s, k_s, v_s = trunk(
        x_sample, c_sample, pos_sample, state_conv, cache_k, cache_v,
        g_a, w_mod_a, b_mod_a, w_in_a, conv_w_a, w_out_a, g_kv, w_mod_kv, b_mod_kv, w_kv,
        g_b, w_mod_b, b_mod_b, w_qz_b, w_o_b, sinks_b, g_final)
    return (y_prompt, y_sample, conv_p, conv_s, k_p, v_p, k_s, v_s)
```

```python
import contextlib
import math
import numpy as np
import concourse.bass as bass
import concourse.mybir as mybir
from concourse.bass_utils import run_bass_kernel_spmd

F32 = mybir.dt.float32
BF16 = mybir.dt.bfloat16
I32 = mybir.dt.int32
ALU = mybir.AluOpType
AF = mybir.ActivationFunctionType
AX = mybir.AxisListType

D = 1024
KC = 8
NCORES = 8
EPS = 1e-6
ROPE_THETA = 500000.0
PAST_LEN = 4096
NSLOT = 5
ENGS = ("pe", "act", "dve", "pool", "sp")


class _Op:
    __slots__ = ("eng", "fn", "deps", "idx", "is_dma", "dma_sem", "sig", "token")


class Prog:
    def __init__(self, nc):
        self.nc = nc
        self.ops = []
        self.last_w = {}
        self.readers = {}

    def op(self, eng, fn, reads=(), writes=(), dma=None):
        o = _Op()
        o.eng, o.fn, o.idx = eng, fn, len(self.ops)
        o.is_dma, o.dma_sem, o.sig, o.token = dma is not None, dma, False, None
        deps = {}
        psum_reads = [k for k in reads if k.startswith(("mm", "pt", "trb", "ops"))]
        writes = list(writes) + psum_reads
        for k in reads:
            w = self.last_w.get(k)
            if w is not None:
                deps[w] = True
        for k in writes:
            w = self.last_w.get(k)
            if w is not None:
                deps.setdefault(w, False)
            for r in self.readers.get(k, ()):
                deps.setdefault(r, False)
        o.deps = deps
        for k in reads:
            self.readers.setdefault(k, []).append(o.idx)
        for k in writes:
            self.last_w[k] = o.idx
            self.readers[k] = []
        self.ops.append(o)
        return o.idx

    def emit(self, final_wait_keys=()):
        nc, ops = self.nc, self.ops
        fdeps = set()
        for k in final_wait_keys:
            w = self.last_w.get(k)
            if w is not None:
                fdeps.add(w)
        needed = set(fdeps)
        for o in ops:
            best = {}
            for d, raw in o.deps.items():
                p = ops[d]
                if (not o.is_dma) and (not p.is_dma) and p.eng == o.eng and o.eng == "pe":
                    continue
                key = ("d", p.dma_sem) if p.is_dma else ("e", p.eng)
                if key not in best or best[key] < d:
                    best[key] = d
            o.deps = set(best.values())
            needed.update(o.deps)
        for d in needed:
            ops[d].sig = True
        stack = contextlib.ExitStack()
        eng_sem = {e: stack.enter_context(nc.semaphore("s_" + e)) for e in ENGS}
        dma_sem = {}
        for o in ops:
            if o.is_dma and o.dma_sem not in dma_sem:
                dma_sem[o.dma_sem] = stack.enter_context(nc.semaphore("d_" + o.dma_sem))
        cnt = {e: 0 for e in ENGS}
        dcnt = {k: 0 for k in dma_sem}
        for o in ops:
            if o.is_dma:
                dcnt[o.dma_sem] += 16
                o.token = (dma_sem[o.dma_sem], dcnt[o.dma_sem], "d_" + o.dma_sem)
            elif o.sig:
                cnt[o.eng] += 1
                o.token = (eng_sem[o.eng], cnt[o.eng], "s_" + o.eng)
        per_eng = {e: [o for o in ops if o.eng == e] for e in ENGS}
        self.stats = {"ops": len(ops), "per_eng": {e: len(v) for e, v in per_eng.items()}, "sig": dict(cnt)}

        def run(e, h):
            waited = {}
            for o in per_eng[e]:
                for d in sorted(o.deps):
                    sem, val, sname = ops[d].token
                    if waited.get(sname, 0) >= val:
                        continue
                    h.wait_ge(sem, val)
                    waited[sname] = val
                ins = o.fn(h)
                if o.is_dma:
                    ins.then_inc(o.token[0], 16)
                elif o.sig:
                    ins.then_inc(o.token[0], 1)
            if e == "sp":
                fin = {}
                for d in fdeps:
                    sem, val, sname = ops[d].token
                    if sname not in fin or fin[sname][1] < val:
                        fin[sname] = (sem, val)
                for sname, (sem, val) in sorted(fin.items()):
                    if waited.get(sname, 0) >= val:
                        continue
                    h.wait_ge(sem, val)
                    waited[sname] = val

        with stack:
            with nc.Block() as block:
                @block.tensor
                def _(h):
                    run("pe", h)

                @block.scalar
                def _(h):
                    run("act", h)

                @block.vector
                def _(h):
                    run("dve", h)

                @block.gpsimd
                def _(h):
                    run("pool", h)

                @block.sync
                def _(h):
                    run("sp", h)


def build_program(SEQ, with_sample=True, debug=False, stage=99):
    assert SEQ % 512 == 0
    NT = SEQ // 512
    nc = bass.Bass("TRN2", target_bir_lowering=False)

    def din(name, shape):
        return nc.dram_tensor(name, list(shape), F32, kind="ExternalInput").ap()

    def dout(name, shape):
        return nc.dram_tensor(name, list(shape), F32, kind="ExternalOutput").ap()

    x_p = din("x_p", [SEQ, D])
    x_s = din("x_s", [32, D])
    cmat = din("cmat", [3, D])
    st_conv = din("st_conv", [2, 2, D])
    cache_k = din("cache_k", [2, 128, 256])
    cache_v = din("cache_v", [2, 128, 256])
    g_a = din("g_a", [D]); w_mod_a = din("w_mod_a", [D, 3 * D]); b_mod_a = din("b_mod_a", [3 * D])
    w_in = din("w_in", [D, 4 * D]); conv_w = din("conv_w", [3, D]); w_out = din("w_out", [D, D])
    g_kv = din("g_kv", [D]); w_mod_kv = din("w_mod_kv", [D, 2 * D]); b_mod_kv = din("b_mod_kv", [2 * D])
    w_kv = din("w_kv", [D, 512])
    g_b = din("g_b", [D]); w_mod_b = din("w_mod_b", [D, 3 * D]); b_mod_b = din("b_mod_b", [3 * D])
    w_qz = din("w_qz", [D, 2 * D]); w_o = din("w_o", [D, D]); sinks = din("sinks", [16]); g_f = din("g_f", [D])

    y_p = dout("y_p", [SEQ, D]); y_s = dout("y_s", [32, D])
    conv_p = dout("conv_p", [2, D]); conv_s = dout("conv_s", [2, 2, D])
    k_p = dout("k_p", [128, 256]); v_p = dout("v_p", [128, 256])
    k_s = dout("k_s", [2, 128, 256]); v_s = dout("v_s", [2, 128, 256])

    NCH = 17
    scr = nc.dram_tensor("wscr", [NCH, 128, KC, 512], BF16).ap()

    st = contextlib.ExitStack()

    def sb(name, shape, dt):
        return st.enter_context(nc.sbuf_tensor(name, list(shape), dt))

    def ps(name, shape, dt):
        return st.enter_context(nc.psum_tensor(name, list(shape), dt))

    with st:
        st.enter_context(nc.allow_non_contiguous_dma(reason="small strided vector loads/stores"))
        P = Prog(nc)
        outkeys = []

        ring = sb("ring", [128, NSLOT, KC, 512], BF16)
        identf = sb("identf", [128, 128], F32)
        identb = sb("identb", [128, 128], BF16)
        onesm = sb("onesm", [128, 128], BF16)
        ones1 = sb("ones1", [128, 128], BF16)
        iot = sb("iot", [128, 128], I32)
        cosT = sb("cosT", [128, 65, 8], F32)
        sinT = sb("sinT", [128, 65, 8], F32)
        nsinT = sb("nsinT", [128, 65, 8], F32)
        posi = sb("posi", [128, 65], I32)
        posf = sb("posf", [128, 65], F32)
        vecs = sb("vecs", [128, 128], F32)
        vecT = sb("vecT", [128, 128], F32)
        modT = sb("modT", [128, 64, 3], F32)
        cT = sb("cT", [128, KC, 3], F32)
        scT = sb("scT", [128, KC, 3], BF16)
        mrow = sb("mrow", [3, 512], F32)
        gsA = sb("gsA", [128, KC, 3], F32); gsKV = sb("gsKV", [128, KC, 3], F32); gsB = sb("gsB", [128, KC, 3], F32)
        shpad = sb("shpad", [128, 3, KC, 65], BF16)
        bias_in = sb("bias_in", [128, 32, 3], F32)
        bias_z = sb("bias_z", [128, 8, 3], F32)
        brow_hi = sb("brow_hi", [65, 1536], BF16)
        brow_lo = sb("brow_lo", [65, 1536], BF16)
        brow_f = sb("brow_f", [65, 512], F32)
        brow_t = sb("brow_t", [65, 512], F32)
        sink_bc = sb("sink_bc", [128, 16], F32)
        nsink = sb("nsink", [128, 16], F32)
        sink_p = sb("sink_p", [128, 16], F32)
        nsmax = sb("nsmax", [128, 4], F32)
        nsink_p = sb("nsink_p", [128, 16], F32)
        carry = sb("carry", [128, 3, KC, 2], F32)

        xtok = sb("xtok", [128, 2, D], F32)
        ytok = sb("ytok", [128, 2, D], F32)
        xT = sb("xT", [128, KC, 512], F32)
        def _alias(a):
            return xT[:, a:a + 2, :].rearrange("p a b -> p (a b)")[:, 0:520].rearrange("p (b i) -> p b i", i=8)
        rtmp, rtmp2, rki = _alias(0), _alias(2), _alias(4).bitcast(I32)
        sq = sb("sq", [128, 2, 512], BF16)
        rstd = sb("rstd", [128, 512], F32)
        h1 = sb("h1", [128, KC, 512], BF16)
        h2 = sb("h2", [128, KC, 512], BF16)
        mT = sb("mT", [128, KC, 512], BF16)
        c_sb = sb("c_sb", [128, 2, 512], F32)
        vtmp = sb("vtmp", [128, 2, 516], F32)
        ycv = sb("ycv", [128, 2, 512], F32)
        szA = sb("szA", [128, 2, 512], F32)
        szT = sb("szT", [128, KC, 512], BF16)
        ogT = sb("ogT", [128, KC, 512], BF16)
        kf = sb("kf", [128, 2, 256], F32)
        vf = sb("vf", [128, 256], F32)
        kb2 = sb("kb2", [128, 4, 2, 64], BF16)
        KT2 = sb("KT2", [128, 4, 640], BF16)
        Vaug = sb("Vaug", [128, 5, 4, 65], BF16)
        qf = sb("qf", [128, 2, D], F32)
        ra = sb("ra", [128, 16, 16], F32)
        rb = sb("rb", [128, 16, 16], F32)
        qb = sb("qb", [128, 2, D], BF16)
        qT = sb("qT", [128, 4, KC, 128], BF16)
        Pm = sb("Pm", [128, 3, 4, 256], BF16)
        PTs = sb("PTs", [128, 2, 2, 4, 128], BF16)
        stt = sb("stt", [128, 4, 8, 4], F32)
        otok = sb("otok", [128, 2, D], BF16)

        big = ps("big", [128, 4, 512], F32)
        ptp = ps("ptp", [128, 3, 1024], BF16)
        ops_ = ps("ops_", [128, 512], F32)

        mmctr = [0]

        mm_reserved = set()

        attn_mode = [False]

        def mm1():
            if attn_mode[0]:
                mmctr[0] += 1
                return 2 + mmctr[0] % 2
            while True:
                k = mmctr[0] % 4
                mmctr[0] += 1
                if k not in mm_reserved:
                    return k

        def mm2():
            if attn_mode[0]:
                return 0
            while True:
                if mmctr[0] % 2:
                    mmctr[0] += 1
                k = mmctr[0] % 4
                mmctr[0] += 2
                if k not in mm_reserved and (k + 1) not in mm_reserved:
                    return k

        P.op("pool", lambda e: e.iota(iot[:], [[1, 128]], base=0, channel_multiplier=-1), writes=["iot"])
        P.op("dve", lambda e: e.tensor_single_scalar(out=identf[:], in_=iot[:], scalar=0, op=ALU.is_equal),
             reads=["iot"], writes=["identf"])
        P.op("dve", lambda e: e.tensor_copy(out=identb[:], in_=identf[:]), reads=["identf"], writes=["identb"])
        def wview(w):
            return w.rearrange("(kc p) f -> p kc f", p=128)

        w_in_v, w_out_v, w_kv_v, w_qz_v, w_o_v = wview(w_in), wview(w_out), wview(w_kv), wview(w_qz), wview(w_o)
        CH_IN = list(range(8)); CH_OUT = [8, 9]; CH_Z = [10, 11]; CH_KV = 12; CH_Q = [13, 14]; CH_O = [15, 16]
        cast_q = []
        for i in range(8):
            for sec in range(4):
                cast_q.append(lambda i=i, sec=sec: P.op("pool", lambda e: e.dma_start(
                    out=scr[i, :, :, sec * 128:(sec + 1) * 128], in_=w_in_v[:, :, sec * 1024 + i * 128: sec * 1024 + (i + 1) * 128]),
                    writes=["scr%d_%d" % (i, sec)], dma="wc%d" % i))
        scr_keys = {i: ["scr%d_%d" % (i, s) for s in range(4)] for i in range(8)}

        def cast_chunk(ch, src, c0):
            cast_q.append(lambda: P.op("pool", lambda e: e.dma_start(out=scr[ch], in_=src[:, :, c0:c0 + 512]), writes=["scr%d" % ch], dma="wc%d" % ch))
            scr_keys[ch] = ["scr%d" % ch]

        cast_chunk(8, w_out_v, 0); cast_chunk(9, w_out_v, 512)
        cast_chunk(10, w_qz_v, 1024); cast_chunk(11, w_qz_v, 1536)
        cast_chunk(12, w_kv_v, 0)
        cast_chunk(13, w_qz_v, 0); cast_chunk(14, w_qz_v, 512)
        cast_chunk(15, w_o_v, 0); cast_chunk(16, w_o_v, 512)

        P.op("pool", lambda e: e.memset(onesm[:], 1.0 / 1024.0), writes=["onesm"])
        P.op("pool", lambda e: e.memset(ones1[:], 1.0), writes=["ones1"])
        P.op("pool", lambda e: e.memset(shpad[:], 0.0), writes=["shpad"])
        P.op("pool", lambda e: e.memset(vecs[:], 0.0), writes=["vecs"])
        P.op("pool", lambda e: e.memset(carry[:], 0.0), writes=["carry0", "carry1", "carry2"])
        P.op("pool", lambda e: e.memset(KT2[:], 0.0), writes=["KT2"])
        P.op("pool", lambda e: e.memset(Vaug[:], 0.0), writes=["Vaug%d" % s for s in range(5)])
        P.op("pool", lambda e: e.memset(Vaug[:, :, :, 64:65], 1.0), writes=["Vaug%d" % s for s in range(5)])
        P.op("pool", lambda e: e.memset(Pm[:], 0.0), writes=["Pm0", "Pm1", "Pm2"])
        P.op("pool", lambda e: e.memset(vtmp[:], 0.0), writes=["vtmp0", "vtmp1"])

        P.op("pool", lambda e: e.iota(posi[:, 0:64], [[128, 64]], base=0, channel_multiplier=1), writes=["posi"])
        P.op("pool", lambda e: e.iota(posi[:, 64:65], [[0, 1]], base=PAST_LEN, channel_multiplier=1), writes=["posi"])
        P.op("dve", lambda e: e.tensor_copy(out=posf[:], in_=posi[:]), reads=["posi"], writes=["posf"])
        inv = [float(np.float32(1.0) / (np.float32(ROPE_THETA) ** (np.float32(2 * i) / np.float32(16)))) for i in range(8)]
        TWO_PI = 2.0 * math.pi
        C1 = float(np.float32(6.28125))
        C2 = float(TWO_PI - 6.28125)
        for i in range(8):
            P.op("dve", lambda e, i=i: e.tensor_single_scalar(out=rtmp[:, :, i], in_=posf[:], scalar=inv[i], op=ALU.mult),
                 reads=["posf"], writes=["xT0", "xT1"])

        def sincos(dst, phase, dname):
            P.op("dve", lambda e: e.tensor_scalar(out=rtmp2[:], in0=rtmp[:], scalar1=phase, scalar2=1.0 / TWO_PI, op0=ALU.add, op1=ALU.mult),
                 reads=["xT0", "xT1"], writes=["xT2", "xT3"])
            P.op("dve", lambda e: e.tensor_copy(out=rki[:], in_=rtmp2[:]), reads=["xT2", "xT3"], writes=["xT4", "xT5"])
            P.op("dve", lambda e: e.tensor_copy(out=rtmp2[:], in_=rki[:]), reads=["xT4", "xT5"], writes=["xT2", "xT3"])
            P.op("dve", lambda e: e.scalar_tensor_tensor(out=dst[:], in0=rtmp2[:], scalar=-C1, in1=rtmp[:], op0=ALU.mult, op1=ALU.add),
                 reads=["xT2", "xT3", "xT0", "xT1"], writes=[dname])
            P.op("dve", lambda e: e.tensor_single_scalar(out=dst[:], in_=dst[:], scalar=phase, op=ALU.add),
                 reads=[dname], writes=[dname])
            P.op("dve", lambda e: e.scalar_tensor_tensor(out=dst[:], in0=rtmp2[:], scalar=-C2, in1=dst[:], op0=ALU.mult, op1=ALU.add),
                 reads=["xT2", "xT3", dname], writes=[dname])
            for sgn in (1.0, -1.0):
                cmpop = ALU.is_gt if sgn > 0 else ALU.is_lt
                P.op("dve", lambda e, cmpop=cmpop, sgn=sgn: e.tensor_single_scalar(out=rtmp2[:], in_=dst[:], scalar=sgn * math.pi, op=cmpop),
                     reads=[dname], writes=["xT2", "xT3"])
                P.op("dve", lambda e, sgn=sgn: e.scalar_tensor_tensor(out=dst[:], in0=rtmp2[:], scalar=-sgn * TWO_PI, in1=dst[:], op0=ALU.mult, op1=ALU.add),
                     reads=["xT2", "xT3", dname], writes=[dname])
            P.op("dve", lambda e: e.tensor_scalar(out=dst[:], in0=dst[:], scalar1=-math.pi, scalar2=math.pi, op0=ALU.max, op1=ALU.min),
                 reads=[dname], writes=[dname])
            P.op("act", lambda e: e.activation(out=dst[:], in_=dst[:], func=AF.Sin), reads=[dname], writes=[dname])

        sincos(sinT, 0.0, "sinT")
        sincos(cosT, math.pi / 2.0, "cosT")
        P.op("dve", lambda e: e.tensor_single_scalar(out=nsinT[:], in_=sinT[:], scalar=-1.0, op=ALU.mult),
             reads=["sinT"], writes=["nsinT"])

        ring_state = {"next_slot": 0, "loaded": {}}

        def ring_load(ch):
            s = ring_state["next_slot"] % NSLOT
            ring_state["next_slot"] += 1
            P.op("sp", lambda e: e.dma_start(out=ring[:, s], in_=scr[ch]), reads=scr_keys[ch], writes=["ring%d" % s], dma="ring%d" % s)
            ring_state["loaded"][ch] = s
            return s

        def vload(r0, src, nrows):
            P.op("sp", lambda e: e.dma_start(out=vecs[r0:r0 + nrows, :], in_=src.rearrange("(r c) -> r c", c=128)),
                 reads=["vecs"], writes=["vecs_%d" % r0], dma="vec%d" % r0)
            return "vecs_%d" % r0

        vk = [vload(0, b_mod_a, 24), vload(24, b_mod_kv, 16), vload(40, b_mod_b, 24), vload(64, g_a, 8),
              vload(72, g_kv, 8), vload(80, g_b, 8), vload(88, g_f, 8), vload(96, conv_w.rearrange("t d -> (t d)"), 24)]
        k0 = mm1()
        P.op("pe", lambda e: e.transpose(out=big[:, k0, 0:128], in_=vecs[:, :], identity=identf[:]),
             reads=vk + ["vecs", "identf"], writes=["mm%d" % k0])
        P.op("dve", lambda e: e.tensor_copy(out=vecT[:], in_=big[:, k0, 0:128]), reads=["mm%d" % k0], writes=["vecT"])
        gaT, gkvT, gbT, gfT = vecT[:, 64:72], vecT[:, 72:80], vecT[:, 80:88], vecT[:, 88:96]
        cwT = vecT[:, 96:120]

        P.op("sp", lambda e: e.dma_start(out=sink_bc[:], in_=sinks.partition_broadcast(128)), writes=["sink_bc"], dma="vsink")
        P.op("dve", lambda e: e.tensor_single_scalar(out=nsink[:], in_=sink_bc[:], scalar=-1.0, op=ALU.mult),
             reads=["sink_bc"], writes=["nsink"])
        P.op("dve", lambda e: e.tensor_reduce(out=nsmax[:], in_=sink_bc[:].rearrange("p (q g) -> p q g", g=4), axis=AX.X, op=ALU.max),
             reads=["sink_bc"], writes=["nsmax"])
        P.op("dve", lambda e: e.tensor_single_scalar(out=nsmax[:], in_=nsmax[:], scalar=-1.0, op=ALU.mult), reads=["nsmax"], writes=["nsmax"])
        P.op("dve", lambda e: e.tensor_copy(out=sink_p[:].rearrange("p (q a t) -> p q a t", a=2, t=2),
                                            in_=sink_bc[:].rearrange("p (q t a) -> p q a t", a=2, t=2)), reads=["sink_bc"], writes=["sink_p"])
        P.op("dve", lambda e: e.tensor_copy(out=nsink_p[:].rearrange("p (q a t) -> p q a t", a=2, t=2),
                                            in_=nsink[:].rearrange("p (q t a) -> p q a t", a=2, t=2)), reads=["nsink"], writes=["nsink_p"])

        for n_ in range(3):
            P.op("sp", lambda e, n_=n_: e.dma_start(out=cT[:, :, n_], in_=cmat[n_].rearrange("(kc p) -> p kc", p=128)), writes=["cT"], dma="vcT")
        P.op("act", lambda e: e.activation(out=scT[:], in_=cT[:], func=AF.Silu), reads=["cT"], writes=["scT"])

        mods = [(w_mod_a, 3 * D, 0), (w_mod_kv, 2 * D, 24), (w_mod_b, 3 * D, 40)]
        for wm, width, j0 in mods:
            wmv = wview(wm)
            for blk in range(width // 512):
                s = ring_state["next_slot"] % NSLOT
                ring_state["next_slot"] += 1
                P.op("pool", lambda e, s=s, wmv=wmv, blk=blk: e.dma_start(out=ring[:, s], in_=wmv[:, :, blk * 512:(blk + 1) * 512]),
                     writes=["ring%d" % s], dma="ringp%d" % s)
                for _ in range(3):
                    if cast_q:
                        cast_q.pop(0)()
                k = mm1()
                for kc in range(KC):
                    P.op("pe", lambda e, s=s, kc=kc, k=k: e.matmul(big[0:3, k, :], lhsT=scT[:, kc, :], rhs=ring[:, s, kc, :],
                                                                   start=(kc == 0), stop=(kc == KC - 1)),
                         reads=["scT", "ring%d" % s], writes=["mm%d" % k])
                P.op("dve", lambda e, k=k: e.tensor_copy(out=mrow[:], in_=big[0:3, k, :]), reads=["mm%d" % k], writes=["mrow"])
                k2 = mm1()
                for q4 in range(4):
                    P.op("pe", lambda e, q4=q4, k2=k2: e.transpose(out=big[:, k2, q4 * 4:q4 * 4 + 3], in_=mrow[:, q4 * 128:(q4 + 1) * 128],
                                                                  identity=identf[0:3, 0:3]),
                         reads=["mrow", "identf"], writes=["mm%d" % k2])
                jj = j0 + blk * 4
                P.op("dve", lambda e, k2=k2, jj=jj: e.tensor_tensor(
                    out=modT[:, jj:jj + 4, :], in0=big[:, k2, 0:16].rearrange("p (q n) -> p q n", n=4)[:, :, 0:3],
                    in1=vecT[:, jj:jj + 4].unsqueeze(2).to_broadcast([128, 4, 3]), op=ALU.add),
                    reads=["mm%d" % k2, "vecT"], writes=["modT"])
        while cast_q:
            cast_q.pop(0)()
        gateA = modT[:, 16:24, :]
        gateB = modT[:, 56:64, :]
        for dst, sc0, gT, dname in ((gsA, 8, gaT, "gsA"), (gsKV, 32, gkvT, "gsKV"), (gsB, 48, gbT, "gsB")):
            P.op("dve", lambda e, dst=dst, sc0=sc0, gT=gT: e.scalar_tensor_tensor(
                out=dst[:], in0=modT[:, sc0:sc0 + 8, :], scalar=1.0, in1=gT.unsqueeze(2).to_broadcast([128, 8, 3]),
                op0=ALU.add, op1=ALU.mult), reads=["modT", "vecT"], writes=[dname])
        for si, sh0 in enumerate((0, 24, 40)):
            for n in range(3):
                P.op("dve", lambda e, si=si, sh0=sh0, n=n: e.tensor_copy(out=shpad[:, si, :, 32 * n], in_=modT[:, sh0:sh0 + 8, n]),
                     reads=["modT", "shpad"], writes=["shpad"])

        def bias_fm(ch, s):
            for blk in range(4):
                k = mm1()
                si = 0 if ch < 8 else 2
                for kc in range(KC):
                    P.op("pe", lambda e, s=s, kc=kc, k=k, blk=blk, si=si: e.matmul(
                        big[:, k, 0:65], lhsT=ring[:, s, kc, blk * 128:(blk + 1) * 128], rhs=shpad[:, si, kc, :],
                        start=(kc == 0), stop=(kc == KC - 1)), reads=["ring%d" % s, "shpad"], writes=["mm%d" % k])
                if ch < 8:
                    dstap = bias_in[:, blk * 8 + ch, :]
                    dk = "bias_in"
                else:
                    dstap = bias_z[:, (ch - CH_Z[0]) * 4 + blk, :]
                    dk = "bias_z"
                P.op("dve", lambda e, k=k, dstap=dstap: e.tensor_copy(out=dstap, in_=big[:, k, 0:65:32]),
                     reads=["mm%d" % k], writes=[dk])

        def bias_row(ci, s, si):
            k = mm1()
            for kc in range(KC):
                P.op("pe", lambda e, s=s, kc=kc, k=k, si=si: e.matmul(big[0:65, k, :], lhsT=shpad[:, si, kc, :], rhs=ring[:, s, kc, :],
                                                                        start=(kc == 0), stop=(kc == KC - 1)),
                     reads=["ring%d" % s, "shpad"], writes=["mm%d" % k])
            P.op("dve", lambda e, k=k, ci=ci: e.tensor_copy(out=brow_hi[:, ci * 512:(ci + 1) * 512], in_=big[0:65, k, :]),
                 reads=["mm%d" % k], writes=["brow_hi%d" % ci])

        for ch in CH_IN + CH_Z:
            bias_fm(ch, ring_load(ch))
        for ci, (ch, si) in enumerate(((CH_KV, 1), (CH_Q[0], 2), (CH_Q[1], 2))):
            bias_row(ci, ring_load(ch), si)

        if with_sample:
            for s_ in range(2):
                for t_ in range(2):
                    P.op("sp", lambda e, s_=s_, t_=t_: e.dma_start(out=carry[:, 1 + s_, :, t_], in_=st_conv[s_, t_].rearrange("(c p) -> p c", p=128)),
                         reads=[], writes=["carry%d" % (1 + s_)], dma="vcar%d" % s_)
                P.op("sp", lambda e, s_=s_: e.dma_start(out=k_s[s_, 0:112, :], in_=cache_k[s_, 16:128, :]), writes=["o_ksc%d" % s_], dma="outm")
                P.op("sp", lambda e, s_=s_: e.dma_start(out=v_s[s_, 0:112, :], in_=cache_v[s_, 16:128, :]), writes=["o_vsc%d" % s_], dma="outm")
                outkeys.extend(["o_ksc%d" % s_, "o_vsc%d" % s_])

        xin_ctr = [0]
        xpre = {}
        yout_ctr = [0]
        ktr = [0]

        class NormAcc:
            def __init__(self, T):
                self.T, self.k, self.n, self.pend = T, mm1(), 0, None
                mm_reserved.add(self.k)

            def add(self, kc, xkeys):
                T, k = self.T, self.k
                sl = ktr[0] % 2
                ktr[0] += 1
                P.op("act", lambda e: e.activation(out=sq[:, sl, 0:T], in_=xT[:, kc, 0:T], func=AF.Square), reads=xkeys, writes=["sq%d" % sl])
                self.flush()
                self.pend = sl

            def flush(self):
                if self.pend is None:
                    return
                T, k, sl, n = self.T, self.k, self.pend, self.n
                P.op("pe", lambda e: e.matmul(big[:, k, 0:T], lhsT=onesm[:], rhs=sq[:, sl, 0:T], start=(n == 0), stop=(n == KC - 1)),
                     reads=["sq%d" % sl, "onesm"], writes=["mm%d" % k])
                self.n += 1
                self.pend = None

            def finish(self):
                T, k = self.T, self.k
                self.flush()
                assert self.n == KC
                P.op("act", lambda e: e.activation(out=rstd[:, 0:T], in_=big[:, k, 0:T], func=AF.Ln, bias=EPS, scale=1.0),
                     reads=["mm%d" % k], writes=["rstd"])
                P.op("act", lambda e: e.activation(out=rstd[:, 0:T], in_=rstd[:, 0:T], func=AF.Exp, scale=-0.5),
                     reads=["rstd"], writes=["rstd"])
                mm_reserved.discard(k)

        def norm_stats(T, xkeys_of_kc):
            na = NormAcc(T)
            for kc in range(KC):
                na.add(kc, xkeys_of_kc(kc))
            na.finish()

        def run_tile(T, segs, ioblocks, blocks, tile_idx, next_iob, last_tile_out):
            nblk_keys = [b_["c0"] for b_ in blocks]

            def xk(kc):
                return ["xT%d" % kc]

            def xload(ioe, key):
                (c0_, rows_, src_, dst_) = ioe
                sl_ = xin_ctr[0] % 2
                xin_ctr[0] += 1
                P.op("sp", lambda e, sl_=sl_, rows_=rows_, src_=src_: e.dma_start(out=xtok[0:rows_, sl_, :], in_=src_),
                     writes=["xtok%d" % sl_], dma="xin%d" % sl_)
                xpre[key] = sl_

            for bidx, (c0, rows, src, dst) in enumerate(ioblocks):
                if (tile_idx, bidx) not in xpre:
                    xload((c0, rows, src, dst), (tile_idx, bidx))
                sl = xpre.pop((tile_idx, bidx))
                for half in range(2):
                    k = mm1()
                    for q4 in range(4):
                        kc = half * 4 + q4
                        P.op("pe", lambda e, sl=sl, rows=rows, kc=kc, q4=q4, k=k: e.transpose(
                            out=big[:, k, q4 * 128:q4 * 128 + rows], in_=xtok[0:rows, sl, kc * 128:(kc + 1) * 128],
                            identity=identf[0:rows, 0:rows]), reads=["xtok%d" % sl, "identf"], writes=["mm%d" % k])
                    P.op("act", lambda e, half=half, k=k, c0=c0, rows=rows: e.activation(
                        out=xT[:, half * 4:half * 4 + 4, c0:c0 + rows],
                        in_=big[:, k, :].rearrange("p (q t) -> p q t", t=128)[:, :, 0:rows], func=AF.Copy),
                        reads=["mm%d" % k], writes=["xT%d" % kc_ for kc_ in range(half * 4, half * 4 + 4)])

            if next_iob is not None:
                for bidx in range(min(2, len(next_iob))):
                    xload(next_iob[bidx], (tile_idx + 1, bidx))
            if stage < 1:
                return
            norm_stats(T, xk)
            for kc in range(KC):
                for (c0, n, bt, ci) in segs:
                    P.op("dve", lambda e, kc=kc, c0=c0, n=n, bt=bt: e.scalar_tensor_tensor(
                        out=h1[:, kc, c0:c0 + n], in0=xT[:, kc, c0:c0 + n], scalar=gsA[:, kc, bt:bt + 1], in1=rstd[:, c0:c0 + n],
                        op0=ALU.mult, op1=ALU.mult), reads=xk(kc) + ["rstd", "gsA"], writes=["h1_%d" % kc])
            h1keys = ["h1_%d" % kc for kc in range(KC)]
            h2keys = ["h2_%d" % kc for kc in range(KC)]

            if stage < 2:
                return
            pending = None
            for i in range(8):
                s = ring_state["loaded"].pop(i) if i in ring_state["loaded"] else ring_load(i)
                sl = i % 2
                accs = {}
                for sec in (1, 2, 3, 0):
                    k = mm1()
                    accs[sec] = k
                    for kc in range(KC):
                        P.op("pe", lambda e, s=s, kc=kc, k=k, sec=sec: e.matmul(
                            big[:, k, 0:T], lhsT=ring[:, s, kc, sec * 128:(sec + 1) * 128], rhs=h1[:, kc, 0:T],
                            start=(kc == 0), stop=(kc == KC - 1)), reads=["ring%d" % s, "h1_%d" % kc], writes=["mm%d" % k])
                    for si_, (c0, n, bt, ci) in enumerate(segs):
                        if sec == 1:
                            P.op("act", lambda e, k=k, c0=c0, n=n, bt=bt, sl=sl, i=i: e.activation(
                                out=c_sb[:, sl, c0:c0 + n], in_=big[:, k, c0:c0 + n], func=AF.Identity,
                                bias=bias_in[:, 8 + i, bt:bt + 1], scale=1.0), reads=["mm%d" % k, "bias_in"], writes=["c_sb%d" % sl])
                        elif sec == 2:
                            P.op("pool", lambda e, sl=sl, si_=si_, ci=ci, i=i: e.tensor_copy(out=vtmp[:, sl, si_ * 40:si_ * 40 + 2], in_=carry[:, ci, i, :]),
                                 reads=["carry%d" % ci], writes=["vtmp%d" % sl])
                            P.op("dve", lambda e, k=k, c0=c0, n=n, bt=bt, sl=sl, si_=si_, i=i: e.scalar_tensor_tensor(
                                out=vtmp[:, sl, si_ * 40 + 2:si_ * 40 + 2 + n], in0=big[:, k, c0:c0 + n], scalar=bias_in[:, 16 + i, bt:bt + 1],
                                in1=c_sb[:, sl, c0:c0 + n], op0=ALU.add, op1=ALU.mult),
                                reads=["mm%d" % k, "bias_in", "c_sb%d" % sl], writes=["vtmp%d" % sl])
                            P.op("pool", lambda e, sl=sl, si_=si_, ci=ci, i=i, n=n: e.tensor_copy(out=carry[:, ci, i, :], in_=vtmp[:, sl, si_ * 40 + n:si_ * 40 + n + 2]),
                                 reads=["vtmp%d" % sl], writes=["carry%d" % ci])
                            P.op("act", lambda e, sl=sl, si_=si_, c0=c0, n=n, i=i: e.activation(
                                out=ycv[:, sl, c0:c0 + n], in_=vtmp[:, sl, si_ * 40:si_ * 40 + n], func=AF.Copy, scale=cwT[:, i:i + 1]),
                                reads=["vtmp%d" % sl, "vecT"], writes=["ycv%d" % sl])
                            for tap in (1, 2):
                                P.op("dve", lambda e, sl=sl, si_=si_, c0=c0, n=n, i=i, tap=tap: e.scalar_tensor_tensor(
                                    out=ycv[:, sl, c0:c0 + n], in0=vtmp[:, sl, si_ * 40 + tap:si_ * 40 + tap + n], scalar=cwT[:, tap * 8 + i:tap * 8 + i + 1],
                                    in1=ycv[:, sl, c0:c0 + n], op0=ALU.mult, op1=ALU.add),
                                    reads=["vtmp%d" % sl, "vecT", "ycv%d" % sl], writes=["ycv%d" % sl])
                        elif sec == 3:
                            P.op("act", lambda e, k=k, c0=c0, n=n, bt=bt, sl=sl, i=i: e.activation(
                                out=szA[:, sl, c0:c0 + n], in_=big[:, k, c0:c0 + n], func=AF.Silu,
                                bias=bias_in[:, 24 + i, bt:bt + 1], scale=1.0), reads=["mm%d" % k, "bias_in"], writes=["szA%d" % sl])
                            P.op("pool", lambda e, sl=sl, c0=c0, n=n: e.tensor_tensor(
                                out=ycv[:, sl, c0:c0 + n], in0=ycv[:, sl, c0:c0 + n], in1=szA[:, sl, c0:c0 + n], op=ALU.mult),
                                reads=["ycv%d" % sl, "szA%d" % sl], writes=["ycv%d" % sl])
                        else:
                            P.op("dve", lambda e, k=k, c0=c0, n=n, bt=bt, sl=sl, i=i: e.scalar_tensor_tensor(
                                out=mT[:, i, c0:c0 + n], in0=big[:, k, c0:c0 + n], scalar=bias_in[:, i, bt:bt + 1],
                                in1=ycv[:, sl, c0:c0 + n], op0=ALU.add, op1=ALU.mult),
                                reads=["mm%d" % k, "bias_in", "ycv%d" % sl], writes=["mT%d" % i])
                nxt = {0: 5, 1: 6, 2: 7, 3: 8, 4: 9, 5: 10, 6: 11, 7: 12}[i]
                ring_load(nxt)

            if stage < 3:
                return
            na1 = NormAcc(T)
            for grp in ((0, 1, 2), (3, 4, 5), (6, 7)):
                ks = {oc: mm1() for oc in grp}
                for last in (False, True):
                    for oc in grp:
                        s = ring_state["loaded"][CH_OUT[oc // 4]]
                        k = ks[oc]
                        for ic in (range(KC - 1) if not last else (KC - 1,)):
                            P.op("pe", lambda e, s=s, ic=ic, k=k, oc=oc: e.matmul(
                                big[:, k, 0:T], lhsT=ring[:, s, ic, (oc % 4) * 128:(oc % 4 + 1) * 128], rhs=mT[:, ic, 0:T],
                                start=(ic == 0), stop=(ic == KC - 1)), reads=["ring%d" % s, "mT%d" % ic], writes=["mm%d" % k])
                for oc in grp:
                    k = ks[oc]
                    for (c0, n, bt, ci) in segs:
                        P.op("dve", lambda e, k=k, oc=oc, c0=c0, n=n, bt=bt: e.scalar_tensor_tensor(
                            out=xT[:, oc, c0:c0 + n], in0=big[:, k, c0:c0 + n], scalar=gateA[:, oc, bt:bt + 1], in1=xT[:, oc, c0:c0 + n],
                            op0=ALU.mult, op1=ALU.add), reads=["mm%d" % k, "modT"] + xk(oc), writes=xk(oc))
                    na1.add(oc, xk(oc))
                if 3 in grp:
                    ring_state["loaded"].pop(CH_OUT[0]); ring_load(13)
                if 7 in grp:
                    ring_state["loaded"].pop(CH_OUT[1]); ring_load(14)
            na1.finish()
            for hbuf, gsv, hname, gname in ((h2, gsB, "h2_%d", "gsB"), (h1, gsKV, "h1_%d", "gsKV")):
                for kc in range(KC):
                    for (c0, n, bt, ci) in segs:
                        P.op("dve", lambda e, kc=kc, c0=c0, n=n, bt=bt, hbuf=hbuf, gsv=gsv: e.scalar_tensor_tensor(
                            out=hbuf[:, kc, c0:c0 + n], in0=xT[:, kc, c0:c0 + n], scalar=gsv[:, kc, bt:bt + 1], in1=rstd[:, c0:c0 + n],
                            op0=ALU.mult, op1=ALU.mult), reads=xk(kc) + ["rstd", gname], writes=[hname % kc])

            if stage < 5:
                return
            s_kv = ring_state["loaded"][CH_KV]
            s_q = [ring_state["loaded"][CH_Q[0]], ring_state["loaded"][CH_Q[1]]]

            if stage < 6:
                return
            def pjkv(bi):
                B = blocks[bi]
                c0, R, bt, osl, sl = B["c0"], B["R"], B["n"], B["oslot"], bi % 2
                k = mm1()
                P.op("pe", lambda e, k=k, R=R, bt=bt: e.matmul(
                    big[0:R, k, :], lhsT=ones1[32 * bt:32 * bt + 1, 0:R], rhs=brow_hi[32 * bt:32 * bt + 1, 0:512],
                    start=True, stop=False), reads=["ones1", "brow_hi0"], writes=["mm%d" % k])
                for kc in range(KC):
                    P.op("pe", lambda e, kc=kc, k=k, c0=c0, R=R: e.matmul(
                        big[0:R, k, :], lhsT=h1[:, kc, c0:c0 + R], rhs=ring[:, s_kv, kc, :], start=False, stop=(kc == KC - 1)),
                        reads=["h1_%d" % kc, "ring%d" % s_kv], writes=["mm%d" % k])
                P.op("act", lambda e, k=k, R=R, sl=sl: e.activation(out=kf[0:R, sl, :], in_=big[0:R, k, 0:256], func=AF.Copy),
                     reads=["mm%d" % k], writes=["kf%d" % sl])
                P.op("dve", lambda e, k=k, R=R, osl=osl: e.tensor_copy(
                    out=Vaug[0:R, osl, :, 0:64], in_=big[0:R, k, 256:512].rearrange("p (h d) -> p h d", d=64)),
                    reads=["mm%d" % k], writes=["Vaug%d" % osl])
                if B["vout"] is not None:
                    P.op("act", lambda e, k=k, R=R: e.activation(out=vf[0:R, :], in_=big[0:R, k, 256:512], func=AF.Copy),
                         reads=["mm%d" % k], writes=["vf"])
                    ok = "o_v_%d_%d" % (bt, bi)
                    P.op("sp", lambda e, R=R, dst=B["vout"]: e.dma_start(out=dst, in_=vf[0:R, :]), reads=["vf"], writes=[ok], dma="outv%d" % bi)
                    outkeys.append(ok)

            def pjq(bi, hf):
                B = blocks[bi]
                c0, R, bt, sl = B["c0"], B["R"], B["n"], bi % 2
                k = mm1()
                P.op("pe", lambda e, k=k, R=R, bt=bt: e.matmul(
                    big[0:R, k, :], lhsT=ones1[32 * bt:32 * bt + 1, 0:R], rhs=brow_hi[32 * bt:32 * bt + 1, 512 * (1 + hf):512 * (2 + hf)],
                    start=True, stop=False), reads=["ones1", "brow_hi%d" % (1 + hf)], writes=["mm%d" % k])
                for kc in range(KC):
                    P.op("pe", lambda e, kc=kc, k=k, c0=c0, R=R: e.matmul(
                        big[0:R, k, :], lhsT=h2[:, kc, c0:c0 + R], rhs=ring[:, s_q[hf], kc, :], start=False, stop=(kc == KC - 1)),
                        reads=["h2_%d" % kc, "ring%d" % s_q[hf]], writes=["mm%d" % k])
                if hf == 0:
                    P.op("act", lambda e, k=k, R=R, sl=sl: e.activation(out=qf[0:R, sl, 0:512], in_=big[0:R, k, :], func=AF.Copy),
                         reads=["mm%d" % k], writes=["qf%d_0" % sl])
                else:
                    P.op("dve", lambda e, k=k, R=R, sl=sl: e.tensor_copy(out=qf[0:R, sl, 512:1024], in_=big[0:R, k, :]),
                         reads=["mm%d" % k], writes=["qf%d_1" % sl])

            def pj0(bi):
                pjkv(bi); pjq(bi, 0); pjq(bi, 1)

            def pj1(bi):
                B = blocks[bi]
                R, bt, pb, sl = B["R"], B["n"], B["posblk"], bi % 2
                cosb = cosT[0:R, pb, :]
                for (src, nh, keys) in ((kf[0:R, sl, :].rearrange("p (h d) -> p h d", d=64), 4, ["kf%d" % sl]),
                                        (qf[0:R, sl, :].rearrange("p (h d) -> p h d", d=64), 16, ["qf%d_0" % sl, "qf%d_1" % sl])):
                    P.op("pool", lambda e, R=R, src=src, nh=nh, cosb=cosb: e.tensor_tensor(
                        out=ra[0:R, 0:nh, :].rearrange("p h (t i) -> p h t i", i=8), in0=src[:, :, 0:16].rearrange("p h (t i) -> p h t i", i=8),
                        in1=cosb.unsqueeze(1).unsqueeze(1).to_broadcast([R, nh, 2, 8]), op=ALU.mult),
                        reads=keys + ["cosT"], writes=["ra"])
                    P.op("pool", lambda e, R=R, src=src, nh=nh, pb=pb: e.tensor_tensor(
                        out=rb[0:R, 0:nh, 0:8], in0=src[:, :, 8:16], in1=nsinT[0:R, pb, :].unsqueeze(1).to_broadcast([R, nh, 8]), op=ALU.mult),
                        reads=keys + ["nsinT"], writes=["rb"])
                    P.op("pool", lambda e, R=R, src=src, nh=nh, pb=pb: e.tensor_tensor(
                        out=rb[0:R, 0:nh, 8:16], in0=src[:, :, 0:8], in1=sinT[0:R, pb, :].unsqueeze(1).to_broadcast([R, nh, 8]), op=ALU.mult),
                        reads=keys + ["sinT"], writes=["rb"])
                    P.op("pool", lambda e, R=R, src=src, nh=nh: e.tensor_tensor(out=src[:, :, 0:16], in0=ra[0:R, 0:nh, :], in1=rb[0:R, 0:nh, :], op=ALU.add),
                         reads=["ra", "rb"], writes=keys)
                if B["kout"] is not None:
                    ok = "o_k_%d_%d" % (bt, bi)
                    P.op("sp", lambda e, R=R, sl=sl, dst=B["kout"]: e.dma_start(out=dst, in_=kf[0:R, sl, :]), reads=["kf%d" % sl], writes=[ok], dma="outk%d" % bi)
                    outkeys.append(ok)

            def pj2(bi):
                B = blocks[bi]
                R, kown, sl = B["R"], B["kbase"] + 128, bi % 2
                P.op("act", lambda e, R=R, sl=sl: e.activation(
                    out=kb2[0:R], in_=kf[0:R, sl, :].rearrange("p (h d) -> p h d", d=64).unsqueeze(2).to_broadcast([R, 4, 2, 64]), func=AF.Copy),
                    reads=["kf%d" % sl], writes=["kb2"])
                P.op("act", lambda e, R=R, sl=sl: e.activation(out=qb[0:R, sl, 0:512], in_=qf[0:R, sl, 0:512], func=AF.Copy, scale=0.125),
                     reads=["qf%d_0" % sl], writes=["qb%d_0" % sl])
                P.op("dve", lambda e, R=R, sl=sl: e.tensor_single_scalar(out=qb[0:R, sl, 512:1024], in_=qf[0:R, sl, 512:1024], scalar=0.125, op=ALU.mult),
                     reads=["qf%d_1" % sl], writes=["qb%d_1" % sl])
                for h in range(4):
                    P.op("pe", lambda e, R=R, h=h: e.transpose(out=ptp[:, 2, h * 128:h * 128 + R],
                                                                in_=kb2[0:R, h].rearrange("p t d -> p (t d)"), identity=identb[0:R, 0:R]),
                         reads=["kb2", "identb"], writes=["trb"])
                P.op("dve", lambda e, R=R, kown=kown: e.tensor_copy(
                    out=KT2[:, :, kown:kown + R], in_=ptp[:, 2, 0:512].rearrange("p (h t) -> p h t", t=128)[:, :, 0:R]),
                    reads=["trb"], writes=[B["kok"]])
                for c in range(8):
                    P.op("pe", lambda e, R=R, c=c, sl=sl: e.transpose(out=ptp[:, 2, c * 128:c * 128 + R], in_=qb[0:R, sl, c * 128:(c + 1) * 128],
                                                                       identity=identb[0:R, 0:R]),
                         reads=["qb%d_%d" % (sl, c // 4), "identb"], writes=["trb"])
                P.op("act", lambda e, R=R, bi=bi: e.activation(
                    out=qT[:, bi, :, 0:R], in_=ptp[:, 2, :].rearrange("p (c t) -> p c t", t=128)[:, :, 0:R], func=AF.Copy),
                    reads=["trb"], writes=["qT%d" % bi])

            units = [(bi, Q) for bi in range(len(blocks)) for Q in range(4)]
            ustate = {}

            def stA(u):
                bi, Q = units[u]
                B = blocks[bi]
                R, kbase = B["R"], B["kbase"]
                N = 128 + R
                ps_, st3 = u % 3, u % 4
                k = mm2()
                S = big[:, k:k + 2, :].rearrange("p a (g n) -> p (a g) n", n=256)
                kkeys = [B["khk"], B["kok"]]
                for g in (0, 2, 1, 3):
                    hq = 4 * Q + 2 * (g % 2) + g // 2
                    c, j = hq // 2, hq % 2
                    P.op("pe", lambda e, R=R, N=N, g=g, c=c, j=j, Q=Q, bi=bi, S=S, kbase=kbase: e.matmul(
                        S[0:R, g, 0:N], lhsT=qT[64 * j:64 * j + 64, bi, c, 0:R], rhs=KT2[64 * j:64 * j + 64, Q, kbase:kbase + N],
                        start=True, stop=True), reads=["qT%d" % bi] + kkeys, writes=["mm%d" % (k + g // 2)])
                skeys = ["mm%d" % k, "mm%d" % (k + 1)]
                sk = "stt%d" % st3
                P.op("dve", lambda e, R=R, N=N, S=S, st3=st3: e.tensor_reduce(
                    out=stt[0:R, st3, 0, 0:1], in_=S[0:R, :, 0:N], axis=AX.XY, op=ALU.max), reads=skeys, writes=[sk + "mx"])
                P.op("dve", lambda e, R=R, st3=st3, Q=Q: e.scalar_tensor_tensor(
                    out=stt[0:R, st3, 1, 0:1], in0=stt[0:R, st3, 0, 0:1], scalar=-1.0, in1=nsmax[0:R, Q:Q + 1], op0=ALU.mult, op1=ALU.min),
                    reads=[sk + "mx", "nsmax"], writes=[sk + "nb"])
                P.op("act", lambda e, R=R, N=N, S=S, ps_=ps_, st3=st3: e.activation(
                    out=Pm[0:R, ps_, :, 0:N], in_=S[0:R, :, 0:N], func=AF.Exp, bias=stt[0:R, st3, 1, 0:1], scale=1.0),
                    reads=skeys + [sk + "nb"], writes=["Pm%d" % ps_])
                if B["mask"]:
                    P.op("pool", lambda e, ps_=ps_: e.memset(Pm[0:64, ps_, :, 192:256], 0.0), writes=["Pm%d" % ps_])
                    if B["first"]:
                        P.op("pool", lambda e, ps_=ps_: e.memset(Pm[0:64, ps_, :, 0:128], 0.0), writes=["Pm%d" % ps_])
                        P.op("pool", lambda e, ps_=ps_: e.memset(Pm[64:128, ps_, :, 0:128], 0.0), writes=["Pm%d" % ps_])
                    else:
                        P.op("pool", lambda e, ps_=ps_: e.memset(Pm[64:128, ps_, :, 0:64], 0.0), writes=["Pm%d" % ps_])
                P.op("act", lambda e, R=R, st3=st3, Q=Q: e.activation(
                    out=stt[0:R, st3, 3, :], in_=sink_p[0:R, 4 * Q:4 * Q + 4], func=AF.Exp, bias=stt[0:R, st3, 1, 0:1], scale=1.0),
                    reads=[sk + "nb", "sink_p"], writes=[sk + "es"])

            def stB(u):
                bi, Q = units[u]
                B = blocks[bi]
                R = B["R"]
                ps_, pm_ = u % 2, u % 3
                for g in range(4):
                    for kb in range(2):
                        nk = 128 if kb == 0 else R
                        P.op("pe", lambda e, R=R, g=g, kb=kb, nk=nk, ps_=ps_, pm_=pm_: e.transpose(
                            out=ptp[0:nk, ps_, (kb * 4 + g) * 128:(kb * 4 + g) * 128 + R], in_=Pm[0:R, pm_, g, kb * 128:kb * 128 + nk],
                            identity=identb[0:R, 0:R]), reads=["Pm%d" % pm_, "identb"], writes=["pt%d" % ps_])
                if R == 128:
                    P.op("act", lambda e, ps_=ps_: e.activation(
                        out=PTs[:, ps_].rearrange("p k g t -> p (k g) t"), in_=ptp[:, ps_, :].rearrange("p (kg t) -> p kg t", t=128), func=AF.Copy),
                        reads=["pt%d" % ps_], writes=["PTs%d_0" % ps_, "PTs%d_1" % ps_])
                else:
                    for kb in range(2):
                        nk = 128 if kb == 0 else R
                        src = ptp[0:nk, ps_, kb * 512:(kb + 1) * 512].rearrange("p (g t) -> p g t", t=128)[:, :, 0:R]
                        P.op("dve", lambda e, nk=nk, R=R, kb=kb, ps_=ps_, src=src: e.tensor_copy(out=PTs[0:nk, ps_, kb, :, 0:R], in_=src),
                             reads=["pt%d" % ps_], writes=["PTs%d_%d" % (ps_, kb)])

            def stC(u):
                bi, Q = units[u]
                B = blocks[bi]
                R, hsl, osl = B["R"], B["hslot"], B["oslot"]
                ps_, st3, sl = u % 2, u % 4, bi % 2
                sk = "stt%d" % st3
                O3 = ops_[:, 0:260].rearrange("p (g d) -> p g d", d=65)
                for g in range(4):
                    for ki, kb in enumerate((1, 0)):
                        nk = 128 if kb == 0 else R
                        vs = hsl if kb == 0 else osl
                        P.op("pe", lambda e, R=R, g=g, kb=kb, ki=ki, nk=nk, vs=vs, Q=Q, ps_=ps_, O3=O3: e.matmul(
                            O3[0:R, g, :], lhsT=PTs[0:nk, ps_, kb, g, 0:R], rhs=Vaug[0:nk, vs, Q, :], start=(ki == 0), stop=(ki == 1)),
                            reads=["PTs%d_%d" % (ps_, kb), "Vaug%d" % vs], writes=["ops"])
                P.op("dve", lambda e, R=R, st3=st3, O3=O3: e.tensor_tensor(
                    out=stt[0:R, st3, 4, :], in0=O3[0:R, :, 64], in1=stt[0:R, st3, 3, :], op=ALU.add),
                    reads=["ops", sk + "es"], writes=[sk + "den"])
                P.op("dve", lambda e, R=R, st3=st3: e.reciprocal(out=stt[0:R, st3, 5, :], in_=stt[0:R, st3, 4, :]),
                     reads=[sk + "den"], writes=[sk + "ri"])
                P.op("dve", lambda e, R=R, st3=st3, Q=Q, sl=sl, O3=O3: e.tensor_tensor(
                    out=otok[0:R, sl, 256 * Q:256 * (Q + 1)].rearrange("p (t a d) -> p a t d", t=2, a=2, d=64),
                    in0=O3[0:R, :, 0:64].rearrange("p (a t) d -> p a t d", a=2),
                    in1=stt[0:R, st3, 5, :].rearrange("p (a t) -> p a t", a=2).unsqueeze(3).to_broadcast([R, 2, 2, 64]), op=ALU.mult),
                    reads=["ops", sk + "ri"], writes=["otok%d" % sl])
                if Q == 3:
                    c0 = B["c0"]
                    for c in range(8):
                        P.op("pe", lambda e, R=R, c=c, sl=sl: e.transpose(out=ptp[:, 2, c * 128:c * 128 + R], in_=otok[0:R, sl, c * 128:(c + 1) * 128],
                                                                           identity=identb[0:R, 0:R]),
                             reads=["otok%d" % sl, "identb"], writes=["trb"])
                    P.op("dve", lambda e, R=R, c0=c0: e.tensor_tensor(
                        out=ogT[:, :, c0:c0 + R], in0=ptp[:, 2, :].rearrange("p (c t) -> p c t", t=128)[:, :, 0:R], in1=szT[:, :, c0:c0 + R], op=ALU.mult),
                        reads=["trb"] + ["szT%d" % z for z in range(8)], writes=["ogT"])

            def zgate(zlist):
                for zc in zlist:
                    ch = CH_Z[zc // 4]
                    s = ring_state["loaded"][ch]
                    k = mm1()
                    for kc in range(KC):
                        P.op("pe", lambda e, s=s, kc=kc, k=k, zc=zc: e.matmul(
                            big[:, k, 0:T], lhsT=ring[:, s, kc, (zc % 4) * 128:(zc % 4 + 1) * 128], rhs=h2[:, kc, 0:T],
                            start=(kc == 0), stop=(kc == KC - 1)), reads=["ring%d" % s, "h2_%d" % kc], writes=["mm%d" % k])
                    for (c0, n, bt, ci) in segs:
                        P.op("act", lambda e, k=k, zc=zc, c0=c0, n=n, bt=bt: e.activation(
                            out=szT[:, zc, c0:c0 + n], in_=big[:, k, c0:c0 + n], func=AF.Silu, bias=bias_z[:, zc, bt:bt + 1], scale=1.0),
                            reads=["mm%d" % k, "bias_z"], writes=["szT%d" % zc])
                    if zc == 3:
                        ring_state["loaded"].pop(CH_Z[0]); ring_load(15)
                    if zc == 7:
                        ring_state["loaded"].pop(CH_Z[1]); ring_load(16)
            zgate(range(0, 4))
            pj0(0)
            pj1(0)
            zgate(range(4, 8))
            pj2(0)
            nu = len(units)
            DB, DC = 1, 2

            def proj_pieces(step):
                for nb in range(1, len(blocks)):
                    d = step - (4 * nb - 5)
                    if d == 0:
                        pjkv(nb)
                    elif d == 1:
                        pjq(nb, 0)
                    elif d == 2:
                        pjq(nb, 1)
                        pj1(nb)
                    elif d == 3:
                        pj2(nb)

            proj_pieces(-1)
            attn_mode[0] = True
            for s_ in range(nu + DC):
                if 0 <= s_ - DB < nu:
                    stB(s_ - DB)
                if s_ < nu:
                    stA(s_)
                if 0 <= s_ - DC < nu:
                    stC(s_ - DC)
                proj_pieces(s_)
            attn_mode[0] = False

            if blocks[0]["mask"]:
                P.op("pool", lambda e: e.tensor_copy(out=KT2[:, :, 0:128], in_=KT2[:, :, 512:640]), reads=["KT2_3"], writes=["KT2"])
                P.op("pool", lambda e: e.tensor_copy(out=Vaug[:, 0, :, 0:64], in_=Vaug[:, 4, :, 0:64]), reads=["Vaug4"], writes=["Vaug0"])
            ring_state["loaded"].pop(CH_KV); ring_state["loaded"].pop(CH_Q[0]); ring_state["loaded"].pop(CH_Q[1])
            if not last_tile_out:
                ring_load(0); ring_load(1); ring_load(2)

            if stage < 8:
                return
            na2 = NormAcc(T)
            for oc in range(8):
                ch = CH_O[oc // 4]
                s = ring_state["loaded"][ch]
                k = mm1()
                for ic in range(KC):
                    P.op("pe", lambda e, s=s, ic=ic, k=k, oc=oc: e.matmul(
                        big[:, k, 0:T], lhsT=ring[:, s, ic, (oc % 4) * 128:(oc % 4 + 1) * 128], rhs=ogT[:, ic, 0:T],
                        start=(ic == 0), stop=(ic == KC - 1)), reads=["ring%d" % s, "ogT"], writes=["mm%d" % k])
                for (c0, n, bt, ci) in segs:
                    P.op("dve", lambda e, k=k, oc=oc, c0=c0, n=n, bt=bt: e.scalar_tensor_tensor(
                        out=xT[:, oc, c0:c0 + n], in0=big[:, k, c0:c0 + n], scalar=gateB[:, oc, bt:bt + 1], in1=xT[:, oc, c0:c0 + n],
                        op0=ALU.mult, op1=ALU.add), reads=["mm%d" % k, "modT"] + xk(oc), writes=xk(oc))
                na2.add(oc, xk(oc))
                if oc == 3:
                    ring_state["loaded"].pop(CH_O[0])
                    if not last_tile_out:
                        ring_load(3)
                if oc == 7:
                    ring_state["loaded"].pop(CH_O[1])
                    if not last_tile_out:
                        ring_load(4)

            if stage < 9:
                return
            na2.finish()
            for kc in range(KC):
                P.op("dve", lambda e, kc=kc: e.scalar_tensor_tensor(
                    out=xT[:, kc, 0:T], in0=xT[:, kc, 0:T], scalar=gfT[:, kc:kc + 1], in1=rstd[:, 0:T], op0=ALU.mult, op1=ALU.mult),
                    reads=xk(kc) + ["rstd", "vecT"], writes=xk(kc))
            for (c0, rows, src, dst) in ioblocks:
                sl = yout_ctr[0] % 2
                yout_ctr[0] += 1
                for half in range(2):
                    k = mm1()
                    for q4 in range(4):
                        kc = half * 4 + q4
                        P.op("pe", lambda e, rows=rows, kc=kc, q4=q4, k=k, c0=c0: e.transpose(
                            out=big[0:rows, k, q4 * 128:(q4 + 1) * 128], in_=xT[:, kc, c0:c0 + rows], identity=identf[:]),
                            reads=xk(kc) + ["identf"], writes=["mm%d" % k])
                    P.op("act", lambda e, half=half, k=k, rows=rows, sl=sl: e.activation(
                        out=ytok[0:rows, sl, half * 512:(half + 1) * 512], in_=big[0:rows, k, :], func=AF.Copy),
                        reads=["mm%d" % k], writes=["ytok%d_%d" % (sl, half)])
                ok = "o_y_%d" % yout_ctr[0]
                P.op("sp", lambda e, rows=rows, sl=sl, dst=dst: e.dma_start(out=dst, in_=ytok[0:rows, sl, :]),
                     reads=["ytok%d_0" % sl, "ytok%d_1" % sl], writes=[ok], dma="yout%d" % sl)
                outkeys.append(ok)

        ring_state["loaded"].clear()
        for ch in (0, 1, 2, 3, 4):
            ring_load(ch)
        tiles = []
        for t in range(NT):
            segs = [(0, 512, 0, 0)]
            iob = [(b * 128, 128, x_p[t * 512 + b * 128: t * 512 + (b + 1) * 128, :], y_p[t * 512 + b * 128: t * 512 + (b + 1) * 128, :]) for b in range(4)]
            blocks = []
            for b in range(4):
                last = (t == NT - 1 and b == 3)
                blocks.append(dict(c0=b * 128, R=128, n=0, kbase=b * 128, hslot=b, oslot=b + 1, posblk=t * 4 + b, mask=True,
                                   khk=("KT2" if b == 0 else "KT2_%d" % (b - 1)), kok="KT2_%d" % b,
                                   first=(t == 0 and b == 0), kout=(k_p[:, :] if last else None), vout=(v_p[:, :] if last else None)))
            tiles.append((512, segs, iob, blocks))
        if with_sample:
            segs = [(0, 16, 1, 1), (16, 16, 2, 2)]
            iob = [(0, 32, x_s[:, :], y_s[:, :])]
            blocks = []
            for s_ in range(2):
                blocks.append(dict(c0=16 * s_, R=16, n=1 + s_, kbase=256 * s_, hslot=2 * s_, oslot=2 * s_ + 1, posblk=64, mask=False,
                                   khk=("KT2" if s_ == 0 else "KT2_1"), kok=("KT2_0" if s_ == 0 else "KT2_2"),
                                   first=False, kout=k_s[s_, 112:128, :], vout=v_s[s_, 112:128, :]))
            tiles.append((32, segs, iob, blocks))

        if stage < 1:
            tiles = []
        for ti, (T, segs, iob, blocks) in enumerate(tiles):
            is_sample = with_sample and ti == len(tiles) - 1
            if is_sample:
                for s_ in range(2):
                    P.op("sp", lambda e, s_=s_: e.dma_start(out=kf[:, s_, :], in_=cache_k[s_]), writes=["kf%d" % s_], dma="cink%d" % s_)
                    P.op("sp", lambda e, s_=s_: e.dma_start(out=vf[:, :], in_=cache_v[s_]), writes=["vf"], dma="cinv%d" % s_)
                    P.op("dve", lambda e, s_=s_: e.tensor_copy(out=Vaug[:, 2 * s_, :, 0:64], in_=vf[:, :].rearrange("p (h d) -> p h d", d=64)),
                         reads=["vf"], writes=["Vaug%d" % (2 * s_)])
                    P.op("act", lambda e, s_=s_: e.activation(
                        out=kb2[:], in_=kf[:, s_, :].rearrange("p (h d) -> p h d", d=64).unsqueeze(2).to_broadcast([128, 4, 2, 64]), func=AF.Copy),
                        reads=["kf%d" % s_], writes=["kb2"])
                    for h in range(4):
                        P.op("pe", lambda e, h=h: e.transpose(out=ptp[:, 2, h * 128:(h + 1) * 128], in_=kb2[:, h].rearrange("p t d -> p (t d)"),
                                                               identity=identb[:]), reads=["kb2", "identb"], writes=["trb"])
                    P.op("dve", lambda e, s_=s_: e.tensor_copy(out=KT2[:, :, 256 * s_:256 * s_ + 128],
                                                               in_=ptp[:, 2, 0:512].rearrange("p (h t) -> p h t", t=128)),
                         reads=["trb"], writes=["KT2"] if s_ == 0 else ["KT2_1"])
            run_tile(T, segs, iob, blocks, ti, (tiles[ti + 1][2] if ti + 1 < len(tiles) else None), last_tile_out=(ti == len(tiles) - 1))
            if ti == NT - 1:
                for t_ in range(2):
                    P.op("sp", lambda e, t_=t_: e.dma_start(out=conv_p[t_].rearrange("(c p) -> p c", p=128), in_=carry[:, 0, :, t_]),
                         reads=["carry0"], writes=["o_convp%d" % t_], dma="outm")
                    outkeys.append("o_convp%d" % t_)
            if is_sample:
                for s_ in range(2):
                    for t_ in range(2):
                        P.op("sp", lambda e, s_=s_, t_=t_: e.dma_start(out=conv_s[s_, t_].rearrange("(c p) -> p c", p=128), in_=carry[:, 1 + s_, :, t_]),
                             reads=["carry%d" % (1 + s_)], writes=["o_convs%d_%d" % (s_, t_)], dma="outm")
                        outkeys.append("o_convs%d_%d" % (s_, t_))

        P.emit(final_wait_keys=outkeys)
        build_program.stats = P.stats
    return nc


_CACHE = {}


def _get_program(SEQ):
    if SEQ not in _CACHE:
        _CACHE[SEQ] = build_program(SEQ)
    return _CACHE[SEQ]


def kernel(x_prompt, x_sample, c_prompt, c_sample, state_conv, cache_k, cache_v,
           g_a, w_mod_a, b_mod_a, w_in_a, conv_w_a, w_out_a,
           g_kv, w_mod_kv, b_mod_kv, w_kv,
           g_b, w_mod_b, b_mod_b, w_qz_b, w_o_b, sinks_b, g_final):
    f = lambda a: np.ascontiguousarray(np.asarray(a, dtype=np.float32))
    x_prompt, x_sample = f(x_prompt), f(x_sample)
    B, SEQ, _ = x_prompt.shape
    nc = _get_program(SEQ)
    shared = {
        "g_a": f(g_a)[0], "w_mod_a": f(w_mod_a)[0], "b_mod_a": f(b_mod_a)[0], "w_in": f(w_in_a)[0],
        "conv_w": f(conv_w_a)[0], "w_out": f(w_out_a)[0], "g_kv": f(g_kv), "w_mod_kv": f(w_mod_kv),
        "b_mod_kv": f(b_mod_kv), "w_kv": f(w_kv), "g_b": f(g_b)[0], "w_mod_b": f(w_mod_b)[0],
        "b_mod_b": f(b_mod_b)[0], "w_qz": f(w_qz_b)[0], "w_o": f(w_o_b)[0], "sinks": f(sinks_b)[0], "g_f": f(g_final),
    }
    c_prompt, c_sample, state_conv = f(c_prompt), f(c_sample), f(state_conv)
    cache_k, cache_v = f(cache_k), f(cache_v)
    in_maps = []
    for i in range(NCORES):
        m = dict(shared)
        m["x_p"] = x_prompt[i]
        m["x_s"] = x_sample[2 * i:2 * i + 2].reshape(32, D)
        m["cmat"] = np.ascontiguousarray(np.stack([c_prompt[i], c_sample[2 * i], c_sample[2 * i + 1]], axis=0))
        m["st_conv"] = np.ascontiguousarray(state_conv[0, 2 * i:2 * i + 2])
        m["cache_k"] = np.ascontiguousarray(cache_k[2 * i:2 * i + 2].reshape(2, 128, 256))
        m["cache_v"] = np.ascontiguousarray(cache_v[2 * i:2 * i + 2].reshape(2, 128, 256))
        in_maps.append(m)
    res = run_bass_kernel_spmd(nc, in_maps, core_ids=list(range(NCORES)))
    r = res.results
    y_prompt = np.stack([r[i]["y_p"] for i in range(NCORES)], axis=0)
    y_sample = np.concatenate([r[i]["y_s"].reshape(2, 16, D) for i in range(NCORES)], axis=0)
    conv_p = np.stack([r[i]["conv_p"] for i in range(NCORES)], axis=0)[None]
    conv_s = np.concatenate([r[i]["conv_s"] for i in range(NCORES)], axis=0)[None]
    k_p = np.stack([r[i]["k_p"].reshape(128, 4, 64) for i in range(NCORES)], axis=0)
    v_p = np.stack([r[i]["v_p"].reshape(128, 4, 64) for i in range(NCORES)], axis=0)
    k_s = np.concatenate([r[i]["k_s"].reshape(2, 128, 4, 64) for i in range(NCORES)], axis=0)
    v_s = np.concatenate([r[i]["v_s"].reshape(2, 128, 4, 64) for i in range(NCORES)], axis=0)
    return (y_prompt, y_sample, conv_p, conv_s, k_p, v_p, k_s, v_s)
```

```python
import contextlib
import math
import numpy as np
import concourse.bass as bass
import concourse.mybir as mybir
from concourse.bass_utils import run_bass_kernel_spmd

F32 = mybir.dt.float32
BF16 = mybir.dt.bfloat16
I32 = mybir.dt.int32
ALU = mybir.AluOpType
AF = mybir.ActivationFunctionType
AX = mybir.AxisListType

D = 1024
KC = 8
NCORES = 8
EPS = 1e-6
ROPE_THETA = 500000.0
PAST_LEN = 4096
NSLOT = 5
ENGS = ("pe", "act", "dve", "pool", "sp")


class _Op:
    __slots__ = ("eng", "fn", "deps", "idx", "is_dma", "dma_sem", "sig", "token")


class Prog:
    def __init__(self, nc):
        self.nc = nc
        self.ops = []
        self.last_w = {}
        self.readers = {}

    def op(self, eng, fn, reads=(), writes=(), dma=None):
        o = _Op()
        o.eng, o.fn, o.idx = eng, fn, len(self.ops)
        o.is_dma, o.dma_sem, o.sig, o.token = dma is not None, dma, False, None
        deps = {}
        psum_reads = [k for k in reads if k.startswith(("mm", "pt", "trb", "ops"))]
        writes = list(writes) + psum_reads
        for k in reads:
            w = self.last_w.get(k)
            if w is not None:
                deps[w] = True
        for k in writes:
            w = self.last_w.get(k)
            if w is not None:
                deps.setdefault(w, False)
            for r in self.readers.get(k, ()):
                deps.setdefault(r, False)
        o.deps = deps
        for k in reads:
            self.readers.setdefault(k, []).append(o.idx)
        for k in writes:
            self.last_w[k] = o.idx
            self.readers[k] = []
        self.ops.append(o)
        return o.idx

    def emit(self, final_wait_keys=()):
        nc, ops = self.nc, self.ops
        fdeps = set()
        for k in final_wait_keys:
            w = self.last_w.get(k)
            if w is not None:
                fdeps.add(w)
        needed = set(fdeps)
        for o in ops:
            best = {}
            for d, raw in o.deps.items():
                p = ops[d]
                if (not o.is_dma) and (not p.is_dma) and p.eng == o.eng and o.eng == "pe":
                    continue
                key = ("d", p.dma_sem) if p.is_dma else ("e", p.eng)
                if key not in best or best[key] < d:
                    best[key] = d
            o.deps = set(best.values())
            needed.update(o.deps)
        for d in needed:
            ops[d].sig = True
        stack = contextlib.ExitStack()
        eng_sem = {e: stack.enter_context(nc.semaphore("s_" + e)) for e in ENGS}
        dma_sem = {}
        for o in ops:
            if o.is_dma and o.dma_sem not in dma_sem:
                dma_sem[o.dma_sem] = stack.enter_context(nc.semaphore("d_" + o.dma_sem))
        cnt = {e: 0 for e in ENGS}
        dcnt = {k: 0 for k in dma_sem}
        for o in ops:
            if o.is_dma:
                dcnt[o.dma_sem] += 16
                o.token = (dma_sem[o.dma_sem], dcnt[o.dma_sem], "d_" + o.dma_sem)
            elif o.sig:
                cnt[o.eng] += 1
                o.token = (eng_sem[o.eng], cnt[o.eng], "s_" + o.eng)
        per_eng = {e: [o for o in ops if o.eng == e] for e in ENGS}
        self.stats = {"ops": len(ops), "per_eng": {e: len(v) for e, v in per_eng.items()}, "sig": dict(cnt)}

        def run(e, h):
            waited = {}
            for o in per_eng[e]:
                for d in sorted(o.deps):
                    sem, val, sname = ops[d].token
                    if waited.get(sname, 0) >= val:
                        continue
                    h.wait_ge(sem, val)
                    waited[sname] = val
                ins = o.fn(h)
                if o.is_dma:
                    ins.then_inc(o.token[0], 16)
                elif o.sig:
                    ins.then_inc(o.token[0], 1)
            if e == "sp":
                fin = {}
                for d in fdeps:
                    sem, val, sname = ops[d].token
                    if sname not in fin or fin[sname][1] < val:
                        fin[sname] = (sem, val)
                for sname, (sem, val) in sorted(fin.items()):
                    if waited.get(sname, 0) >= val:
                        continue
                    h.wait_ge(sem, val)
                    waited[sname] = val

        with stack:
            with nc.Block() as block:
                @block.tensor
                def _(h):
                    run("pe", h)

                @block.scalar
                def _(h):
                    run("act", h)

                @block.vector
                def _(h):
                    run("dve", h)

                @block.gpsimd
                def _(h):
                    run("pool", h)

                @block.sync
                def _(h):
                    run("sp", h)


def build_program(SEQ, with_sample=True, debug=False, stage=99):
    assert SEQ % 512 == 0
    NT = SEQ // 512
    nc = bass.Bass("TRN2", target_bir_lowering=False)

    def din(name, shape):
        return nc.dram_tensor(name, list(shape), F32, kind="ExternalInput").ap()

    def dout(name, shape):
        return nc.dram_tensor(name, list(shape), F32, kind="ExternalOutput").ap()

    x_p = din("x_p", [SEQ, D])
    x_s = din("x_s", [32, D])
    cmat = din("cmat", [3, D])
    st_conv = din("st_conv", [2, 2, D])
    cache_k = din("cache_k", [2, 128, 256])
    cache_v = din("cache_v", [2, 128, 256])
    g_a = din("g_a", [D]); w_mod_a = din("w_mod_a", [D, 3 * D]); b_mod_a = din("b_mod_a", [3 * D])
    w_in = din("w_in", [D, 4 * D]); conv_w = din("conv_w", [3, D]); w_out = din("w_out", [D, D])
    g_kv = din("g_kv", [D]); w_mod_kv = din("w_mod_kv", [D, 2 * D]); b_mod_kv = din("b_mod_kv", [2 * D])
    w_kv = din("w_kv", [D, 512])
    g_b = din("g_b", [D]); w_mod_b = din("w_mod_b", [D, 3 * D]); b_mod_b = din("b_mod_b", [3 * D])
    w_qz = din("w_qz", [D, 2 * D]); w_o = din("w_o", [D, D]); sinks = din("sinks", [16]); g_f = din("g_f", [D])

    y_p = dout("y_p", [SEQ, D]); y_s = dout("y_s", [32, D])
    conv_p = dout("conv_p", [2, D]); conv_s = dout("conv_s", [2, 2, D])
    k_p = dout("k_p", [128, 256]); v_p = dout("v_p", [128, 256])
    k_s = dout("k_s", [2, 128, 256]); v_s = dout("v_s", [2, 128, 256])

    NCH = 17
    scr = nc.dram_tensor("wscr", [NCH, 128, KC, 512], BF16).ap()

    st = contextlib.ExitStack()

    def sb(name, shape, dt):
        return st.enter_context(nc.sbuf_tensor(name, list(shape), dt))

    def ps(name, shape, dt):
        return st.enter_context(nc.psum_tensor(name, list(shape), dt))

    with st:
        st.enter_context(nc.allow_non_contiguous_dma(reason="small strided vector loads/stores"))
        P = Prog(nc)
        outkeys = []

        ring = sb("ring", [128, NSLOT, KC, 512], BF16)
        identf = sb("identf", [128, 128], F32)
        identb = sb("identb", [128, 128], BF16)
        onesm = sb("onesm", [128, 128], BF16)
        ones1 = sb("ones1", [128, 128], BF16)
        iot = sb("iot", [128, 128], I32)
        cosT = sb("cosT", [128, 65, 8], F32)
        sinT = sb("sinT", [128, 65, 8], F32)
        nsinT = sb("nsinT", [128, 65, 8], F32)
        posi = sb("posi", [128, 65], I32)
        posf = sb("posf", [128, 65], F32)
        vecs = sb("vecs", [128, 128], F32)
        vecT = sb("vecT", [128, 128], F32)
        modT = sb("modT", [128, 64, 3], F32)
        cT = sb("cT", [128, KC, 3], F32)
        scT = sb("scT", [128, KC, 3], BF16)
        mrow = sb("mrow", [3, 512], F32)
        gsA = sb("gsA", [128, KC, 3], F32); gsKV = sb("gsKV", [128, KC, 3], F32); gsB = sb("gsB", [128, KC, 3], F32)
        shpad = sb("shpad", [128, 3, KC, 65], BF16)
        bias_in = sb("bias_in", [128, 32, 3], F32)
        bias_z = sb("bias_z", [128, 8, 3], F32)
        brow_hi = sb("brow_hi", [65, 1536], BF16)
        sink_bc = sb("sink_bc", [128, 16], F32)
        nsink = sb("nsink", [128, 16], F32)
        sink_p = sb("sink_p", [128, 16], F32)
        nsmax = sb("nsmax", [128, 4], F32)
        nsink_p = sb("nsink_p", [128, 16], F32)
        carry = sb("carry", [128, 3, KC, 2], F32)

        xtok = sb("xtok", [128, 2, D], F32)
        ytok = sb("ytok", [128, 2, D], F32)
        xT = sb("xT", [128, KC, 512], F32)
        def _alias(a):
            return xT[:, a:a + 2, :].rearrange("p a b -> p (a b)")[:, 0:520].rearrange("p (b i) -> p b i", i=8)
        rtmp, rtmp2, rki = _alias(0), _alias(2), _alias(4).bitcast(I32)
        sq = sb("sq", [128, 2, 512], BF16)
        rstd = sb("rstd", [128, 512], F32)
        h1 = sb("h1", [128, KC, 512], BF16)
        h2 = sb("h2", [128, KC, 512], BF16)
        mT = sb("mT", [128, KC, 512], BF16)
        c_sb = sb("c_sb", [128, 2, 512], F32)
        vtmp = sb("vtmp", [128, 2, 516], F32)
        ycv = sb("ycv", [128, 2, 512], F32)
        szA = sb("szA", [128, 2, 512], F32)
        szT = sb("szT", [128, KC, 512], BF16)
        ogT = sb("ogT", [128, KC, 512], BF16)
        kf = sb("kf", [128, 2, 256], F32)
        vf = sb("vf", [128, 256], F32)
        kb2 = sb("kb2", [128, 4, 2, 64], BF16)
        KT2 = sb("KT2", [128, 4, 640], BF16)
        Vaug = sb("Vaug", [128, 5, 4, 65], BF16)
        qf = sb("qf", [128, 2, D], F32)
        ra = sb("ra", [128, 16, 16], F32)
        rb = sb("rb", [128, 16, 16], F32)
        qb = sb("qb", [128, 2, D], BF16)
        qT = sb("qT", [128, 4, KC, 128], BF16)
        Pm = sb("Pm", [128, 4, 4, 256], BF16)
        PTs = sb("PTs", [128, 2, 2, 4, 128], BF16)
        stt = sb("stt", [128, 5, 8, 4], F32)
        otok = sb("otok", [128, 2, D], BF16)

        big = ps("big", [128, 4, 512], F32)
        ptp = ps("ptp", [128, 3, 1024], BF16)
        ops_ = ps("ops_", [128, 512], F32)

        mmctr = [0]

        mm_reserved = set()

        attn_mode = [False]

        def mm1():
            if attn_mode[0]:
                mmctr[0] += 1
                return 2 + mmctr[0] % 2
            while True:
                k = mmctr[0] % 4
                mmctr[0] += 1
                if k not in mm_reserved:
                    return k

        def mm2():
            if attn_mode[0]:
                return 0
            while True:
                if mmctr[0] % 2:
                    mmctr[0] += 1
                k = mmctr[0] % 4
                mmctr[0] += 2
                if k not in mm_reserved and (k + 1) not in mm_reserved:
                    return k

        P.op("pool", lambda e: e.iota(iot[:], [[1, 128]], base=0, channel_multiplier=-1), writes=["iot"])
        P.op("dve", lambda e: e.tensor_single_scalar(out=identf[:], in_=iot[:], scalar=0, op=ALU.is_equal),
             reads=["iot"], writes=["identf"])
        P.op("dve", lambda e: e.tensor_copy(out=identb[:], in_=identf[:]), reads=["identf"], writes=["identb"])
        def wview(w):
            return w.rearrange("(kc p) f -> p kc f", p=128)

        w_in_v, w_out_v, w_kv_v, w_qz_v, w_o_v = wview(w_in), wview(w_out), wview(w_kv), wview(w_qz), wview(w_o)
        CH_IN = list(range(8)); CH_OUT = [8, 9]; CH_Z = [10, 11]; CH_KV = 12; CH_Q = [13, 14]; CH_O = [15, 16]
        cast_q = []
        for i in range(8):
            for sec in range(4):
                cast_q.append(lambda i=i, sec=sec: P.op("pool", lambda e: e.dma_start(
                    out=scr[i, :, :, sec * 128:(sec + 1) * 128], in_=w_in_v[:, :, sec * 1024 + i * 128: sec * 1024 + (i + 1) * 128]),
                    writes=["scr%d_%d" % (i, sec)], dma="wc%d" % i))
        scr_keys = {i: ["scr%d_%d" % (i, s) for s in range(4)] for i in range(8)}

        def cast_chunk(ch, src, c0):
            cast_q.append(lambda: P.op("pool", lambda e: e.dma_start(out=scr[ch], in_=src[:, :, c0:c0 + 512]), writes=["scr%d" % ch], dma="wc%d" % ch))
            scr_keys[ch] = ["scr%d" % ch]

        cast_chunk(8, w_out_v, 0); cast_chunk(9, w_out_v, 512)
        cast_chunk(10, w_qz_v, 1024); cast_chunk(11, w_qz_v, 1536)
        cast_chunk(12, w_kv_v, 0)
        cast_chunk(13, w_qz_v, 0); cast_chunk(14, w_qz_v, 512)
        cast_chunk(15, w_o_v, 0); cast_chunk(16, w_o_v, 512)

        P.op("pool", lambda e: e.memset(onesm[:], 1.0 / 1024.0), writes=["onesm"])
        P.op("pool", lambda e: e.memset(ones1[:], 1.0), writes=["ones1"])
        P.op("pool", lambda e: e.memset(shpad[:], 0.0), writes=["shpad"])
        P.op("pool", lambda e: e.memset(vecs[:], 0.0), writes=["vecs"])
        P.op("pool", lambda e: e.memset(carry[:], 0.0), writes=["carry0", "carry1", "carry2"])
        P.op("pool", lambda e: e.memset(KT2[:], 0.0), writes=["KT2"])
        P.op("pool", lambda e: e.memset(Vaug[:], 0.0), writes=["Vaug%d" % s for s in range(5)])
        P.op("pool", lambda e: e.memset(Vaug[:, :, :, 64:65], 1.0), writes=["Vaug%d" % s for s in range(5)])
        P.op("pool", lambda e: e.memset(Pm[:], 0.0), writes=["Pm0", "Pm1", "Pm2", "Pm3"])
        P.op("pool", lambda e: e.memset(vtmp[:], 0.0), writes=["vtmp0", "vtmp1"])

        P.op("pool", lambda e: e.iota(posi[:, 0:64], [[128, 64]], base=0, channel_multiplier=1), writes=["posi"])
        P.op("pool", lambda e: e.iota(posi[:, 64:65], [[0, 1]], base=PAST_LEN, channel_multiplier=1), writes=["posi"])
        P.op("dve", lambda e: e.tensor_copy(out=posf[:], in_=posi[:]), reads=["posi"], writes=["posf"])
        inv = [float(np.float32(1.0) / (np.float32(ROPE_THETA) ** (np.float32(2 * i) / np.float32(16)))) for i in range(8)]
        TWO_PI = 2.0 * math.pi
        C1 = float(np.float32(6.28125))
        C2 = float(TWO_PI - 6.28125)
        for i in range(8):
            P.op("dve", lambda e, i=i: e.tensor_single_scalar(out=rtmp[:, :, i], in_=posf[:], scalar=inv[i], op=ALU.mult),
                 reads=["posf"], writes=["xT0", "xT1"])

        def sincos(dst, phase, dname):
            P.op("dve", lambda e: e.tensor_scalar(out=rtmp2[:], in0=rtmp[:], scalar1=phase, scalar2=1.0 / TWO_PI, op0=ALU.add, op1=ALU.mult),
                 reads=["xT0", "xT1"], writes=["xT2", "xT3"])
            P.op("dve", lambda e: e.tensor_copy(out=rki[:], in_=rtmp2[:]), reads=["xT2", "xT3"], writes=["xT4", "xT5"])
            P.op("dve", lambda e: e.tensor_copy(out=rtmp2[:], in_=rki[:]), reads=["xT4", "xT5"], writes=["xT2", "xT3"])
            P.op("dve", lambda e: e.scalar_tensor_tensor(out=dst[:], in0=rtmp2[:], scalar=-C1, in1=rtmp[:], op0=ALU.mult, op1=ALU.add),
                 reads=["xT2", "xT3", "xT0", "xT1"], writes=[dname])
            P.op("dve", lambda e: e.tensor_single_scalar(out=dst[:], in_=dst[:], scalar=phase, op=ALU.add),
                 reads=[dname], writes=[dname])
            P.op("dve", lambda e: e.scalar_tensor_tensor(out=dst[:], in0=rtmp2[:], scalar=-C2, in1=dst[:], op0=ALU.mult, op1=ALU.add),
                 reads=["xT2", "xT3", dname], writes=[dname])
            for sgn in (1.0, -1.0):
                cmpop = ALU.is_gt if sgn > 0 else ALU.is_lt
                P.op("dve", lambda e, cmpop=cmpop, sgn=sgn: e.tensor_single_scalar(out=rtmp2[:], in_=dst[:], scalar=sgn * math.pi, op=cmpop),
                     reads=[dname], writes=["xT2", "xT3"])
                P.op("dve", lambda e, sgn=sgn: e.scalar_tensor_tensor(out=dst[:], in0=rtmp2[:], scalar=-sgn * TWO_PI, in1=dst[:], op0=ALU.mult, op1=ALU.add),
                     reads=["xT2", "xT3", dname], writes=[dname])
            P.op("dve", lambda e: e.tensor_scalar(out=dst[:], in0=dst[:], scalar1=-math.pi, scalar2=math.pi, op0=ALU.max, op1=ALU.min),
                 reads=[dname], writes=[dname])
            P.op("act", lambda e: e.activation(out=dst[:], in_=dst[:], func=AF.Sin), reads=[dname], writes=[dname])

        sincos(sinT, 0.0, "sinT")
        sincos(cosT, math.pi / 2.0, "cosT")
        P.op("dve", lambda e: e.tensor_single_scalar(out=nsinT[:], in_=sinT[:], scalar=-1.0, op=ALU.mult),
             reads=["sinT"], writes=["nsinT"])

        ring_state = {"next_slot": 0, "loaded": {}}

        def ring_load(ch):
            s = ring_state["next_slot"] % NSLOT
            ring_state["next_slot"] += 1
            P.op("sp", lambda e: e.dma_start(out=ring[:, s], in_=scr[ch]), reads=scr_keys[ch], writes=["ring%d" % s], dma="ring%d" % s)
            ring_state["loaded"][ch] = s
            return s

        def vload(r0, src, nrows):
            P.op("sp", lambda e: e.dma_start(out=vecs[r0:r0 + nrows, :], in_=src.rearrange("(r c) -> r c", c=128)),
                 reads=["vecs"], writes=["vecs_%d" % r0], dma="vec%d" % r0)
            return "vecs_%d" % r0

        vk = [vload(0, b_mod_a, 24), vload(24, b_mod_kv, 16), vload(40, b_mod_b, 24), vload(64, g_a, 8),
              vload(72, g_kv, 8), vload(80, g_b, 8), vload(88, g_f, 8), vload(96, conv_w.rearrange("t d -> (t d)"), 24)]
        k0 = mm1()
        P.op("pe", lambda e: e.transpose(out=big[:, k0, 0:128], in_=vecs[:, :], identity=identf[:]),
             reads=vk + ["vecs", "identf"], writes=["mm%d" % k0])
        P.op("dve", lambda e: e.tensor_copy(out=vecT[:], in_=big[:, k0, 0:128]), reads=["mm%d" % k0], writes=["vecT"])
        gaT, gkvT, gbT, gfT = vecT[:, 64:72], vecT[:, 72:80], vecT[:, 80:88], vecT[:, 88:96]
        cwT = vecT[:, 96:120]

        P.op("sp", lambda e: e.dma_start(out=sink_bc[:], in_=sinks.partition_broadcast(128)), writes=["sink_bc"], dma="vsink")
        P.op("dve", lambda e: e.tensor_single_scalar(out=nsink[:], in_=sink_bc[:], scalar=-1.0, op=ALU.mult),
             reads=["sink_bc"], writes=["nsink"])
        P.op("dve", lambda e: e.tensor_reduce(out=nsmax[:], in_=sink_bc[:].rearrange("p (q g) -> p q g", g=4), axis=AX.X, op=ALU.max),
             reads=["sink_bc"], writes=["nsmax"])
        P.op("dve", lambda e: e.tensor_single_scalar(out=nsmax[:], in_=nsmax[:], scalar=-1.0, op=ALU.mult), reads=["nsmax"], writes=["nsmax"])
        P.op("dve", lambda e: e.tensor_copy(out=sink_p[:].rearrange("p (q a t) -> p q a t", a=2, t=2),
                                            in_=sink_bc[:].rearrange("p (q t a) -> p q a t", a=2, t=2)), reads=["sink_bc"], writes=["sink_p"])
        P.op("dve", lambda e: e.tensor_copy(out=nsink_p[:].rearrange("p (q a t) -> p q a t", a=2, t=2),
                                            in_=nsink[:].rearrange("p (q t a) -> p q a t", a=2, t=2)), reads=["nsink"], writes=["nsink_p"])

        for n_ in range(3):
            P.op("sp", lambda e, n_=n_: e.dma_start(out=cT[:, :, n_], in_=cmat[n_].rearrange("(kc p) -> p kc", p=128)), writes=["cT"], dma="vcT")
        P.op("act", lambda e: e.activation(out=scT[:], in_=cT[:], func=AF.Silu), reads=["cT"], writes=["scT"])

        mods = [(w_mod_a, 3 * D, 0), (w_mod_kv, 2 * D, 24), (w_mod_b, 3 * D, 40)]
        for wm, width, j0 in mods:
            wmv = wview(wm)
            for blk in range(width // 512):
                s = ring_state["next_slot"] % NSLOT
                ring_state["next_slot"] += 1
                P.op("pool", lambda e, s=s, wmv=wmv, blk=blk: e.dma_start(out=ring[:, s], in_=wmv[:, :, blk * 512:(blk + 1) * 512]),
                     writes=["ring%d" % s], dma="ringp%d" % s)
                for _ in range(3):
                    if cast_q:
                        cast_q.pop(0)()
                k = mm1()
                for kc in range(KC):
                    P.op("pe", lambda e, s=s, kc=kc, k=k: e.matmul(big[0:3, k, :], lhsT=scT[:, kc, :], rhs=ring[:, s, kc, :],
                                                                   start=(kc == 0), stop=(kc == KC - 1)),
                         reads=["scT", "ring%d" % s], writes=["mm%d" % k])
                P.op("dve", lambda e, k=k: e.tensor_copy(out=mrow[:], in_=big[0:3, k, :]), reads=["mm%d" % k], writes=["mrow"])
                k2 = mm1()
                for q4 in range(4):
                    P.op("pe", lambda e, q4=q4, k2=k2: e.transpose(out=big[:, k2, q4 * 4:q4 * 4 + 3], in_=mrow[:, q4 * 128:(q4 + 1) * 128],
                                                                  identity=identf[0:3, 0:3]),
                         reads=["mrow", "identf"], writes=["mm%d" % k2])
                jj = j0 + blk * 4
                P.op("dve", lambda e, k2=k2, jj=jj: e.tensor_tensor(
                    out=modT[:, jj:jj + 4, :], in0=big[:, k2, 0:16].rearrange("p (q n) -> p q n", n=4)[:, :, 0:3],
                    in1=vecT[:, jj:jj + 4].unsqueeze(2).to_broadcast([128, 4, 3]), op=ALU.add),
                    reads=["mm%d" % k2, "vecT"], writes=["modT"])
        while cast_q:
            cast_q.pop(0)()
        gateA = modT[:, 16:24, :]
        gateB = modT[:, 56:64, :]
        for dst, sc0, gT, dname in ((gsA, 8, gaT, "gsA"), (gsKV, 32, gkvT, "gsKV"), (gsB, 48, gbT, "gsB")):
            P.op("dve", lambda e, dst=dst, sc0=sc0, gT=gT: e.scalar_tensor_tensor(
                out=dst[:], in0=modT[:, sc0:sc0 + 8, :], scalar=1.0, in1=gT.unsqueeze(2).to_broadcast([128, 8, 3]),
                op0=ALU.add, op1=ALU.mult), reads=["modT", "vecT"], writes=[dname])
        for si, sh0 in enumerate((0, 24, 40)):
            for n in range(3):
                P.op("dve", lambda e, si=si, sh0=sh0, n=n: e.tensor_copy(out=shpad[:, si, :, 32 * n], in_=modT[:, sh0:sh0 + 8, n]),
                     reads=["modT", "shpad"], writes=["shpad"])

        def bias_fm(ch, s):
            for blk in range(4):
                k = mm1()
                si = 0 if ch < 8 else 2
                for kc in range(KC):
                    P.op("pe", lambda e, s=s, kc=kc, k=k, blk=blk, si=si: e.matmul(
                        big[:, k, 0:65], lhsT=ring[:, s, kc, blk * 128:(blk + 1) * 128], rhs=shpad[:, si, kc, :],
                        start=(kc == 0), stop=(kc == KC - 1)), reads=["ring%d" % s, "shpad"], writes=["mm%d" % k])
                if ch < 8:
                    dstap = bias_in[:, blk * 8 + ch, :]
                    dk = "bias_in"
                else:
                    dstap = bias_z[:, (ch - CH_Z[0]) * 4 + blk, :]
                    dk = "bias_z"
                P.op("dve", lambda e, k=k, dstap=dstap: e.tensor_copy(out=dstap, in_=big[:, k, 0:65:32]),
                     reads=["mm%d" % k], writes=[dk])

        def bias_row(ci, s, si):
            k = mm1()
            for kc in range(KC):
                P.op("pe", lambda e, s=s, kc=kc, k=k, si=si: e.matmul(big[0:65, k, :], lhsT=shpad[:, si, kc, :], rhs=ring[:, s, kc, :],
                                                                        start=(kc == 0), stop=(kc == KC - 1)),
                     reads=["ring%d" % s, "shpad"], writes=["mm%d" % k])
            P.op("dve", lambda e, k=k, ci=ci: e.tensor_copy(out=brow_hi[:, ci * 512:(ci + 1) * 512], in_=big[0:65, k, :]),
                 reads=["mm%d" % k], writes=["brow_hi%d" % ci])

        for ch in CH_IN + CH_Z:
            bias_fm(ch, ring_load(ch))
        for ci, (ch, si) in enumerate(((CH_KV, 1), (CH_Q[0], 2), (CH_Q[1], 2))):
            bias_row(ci, ring_load(ch), si)

        if with_sample:
            for s_ in range(2):
                for t_ in range(2):
                    P.op("sp", lambda e, s_=s_, t_=t_: e.dma_start(out=carry[:, 1 + s_, :, t_], in_=st_conv[s_, t_].rearrange("(c p) -> p c", p=128)),
                         reads=[], writes=["carry%d" % (1 + s_)], dma="vcar%d" % s_)
                P.op("sp", lambda e, s_=s_: e.dma_start(out=k_s[s_, 0:112, :], in_=cache_k[s_, 16:128, :]), writes=["o_ksc%d" % s_], dma="outm")
                P.op("sp", lambda e, s_=s_: e.dma_start(out=v_s[s_, 0:112, :], in_=cache_v[s_, 16:128, :]), writes=["o_vsc%d" % s_], dma="outm")
                outkeys.extend(["o_ksc%d" % s_, "o_vsc%d" % s_])

        xin_ctr = [0]
        xpre = {}
        yout_ctr = [0]
        ktr = [0]

        class NormAcc:
            def __init__(self, T):
                self.T, self.k, self.n, self.pend = T, mm1(), 0, None
                mm_reserved.add(self.k)

            def add(self, kc, xkeys):
                T, k = self.T, self.k
                sl = ktr[0] % 2
                ktr[0] += 1
                P.op("act", lambda e: e.activation(out=sq[:, sl, 0:T], in_=xT[:, kc, 0:T], func=AF.Square), reads=xkeys, writes=["sq%d" % sl])
                self.flush()
                self.pend = sl

            def flush(self):
                if self.pend is None:
                    return
                T, k, sl, n = self.T, self.k, self.pend, self.n
                P.op("pe", lambda e: e.matmul(big[:, k, 0:T], lhsT=onesm[:], rhs=sq[:, sl, 0:T], start=(n == 0), stop=(n == KC - 1)),
                     reads=["sq%d" % sl, "onesm"], writes=["mm%d" % k])
                self.n += 1
                self.pend = None

            def finish(self):
                T, k = self.T, self.k
                self.flush()
                assert self.n == KC
                P.op("act", lambda e: e.activation(out=rstd[:, 0:T], in_=big[:, k, 0:T], func=AF.Ln, bias=EPS, scale=1.0),
                     reads=["mm%d" % k], writes=["rstd"])
                P.op("act", lambda e: e.activation(out=rstd[:, 0:T], in_=rstd[:, 0:T], func=AF.Exp, scale=-0.5),
                     reads=["rstd"], writes=["rstd"])
                mm_reserved.discard(k)

        def norm_stats(T, xkeys_of_kc):
            na = NormAcc(T)
            for kc in range(KC):
                na.add(kc, xkeys_of_kc(kc))
            na.finish()

        def run_tile(T, segs, ioblocks, blocks, tile_idx, next_iob, last_tile_out):
            nblk_keys = [b_["c0"] for b_ in blocks]

            def xk(kc):
                return ["xT%d" % kc]

            def xload(ioe, key):
                (c0_, rows_, src_, dst_) = ioe
                sl_ = xin_ctr[0] % 2
                xin_ctr[0] += 1
                P.op("sp", lambda e, sl_=sl_, rows_=rows_, src_=src_: e.dma_start(out=xtok[0:rows_, sl_, :], in_=src_),
                     writes=["xtok%d" % sl_], dma="xin%d" % sl_)
                xpre[key] = sl_

            for bidx, (c0, rows, src, dst) in enumerate(ioblocks):
                if (tile_idx, bidx) not in xpre:
                    xload((c0, rows, src, dst), (tile_idx, bidx))
                sl = xpre.pop((tile_idx, bidx))
                for half in range(2):
                    k = mm1()
                    for q4 in range(4):
                        kc = half * 4 + q4
                        P.op("pe", lambda e, sl=sl, rows=rows, kc=kc, q4=q4, k=k: e.transpose(
                            out=big[:, k, q4 * 128:q4 * 128 + rows], in_=xtok[0:rows, sl, kc * 128:(kc + 1) * 128],
                            identity=identf[0:rows, 0:rows]), reads=["xtok%d" % sl, "identf"], writes=["mm%d" % k])
                    P.op("act", lambda e, half=half, k=k, c0=c0, rows=rows: e.activation(
                        out=xT[:, half * 4:half * 4 + 4, c0:c0 + rows],
                        in_=big[:, k, :].rearrange("p (q t) -> p q t", t=128)[:, :, 0:rows], func=AF.Copy),
                        reads=["mm%d" % k], writes=["xT%d" % kc_ for kc_ in range(half * 4, half * 4 + 4)])

            if next_iob is not None:
                for bidx in range(min(2, len(next_iob))):
                    xload(next_iob[bidx], (tile_idx + 1, bidx))
            if stage < 1:
                return
            norm_stats(T, xk)
            for kc in range(KC):
                for (c0, n, bt, ci) in segs:
                    P.op("dve", lambda e, kc=kc, c0=c0, n=n, bt=bt: e.scalar_tensor_tensor(
                        out=h1[:, kc, c0:c0 + n], in0=xT[:, kc, c0:c0 + n], scalar=gsA[:, kc, bt:bt + 1], in1=rstd[:, c0:c0 + n],
                        op0=ALU.mult, op1=ALU.mult), reads=xk(kc) + ["rstd", "gsA"], writes=["h1_%d" % kc])
            h1keys = ["h1_%d" % kc for kc in range(KC)]
            h2keys = ["h2_%d" % kc for kc in range(KC)]

            if stage < 2:
                return
            pending = None
            for i in range(8):
                s = ring_state["loaded"].pop(i) if i in ring_state["loaded"] else ring_load(i)
                sl = i % 2
                accs = {}
                for sec in (1, 2, 3, 0):
                    k = mm1()
                    accs[sec] = k
                    for kc in range(KC):
                        P.op("pe", lambda e, s=s, kc=kc, k=k, sec=sec: e.matmul(
                            big[:, k, 0:T], lhsT=ring[:, s, kc, sec * 128:(sec + 1) * 128], rhs=h1[:, kc, 0:T],
                            start=(kc == 0), stop=(kc == KC - 1)), reads=["ring%d" % s, "h1_%d" % kc], writes=["mm%d" % k])
                    for si_, (c0, n, bt, ci) in enumerate(segs):
                        if sec == 1:
                            P.op("act", lambda e, k=k, c0=c0, n=n, bt=bt, sl=sl, i=i: e.activation(
                                out=c_sb[:, sl, c0:c0 + n], in_=big[:, k, c0:c0 + n], func=AF.Identity,
                                bias=bias_in[:, 8 + i, bt:bt + 1], scale=1.0), reads=["mm%d" % k, "bias_in"], writes=["c_sb%d" % sl])
                        elif sec == 2:
                            P.op("pool", lambda e, sl=sl, si_=si_, ci=ci, i=i: e.tensor_copy(out=vtmp[:, sl, si_ * 40:si_ * 40 + 2], in_=carry[:, ci, i, :]),
                                 reads=["carry%d" % ci], writes=["vtmp%d" % sl])
                            P.op("dve", lambda e, k=k, c0=c0, n=n, bt=bt, sl=sl, si_=si_, i=i: e.scalar_tensor_tensor(
                                out=vtmp[:, sl, si_ * 40 + 2:si_ * 40 + 2 + n], in0=big[:, k, c0:c0 + n], scalar=bias_in[:, 16 + i, bt:bt + 1],
                                in1=c_sb[:, sl, c0:c0 + n], op0=ALU.add, op1=ALU.mult),
                                reads=["mm%d" % k, "bias_in", "c_sb%d" % sl], writes=["vtmp%d" % sl])
                            P.op("pool", lambda e, sl=sl, si_=si_, ci=ci, i=i, n=n: e.tensor_copy(out=carry[:, ci, i, :], in_=vtmp[:, sl, si_ * 40 + n:si_ * 40 + n + 2]),
                                 reads=["vtmp%d" % sl], writes=["carry%d" % ci])
                            P.op("act", lambda e, sl=sl, si_=si_, c0=c0, n=n, i=i: e.activation(
                                out=ycv[:, sl, c0:c0 + n], in_=vtmp[:, sl, si_ * 40:si_ * 40 + n], func=AF.Copy, scale=cwT[:, i:i + 1]),
                                reads=["vtmp%d" % sl, "vecT"], writes=["ycv%d" % sl])
                            for tap in (1, 2):
                                P.op("dve", lambda e, sl=sl, si_=si_, c0=c0, n=n, i=i, tap=tap: e.scalar_tensor_tensor(
                                    out=ycv[:, sl, c0:c0 + n], in0=vtmp[:, sl, si_ * 40 + tap:si_ * 40 + tap + n], scalar=cwT[:, tap * 8 + i:tap * 8 + i + 1],
                                    in1=ycv[:, sl, c0:c0 + n], op0=ALU.mult, op1=ALU.add),
                                    reads=["vtmp%d" % sl, "vecT", "ycv%d" % sl], writes=["ycv%d" % sl])
                        elif sec == 3:
                            P.op("act", lambda e, k=k, c0=c0, n=n, bt=bt, sl=sl, i=i: e.activation(
                                out=szA[:, sl, c0:c0 + n], in_=big[:, k, c0:c0 + n], func=AF.Silu,
                                bias=bias_in[:, 24 + i, bt:bt + 1], scale=1.0), reads=["mm%d" % k, "bias_in"], writes=["szA%d" % sl])
                            P.op("pool", lambda e, sl=sl, c0=c0, n=n: e.tensor_tensor(
                                out=ycv[:, sl, c0:c0 + n], in0=ycv[:, sl, c0:c0 + n], in1=szA[:, sl, c0:c0 + n], op=ALU.mult),
                                reads=["ycv%d" % sl, "szA%d" % sl], writes=["ycv%d" % sl])
                        else:
                            P.op("dve", lambda e, k=k, c0=c0, n=n, bt=bt, sl=sl, i=i: e.scalar_tensor_tensor(
                                out=mT[:, i, c0:c0 + n], in0=big[:, k, c0:c0 + n], scalar=bias_in[:, i, bt:bt + 1],
                                in1=ycv[:, sl, c0:c0 + n], op0=ALU.add, op1=ALU.mult),
                                reads=["mm%d" % k, "bias_in", "ycv%d" % sl], writes=["mT%d" % i])
                nxt = {0: 5, 1: 6, 2: 7, 3: 8, 4: 9, 5: 10, 6: 11, 7: 12}[i]
                ring_load(nxt)

            if stage < 3:
                return
            na1 = NormAcc(T)
            for grp in ((0, 1, 2), (3, 4, 5), (6, 7)):
                ks = {oc: mm1() for oc in grp}
                for last in (False, True):
                    for oc in grp:
                        s = ring_state["loaded"][CH_OUT[oc // 4]]
                        k = ks[oc]
                        for ic in (range(KC - 1) if not last else (KC - 1,)):
                            P.op("pe", lambda e, s=s, ic=ic, k=k, oc=oc: e.matmul(
                                big[:, k, 0:T], lhsT=ring[:, s, ic, (oc % 4) * 128:(oc % 4 + 1) * 128], rhs=mT[:, ic, 0:T],
                                start=(ic == 0), stop=(ic == KC - 1)), reads=["ring%d" % s, "mT%d" % ic], writes=["mm%d" % k])
                for oc in grp:
                    k = ks[oc]
                    for (c0, n, bt, ci) in segs:
                        P.op("dve", lambda e, k=k, oc=oc, c0=c0, n=n, bt=bt: e.scalar_tensor_tensor(
                            out=xT[:, oc, c0:c0 + n], in0=big[:, k, c0:c0 + n], scalar=gateA[:, oc, bt:bt + 1], in1=xT[:, oc, c0:c0 + n],
                            op0=ALU.mult, op1=ALU.add), reads=["mm%d" % k, "modT"] + xk(oc), writes=xk(oc))
                    na1.add(oc, xk(oc))
                if 3 in grp:
                    ring_state["loaded"].pop(CH_OUT[0]); ring_load(13)
                if 7 in grp:
                    ring_state["loaded"].pop(CH_OUT[1]); ring_load(14)
            na1.finish()
            for hbuf, gsv, hname, gname in ((h2, gsB, "h2_%d", "gsB"), (h1, gsKV, "h1_%d", "gsKV")):
                for kc in range(KC):
                    for (c0, n, bt, ci) in segs:
                        P.op("dve", lambda e, kc=kc, c0=c0, n=n, bt=bt, hbuf=hbuf, gsv=gsv: e.scalar_tensor_tensor(
                            out=hbuf[:, kc, c0:c0 + n], in0=xT[:, kc, c0:c0 + n], scalar=gsv[:, kc, bt:bt + 1], in1=rstd[:, c0:c0 + n],
                            op0=ALU.mult, op1=ALU.mult), reads=xk(kc) + ["rstd", gname], writes=[hname % kc])

            if stage < 5:
                return
            s_kv = ring_state["loaded"][CH_KV]
            s_q = [ring_state["loaded"][CH_Q[0]], ring_state["loaded"][CH_Q[1]]]

            if stage < 6:
                return
            def pjkv(bi):
                B = blocks[bi]
                c0, R, bt, osl, sl = B["c0"], B["R"], B["n"], B["oslot"], bi % 2
                k = mm1()
                P.op("pe", lambda e, k=k, R=R, bt=bt: e.matmul(
                    big[0:R, k, :], lhsT=ones1[32 * bt:32 * bt + 1, 0:R], rhs=brow_hi[32 * bt:32 * bt + 1, 0:512],
                    start=True, stop=False), reads=["ones1", "brow_hi0"], writes=["mm%d" % k])
                for kc in range(KC):
                    P.op("pe", lambda e, kc=kc, k=k, c0=c0, R=R: e.matmul(
                        big[0:R, k, :], lhsT=h1[:, kc, c0:c0 + R], rhs=ring[:, s_kv, kc, :], start=False, stop=(kc == KC - 1)),
                        reads=["h1_%d" % kc, "ring%d" % s_kv], writes=["mm%d" % k])
                P.op("act", lambda e, k=k, R=R, sl=sl: e.activation(out=kf[0:R, sl, :], in_=big[0:R, k, 0:256], func=AF.Copy),
                     reads=["mm%d" % k], writes=["kf%d" % sl])
                P.op("dve", lambda e, k=k, R=R, osl=osl: e.tensor_copy(
                    out=Vaug[0:R, osl, :, 0:64], in_=big[0:R, k, 256:512].rearrange("p (h d) -> p h d", d=64)),
                    reads=["mm%d" % k], writes=["Vaug%d" % osl])
                if B["vout"] is not None:
                    P.op("act", lambda e, k=k, R=R: e.activation(out=vf[0:R, :], in_=big[0:R, k, 256:512], func=AF.Copy),
                         reads=["mm%d" % k], writes=["vf"])
                    ok = "o_v_%d_%d" % (bt, bi)
                    P.op("sp", lambda e, R=R, dst=B["vout"]: e.dma_start(out=dst, in_=vf[0:R, :]), reads=["vf"], writes=[ok], dma="outv%d" % bi)
                    outkeys.append(ok)

            def pjq(bi, hf):
                B = blocks[bi]
                c0, R, bt, sl = B["c0"], B["R"], B["n"], bi % 2
                k = mm1()
                P.op("pe", lambda e, k=k, R=R, bt=bt: e.matmul(
                    big[0:R, k, :], lhsT=ones1[32 * bt:32 * bt + 1, 0:R], rhs=brow_hi[32 * bt:32 * bt + 1, 512 * (1 + hf):512 * (2 + hf)],
                    start=True, stop=False), reads=["ones1", "brow_hi%d" % (1 + hf)], writes=["mm%d" % k])
                for kc in range(KC):
                    P.op("pe", lambda e, kc=kc, k=k, c0=c0, R=R: e.matmul(
                        big[0:R, k, :], lhsT=h2[:, kc, c0:c0 + R], rhs=ring[:, s_q[hf], kc, :], start=False, stop=(kc == KC - 1)),
                        reads=["h2_%d" % kc, "ring%d" % s_q[hf]], writes=["mm%d" % k])
                if hf == 0:
                    P.op("act", lambda e, k=k, R=R, sl=sl: e.activation(out=qf[0:R, sl, 0:512], in_=big[0:R, k, :], func=AF.Copy),
                         reads=["mm%d" % k], writes=["qf%d_0" % sl])
                else:
                    P.op("dve", lambda e, k=k, R=R, sl=sl: e.tensor_copy(out=qf[0:R, sl, 512:1024], in_=big[0:R, k, :]),
                         reads=["mm%d" % k], writes=["qf%d_1" % sl])

            def pj0(bi):
                pjkv(bi); pjq(bi, 0); pjq(bi, 1)

            def pj1(bi):
                B = blocks[bi]
                R, bt, pb, sl = B["R"], B["n"], B["posblk"], bi % 2
                cosb = cosT[0:R, pb, :]
                for (src, nh, keys) in ((kf[0:R, sl, :].rearrange("p (h d) -> p h d", d=64), 4, ["kf%d" % sl]),
                                        (qf[0:R, sl, :].rearrange("p (h d) -> p h d", d=64), 16, ["qf%d_0" % sl, "qf%d_1" % sl])):
                    P.op("pool", lambda e, R=R, src=src, nh=nh, cosb=cosb: e.tensor_tensor(
                        out=ra[0:R, 0:nh, :].rearrange("p h (t i) -> p h t i", i=8), in0=src[:, :, 0:16].rearrange("p h (t i) -> p h t i", i=8),
                        in1=cosb.unsqueeze(1).unsqueeze(1).to_broadcast([R, nh, 2, 8]), op=ALU.mult),
                        reads=keys + ["cosT"], writes=["ra"])
                    P.op("pool", lambda e, R=R, src=src, nh=nh, pb=pb: e.tensor_tensor(
                        out=rb[0:R, 0:nh, 0:8], in0=src[:, :, 8:16], in1=nsinT[0:R, pb, :].unsqueeze(1).to_broadcast([R, nh, 8]), op=ALU.mult),
                        reads=keys + ["nsinT"], writes=["rb"])
                    P.op("pool", lambda e, R=R, src=src, nh=nh, pb=pb: e.tensor_tensor(
                        out=rb[0:R, 0:nh, 8:16], in0=src[:, :, 0:8], in1=sinT[0:R, pb, :].unsqueeze(1).to_broadcast([R, nh, 8]), op=ALU.mult),
                        reads=keys + ["sinT"], writes=["rb"])
                    P.op("pool", lambda e, R=R, src=src, nh=nh: e.tensor_tensor(out=src[:, :, 0:16], in0=ra[0:R, 0:nh, :], in1=rb[0:R, 0:nh, :], op=ALU.add),
                         reads=["ra", "rb"], writes=keys)
                if B["kout"] is not None:
                    ok = "o_k_%d_%d" % (bt, bi)
                    P.op("sp", lambda e, R=R, sl=sl, dst=B["kout"]: e.dma_start(out=dst, in_=kf[0:R, sl, :]), reads=["kf%d" % sl], writes=[ok], dma="outk%d" % bi)
                    outkeys.append(ok)

            def pj2(bi):
                B = blocks[bi]
                R, kown, sl = B["R"], B["kbase"] + 128, bi % 2
                P.op("act", lambda e, R=R, sl=sl: e.activation(
                    out=kb2[0:R], in_=kf[0:R, sl, :].rearrange("p (h d) -> p h d", d=64).unsqueeze(2).to_broadcast([R, 4, 2, 64]), func=AF.Copy),
                    reads=["kf%d" % sl], writes=["kb2"])
                P.op("act", lambda e, R=R, sl=sl: e.activation(out=qb[0:R, sl, 0:512], in_=qf[0:R, sl, 0:512], func=AF.Copy, scale=0.125),
                     reads=["qf%d_0" % sl], writes=["qb%d_0" % sl])
                P.op("dve", lambda e, R=R, sl=sl: e.tensor_single_scalar(out=qb[0:R, sl, 512:1024], in_=qf[0:R, sl, 512:1024], scalar=0.125, op=ALU.mult),
                     reads=["qf%d_1" % sl], writes=["qb%d_1" % sl])
                for h in range(4):
                    P.op("pe", lambda e, R=R, h=h: e.transpose(out=ptp[:, 2, h * 128:h * 128 + R],
                                                                in_=kb2[0:R, h].rearrange("p t d -> p (t d)"), identity=identb[0:R, 0:R]),
                         reads=["kb2", "identb"], writes=["trb"])
                P.op("dve", lambda e, R=R, kown=kown: e.tensor_copy(
                    out=KT2[:, :, kown:kown + R], in_=ptp[:, 2, 0:512].rearrange("p (h t) -> p h t", t=128)[:, :, 0:R]),
                    reads=["trb"], writes=[B["kok"]])
                for c in range(8):
                    P.op("pe", lambda e, R=R, c=c, sl=sl: e.transpose(out=ptp[:, 2, c * 128:c * 128 + R], in_=qb[0:R, sl, c * 128:(c + 1) * 128],
                                                                       identity=identb[0:R, 0:R]),
                         reads=["qb%d_%d" % (sl, c // 4), "identb"], writes=["trb"])
                P.op("act", lambda e, R=R, bi=bi: e.activation(
                    out=qT[:, bi, :, 0:R], in_=ptp[:, 2, :].rearrange("p (c t) -> p c t", t=128)[:, :, 0:R], func=AF.Copy),
                    reads=["trb"], writes=["qT%d" % bi])

            units = [(bi, Q) for bi in range(len(blocks)) for Q in range(4)]
            ustate = {}

            def stA(u):
                bi, Q = units[u]
                B = blocks[bi]
                R, kbase = B["R"], B["kbase"]
                N = 128 + R
                ps_, st3 = u % 4, u % 5
                k = mm2()
                S = big[:, k:k + 2, :].rearrange("p a (g n) -> p (a g) n", n=256)
                kkeys = [B["khk"], B["kok"]]
                for g in (0, 2, 1, 3):
                    hq = 4 * Q + 2 * (g % 2) + g // 2
                    c, j = hq // 2, hq % 2
                    P.op("pe", lambda e, R=R, N=N, g=g, c=c, j=j, Q=Q, bi=bi, S=S, kbase=kbase: e.matmul(
                        S[0:R, g, 0:N], lhsT=qT[64 * j:64 * j + 64, bi, c, 0:R], rhs=KT2[64 * j:64 * j + 64, Q, kbase:kbase + N],
                        start=True, stop=True), reads=["qT%d" % bi] + kkeys, writes=["mm%d" % (k + g // 2)])
                skeys = ["mm%d" % k, "mm%d" % (k + 1)]
                sk = "stt%d" % st3
                P.op("dve", lambda e, R=R, N=N, S=S, st3=st3: e.tensor_reduce(
                    out=stt[0:R, st3, 0, 0:1], in_=S[0:R, :, 0:N], axis=AX.XY, op=ALU.max), reads=skeys, writes=[sk + "mx"])
                P.op("dve", lambda e, R=R, st3=st3, Q=Q: e.scalar_tensor_tensor(
                    out=stt[0:R, st3, 1, 0:1], in0=stt[0:R, st3, 0, 0:1], scalar=-1.0, in1=nsmax[0:R, Q:Q + 1], op0=ALU.mult, op1=ALU.min),
                    reads=[sk + "mx", "nsmax"], writes=[sk + "nb"])
                P.op("act", lambda e, R=R, N=N, S=S, ps_=ps_, st3=st3: e.activation(
                    out=Pm[0:R, ps_, :, 0:N], in_=S[0:R, :, 0:N], func=AF.Exp, bias=stt[0:R, st3, 1, 0:1], scale=1.0),
                    reads=skeys + [sk + "nb"], writes=["Pm%d" % ps_])
                if B["mask"]:
                    P.op("pool", lambda e, ps_=ps_: e.memset(Pm[0:64, ps_, :, 192:256], 0.0), writes=["Pm%d" % ps_])
                    if B["first"]:
                        P.op("pool", lambda e, ps_=ps_: e.memset(Pm[0:64, ps_, :, 0:128], 0.0), writes=["Pm%d" % ps_])
                        P.op("pool", lambda e, ps_=ps_: e.memset(Pm[64:128, ps_, :, 0:128], 0.0), writes=["Pm%d" % ps_])
                    else:
                        P.op("pool", lambda e, ps_=ps_: e.memset(Pm[64:128, ps_, :, 0:64], 0.0), writes=["Pm%d" % ps_])
                P.op("act", lambda e, R=R, st3=st3, Q=Q: e.activation(
                    out=stt[0:R, st3, 3, :], in_=sink_p[0:R, 4 * Q:4 * Q + 4], func=AF.Exp, bias=stt[0:R, st3, 1, 0:1], scale=1.0),
                    reads=[sk + "nb", "sink_p"], writes=[sk + "es"])

            def stB(u):
                bi, Q = units[u]
                B = blocks[bi]
                R = B["R"]
                ps_, pm_ = u % 2, u % 4
                for g in range(4):
                    for kb in range(2):
                        nk = 128 if kb == 0 else R
                        P.op("pe", lambda e, R=R, g=g, kb=kb, nk=nk, ps_=ps_, pm_=pm_: e.transpose(
                            out=ptp[0:nk, ps_, (kb * 4 + g) * 128:(kb * 4 + g) * 128 + R], in_=Pm[0:R, pm_, g, kb * 128:kb * 128 + nk],
                            identity=identb[0:R, 0:R]), reads=["Pm%d" % pm_, "identb"], writes=["pt%d" % ps_])
                if R == 128:
                    P.op("act", lambda e, ps_=ps_: e.activation(
                        out=PTs[:, ps_].rearrange("p k g t -> p (k g) t"), in_=ptp[:, ps_, :].rearrange("p (kg t) -> p kg t", t=128), func=AF.Copy),
                        reads=["pt%d" % ps_], writes=["PTs%d_0" % ps_, "PTs%d_1" % ps_])
                else:
                    for kb in range(2):
                        nk = 128 if kb == 0 else R
                        src = ptp[0:nk, ps_, kb * 512:(kb + 1) * 512].rearrange("p (g t) -> p g t", t=128)[:, :, 0:R]
                        P.op("dve", lambda e, nk=nk, R=R, kb=kb, ps_=ps_, src=src: e.tensor_copy(out=PTs[0:nk, ps_, kb, :, 0:R], in_=src),
                             reads=["pt%d" % ps_], writes=["PTs%d_%d" % (ps_, kb)])

            def stC(u):
                bi, Q = units[u]
                B = blocks[bi]
                R, hsl, osl = B["R"], B["hslot"], B["oslot"]
                ps_, st3, sl = u % 2, u % 5, bi % 2
                sk = "stt%d" % st3
                O3 = ops_[:, 0:260].rearrange("p (g d) -> p g d", d=65)
                for g in range(4):
                    for ki, kb in enumerate((1, 0)):
                        nk = 128 if kb == 0 else R
                        vs = hsl if kb == 0 else osl
                        P.op("pe", lambda e, R=R, g=g, kb=kb, ki=ki, nk=nk, vs=vs, Q=Q, ps_=ps_, O3=O3: e.matmul(
                            O3[0:R, g, :], lhsT=PTs[0:nk, ps_, kb, g, 0:R], rhs=Vaug[0:nk, vs, Q, :], start=(ki == 0), stop=(ki == 1)),
                            reads=["PTs%d_%d" % (ps_, kb), "Vaug%d" % vs], writes=["ops"])
                P.op("dve", lambda e, R=R, st3=st3, O3=O3: e.tensor_tensor(
                    out=stt[0:R, st3, 4, :], in0=O3[0:R, :, 64], in1=stt[0:R, st3, 3, :], op=ALU.add),
                    reads=["ops", sk + "es"], writes=[sk + "den"])
                P.op("dve", lambda e, R=R, st3=st3: e.reciprocal(out=stt[0:R, st3, 5, :], in_=stt[0:R, st3, 4, :]),
                     reads=[sk + "den"], writes=[sk + "ri"])
                P.op("dve", lambda e, R=R, st3=st3, Q=Q, sl=sl, O3=O3: e.tensor_tensor(
                    out=otok[0:R, sl, 256 * Q:256 * (Q + 1)].rearrange("p (t a d) -> p a t d", t=2, a=2, d=64),
                    in0=O3[0:R, :, 0:64].rearrange("p (a t) d -> p a t d", a=2),
                    in1=stt[0:R, st3, 5, :].rearrange("p (a t) -> p a t", a=2).unsqueeze(3).to_broadcast([R, 2, 2, 64]), op=ALU.mult),
                    reads=["ops", sk + "ri"], writes=["otok%d" % sl])
                if Q == 3:
                    c0 = B["c0"]
                    for c in range(8):
                        P.op("pe", lambda e, R=R, c=c, sl=sl: e.transpose(out=ptp[:, 2, c * 128:c * 128 + R], in_=otok[0:R, sl, c * 128:(c + 1) * 128],
                                                                           identity=identb[0:R, 0:R]),
                             reads=["otok%d" % sl, "identb"], writes=["trb"])
                    P.op("dve", lambda e, R=R, c0=c0: e.tensor_tensor(
                        out=ogT[:, :, c0:c0 + R], in0=ptp[:, 2, :].rearrange("p (c t) -> p c t", t=128)[:, :, 0:R], in1=szT[:, :, c0:c0 + R], op=ALU.mult),
                        reads=["trb"] + ["szT%d" % z for z in range(8)], writes=["ogT"])

            def zgate(zlist):
                for zc in zlist:
                    ch = CH_Z[zc // 4]
                    s = ring_state["loaded"][ch]
                    k = mm1()
                    for kc in range(KC):
                        P.op("pe", lambda e, s=s, kc=kc, k=k, zc=zc: e.matmul(
                            big[:, k, 0:T], lhsT=ring[:, s, kc, (zc % 4) * 128:(zc % 4 + 1) * 128], rhs=h2[:, kc, 0:T],
                            start=(kc == 0), stop=(kc == KC - 1)), reads=["ring%d" % s, "h2_%d" % kc], writes=["mm%d" % k])
                    for (c0, n, bt, ci) in segs:
                        P.op("act", lambda e, k=k, zc=zc, c0=c0, n=n, bt=bt: e.activation(
                            out=szT[:, zc, c0:c0 + n], in_=big[:, k, c0:c0 + n], func=AF.Silu, bias=bias_z[:, zc, bt:bt + 1], scale=1.0),
                            reads=["mm%d" % k, "bias_z"], writes=["szT%d" % zc])
                    if zc == 3:
                        ring_state["loaded"].pop(CH_Z[0]); ring_load(15)
                    if zc == 7:
                        ring_state["loaded"].pop(CH_Z[1]); ring_load(16)
            zgate(range(0, 4))
            pj0(0)
            pj1(0)
            zgate(range(4, 8))
            pj2(0)
            nu = len(units)
            DB, DC = 3, 4

            def proj_pieces(step):
                for nb in range(1, len(blocks)):
                    d = step - (4 * nb - 5)
                    if d == 0:
                        pjkv(nb)
                    elif d == 1:
                        pjq(nb, 0)
                    elif d == 2:
                        pjq(nb, 1)
                        pj1(nb)
                    elif d == 3:
                        pj2(nb)

            proj_pieces(-1)
            attn_mode[0] = True
            for s_ in range(nu + DC):
                if 0 <= s_ - DB < nu:
                    stB(s_ - DB)
                if s_ < nu:
                    stA(s_)
                if 0 <= s_ - DC < nu:
                    stC(s_ - DC)
                proj_pieces(s_)
            attn_mode[0] = False

            if blocks[0]["mask"]:
                P.op("pool", lambda e: e.tensor_copy(out=KT2[:, :, 0:128], in_=KT2[:, :, 512:640]), reads=["KT2_3"], writes=["KT2"])
                P.op("pool", lambda e: e.tensor_copy(out=Vaug[:, 0, :, 0:64], in_=Vaug[:, 4, :, 0:64]), reads=["Vaug4"], writes=["Vaug0"])
            ring_state["loaded"].pop(CH_KV); ring_state["loaded"].pop(CH_Q[0]); ring_state["loaded"].pop(CH_Q[1])
            if not last_tile_out:
                ring_load(0); ring_load(1); ring_load(2)

            if stage < 8:
                return
            na2 = NormAcc(T)
            for oc in range(8):
                ch = CH_O[oc // 4]
                s = ring_state["loaded"][ch]
                k = mm1()
                for ic in range(KC):
                    P.op("pe", lambda e, s=s, ic=ic, k=k, oc=oc: e.matmul(
                        big[:, k, 0:T], lhsT=ring[:, s, ic, (oc % 4) * 128:(oc % 4 + 1) * 128], rhs=ogT[:, ic, 0:T],
                        start=(ic == 0), stop=(ic == KC - 1)), reads=["ring%d" % s, "ogT"], writes=["mm%d" % k])
                for (c0, n, bt, ci) in segs:
                    P.op("dve", lambda e, k=k, oc=oc, c0=c0, n=n, bt=bt: e.scalar_tensor_tensor(
                        out=xT[:, oc, c0:c0 + n], in0=big[:, k, c0:c0 + n], scalar=gateB[:, oc, bt:bt + 1], in1=xT[:, oc, c0:c0 + n],
                        op0=ALU.mult, op1=ALU.add), reads=["mm%d" % k, "modT"] + xk(oc), writes=xk(oc))
                na2.add(oc, xk(oc))
                if oc == 3:
                    ring_state["loaded"].pop(CH_O[0])
                    if not last_tile_out:
                        ring_load(3)
                if oc == 7:
                    ring_state["loaded"].pop(CH_O[1])
                    if not last_tile_out:
                        ring_load(4)

            if stage < 9:
                return
            na2.finish()
            for kc in range(KC):
                P.op("dve", lambda e, kc=kc: e.scalar_tensor_tensor(
                    out=xT[:, kc, 0:T], in0=xT[:, kc, 0:T], scalar=gfT[:, kc:kc + 1], in1=rstd[:, 0:T], op0=ALU.mult, op1=ALU.mult),
                    reads=xk(kc) + ["rstd", "vecT"], writes=xk(kc))
            for (c0, rows, src, dst) in ioblocks:
                sl = yout_ctr[0] % 2
                yout_ctr[0] += 1
                for half in range(2):
                    k = mm1()
                    for q4 in range(4):
                        kc = half * 4 + q4
                        P.op("pe", lambda e, rows=rows, kc=kc, q4=q4, k=k, c0=c0: e.transpose(
                            out=big[0:rows, k, q4 * 128:(q4 + 1) * 128], in_=xT[:, kc, c0:c0 + rows], identity=identf[:]),
                            reads=xk(kc) + ["identf"], writes=["mm%d" % k])
                    P.op("act", lambda e, half=half, k=k, rows=rows, sl=sl: e.activation(
                        out=ytok[0:rows, sl, half * 512:(half + 1) * 512], in_=big[0:rows, k, :], func=AF.Copy),
                        reads=["mm%d" % k], writes=["ytok%d_%d" % (sl, half)])
                ok = "o_y_%d" % yout_ctr[0]
                P.op("sp", lambda e, rows=rows, sl=sl, dst=dst: e.dma_start(out=dst, in_=ytok[0:rows, sl, :]),
                     reads=["ytok%d_0" % sl, "ytok%d_1" % sl], writes=[ok], dma="yout%d" % sl)
                outkeys.append(ok)

        ring_state["loaded"].clear()
        for ch in (0, 1, 2, 3, 4):
            ring_load(ch)
        tiles = []
        for t in range(NT):
            segs = [(0, 512, 0, 0)]
            iob = [(b * 128, 128, x_p[t * 512 + b * 128: t * 512 + (b + 1) * 128, :], y_p[t * 512 + b * 128: t * 512 + (b + 1) * 128, :]) for b in range(4)]
            blocks = []
            for b in range(4):
                last = (t == NT - 1 and b == 3)
                blocks.append(dict(c0=b * 128, R=128, n=0, kbase=b * 128, hslot=b, oslot=b + 1, posblk=t * 4 + b, mask=True,
                                   khk=("KT2" if b == 0 else "KT2_%d" % (b - 1)), kok="KT2_%d" % b,
                                   first=(t == 0 and b == 0), kout=(k_p[:, :] if last else None), vout=(v_p[:, :] if last else None)))
            tiles.append((512, segs, iob, blocks))
        if with_sample:
            segs = [(0, 16, 1, 1), (16, 16, 2, 2)]
            iob = [(0, 32, x_s[:, :], y_s[:, :])]
            blocks = []
            for s_ in range(2):
                blocks.append(dict(c0=16 * s_, R=16, n=1 + s_, kbase=256 * s_, hslot=2 * s_, oslot=2 * s_ + 1, posblk=64, mask=False,
                                   khk=("KT2" if s_ == 0 else "KT2_1"), kok=("KT2_0" if s_ == 0 else "KT2_2"),
                                   first=False, kout=k_s[s_, 112:128, :], vout=v_s[s_, 112:128, :]))
            tiles.append((32, segs, iob, blocks))

        if stage < 1:
            tiles = []
        for ti, (T, segs, iob, blocks) in enumerate(tiles):
            is_sample = with_sample and ti == len(tiles) - 1
            if is_sample:
                for s_ in range(2):
                    P.op("sp", lambda e, s_=s_: e.dma_start(out=kf[:, s_, :], in_=cache_k[s_]), writes=["kf%d" % s_], dma="cink%d" % s_)
                    P.op("sp", lambda e, s_=s_: e.dma_start(out=vf[:, :], in_=cache_v[s_]), writes=["vf"], dma="cinv%d" % s_)
                    P.op("dve", lambda e, s_=s_: e.tensor_copy(out=Vaug[:, 2 * s_, :, 0:64], in_=vf[:, :].rearrange("p (h d) -> p h d", d=64)),
                         reads=["vf"], writes=["Vaug%d" % (2 * s_)])
                    P.op("act", lambda e, s_=s_: e.activation(
                        out=kb2[:], in_=kf[:, s_, :].rearrange("p (h d) -> p h d", d=64).unsqueeze(2).to_broadcast([128, 4, 2, 64]), func=AF.Copy),
                        reads=["kf%d" % s_], writes=["kb2"])
                    for h in range(4):
                        P.op("pe", lambda e, h=h: e.transpose(out=ptp[:, 2, h * 128:(h + 1) * 128], in_=kb2[:, h].rearrange("p t d -> p (t d)"),
                                                               identity=identb[:]), reads=["kb2", "identb"], writes=["trb"])
                    P.op("dve", lambda e, s_=s_: e.tensor_copy(out=KT2[:, :, 256 * s_:256 * s_ + 128],
                                                               in_=ptp[:, 2, 0:512].rearrange("p (h t) -> p h t", t=128)),
                         reads=["trb"], writes=["KT2"] if s_ == 0 else ["KT2_1"])
            run_tile(T, segs, iob, blocks, ti, (tiles[ti + 1][2] if ti + 1 < len(tiles) else None), last_tile_out=(ti == len(tiles) - 1))
            if ti == NT - 1:
                for t_ in range(2):
                    P.op("sp", lambda e, t_=t_: e.dma_start(out=conv_p[t_].rearrange("(c p) -> p c", p=128), in_=carry[:, 0, :, t_]),
                         reads=["carry0"], writes=["o_convp%d" % t_], dma="outm")
                    outkeys.append("o_convp%d" % t_)
            if is_sample:
                for s_ in range(2):
                    for t_ in range(2):
                        P.op("sp", lambda e, s_=s_, t_=t_: e.dma_start(out=conv_s[s_, t_].rearrange("(c p) -> p c", p=128), in_=carry[:, 1 + s_, :, t_]),
                             reads=["carry%d" % (1 + s_)], writes=["o_convs%d_%d" % (s_, t_)], dma="outm")
                        outkeys.append("o_convs%d_%d" % (s_, t_))

        P.emit(final_wait_keys=outkeys)
        build_program.stats = P.stats
    return nc


_CACHE = {}


def _get_program(SEQ):
    if SEQ not in _CACHE:
        _CACHE[SEQ] = build_program(SEQ)
    return _CACHE[SEQ]


def kernel(x_prompt, x_sample, c_prompt, c_sample, state_conv, cache_k, cache_v,
           g_a, w_mod_a, b_mod_a, w_in_a, conv_w_a, w_out_a,
           g_kv, w_mod_kv, b_mod_kv, w_kv,
           g_b, w_mod_b, b_mod_b, w_qz_b, w_o_b, sinks_b, g_final):
    f = lambda a: np.ascontiguousarray(np.asarray(a, dtype=np.float32))
    x_prompt, x_sample = f(x_prompt), f(x_sample)
    B, SEQ, _ = x_prompt.shape
    nc = _get_program(SEQ)
    shared = {
        "g_a": f(g_a)[0], "w_mod_a": f(w_mod_a)[0], "b_mod_a": f(b_mod_a)[0], "w_in": f(w_in_a)[0],
        "conv_w": f(conv_w_a)[0], "w_out": f(w_out_a)[0], "g_kv": f(g_kv), "w_mod_kv": f(w_mod_kv),
        "b_mod_kv": f(b_mod_kv), "w_kv": f(w_kv), "g_b": f(g_b)[0], "w_mod_b": f(w_mod_b)[0],
        "b_mod_b": f(b_mod_b)[0], "w_qz": f(w_qz_b)[0], "w_o": f(w_o_b)[0], "sinks": f(sinks_b)[0], "g_f": f(g_final),
    }
    c_prompt, c_sample, state_conv = f(c_prompt), f(c_sample), f(state_conv)
    cache_k, cache_v = f(cache_k), f(cache_v)
    in_maps = []
    for i in range(NCORES):
        m = dict(shared)
        m["x_p"] = x_prompt[i]
        m["x_s"] = x_sample[2 * i:2 * i + 2].reshape(32, D)
        m["cmat"] = np.ascontiguousarray(np.stack([c_prompt[i], c_sample[2 * i], c_sample[2 * i + 1]], axis=0))
        m["st_conv"] = np.ascontiguousarray(state_conv[0, 2 * i:2 * i + 2])
        m["cache_k"] = np.ascontiguousarray(cache_k[2 * i:2 * i + 2].reshape(2, 128, 256))
        m["cache_v"] = np.ascontiguousarray(cache_v[2 * i:2 * i + 2].reshape(2, 128, 256))
        in_maps.append(m)
    res = run_bass_kernel_spmd(nc, in_maps, core_ids=list(range(NCORES)))
    r = res.results
    y_prompt = np.stack([r[i]["y_p"] for i in range(NCORES)], axis=0)
    y_sample = np.concatenate([r[i]["y_s"].reshape(2, 16, D) for i in range(NCORES)], axis=0)
    conv_p = np.stack([r[i]["conv_p"] for i in range(NCORES)], axis=0)[None]
    conv_s = np.concatenate([r[i]["conv_s"] for i in range(NCORES)], axis=0)[None]
    k_p = np.stack([r[i]["k_p"].reshape(128, 4, 64) for i in range(NCORES)], axis=0)
    v_p = np.stack([r[i]["v_p"].reshape(128, 4, 64) for i in range(NCORES)], axis=0)
    k_s = np.concatenate([r[i]["k_s"].reshape(2, 128, 4, 64) for i in range(NCORES)], axis=0)
    v_s = np.concatenate([r[i]["v_s"].reshape(2, 128, 4, 64) for i in range(NCORES)], axis=0)
    return (y_prompt, y_sample, conv_p, conv_s, k_p, v_p, k_s, v_s)
```

```python
import contextlib
import math
import numpy as np
import concourse.bass as bass
import concourse.mybir as mybir
from concourse.bass_utils import run_bass_kernel_spmd

F32 = mybir.dt.float32
BF16 = mybir.dt.bfloat16
I32 = mybir.dt.int32
ALU = mybir.AluOpType
AF = mybir.ActivationFunctionType
AX = mybir.AxisListType

D = 1024
KC = 8
NCORES = 8
EPS = 1e-6
ROPE_THETA = 500000.0
PAST_LEN = 4096
NSLOT = 5
ENGS = ("pe", "act", "dve", "pool", "sp")


class _Op:
    __slots__ = ("eng", "fn", "deps", "idx", "is_dma", "dma_sem", "sig", "token")


class Prog:
    def __init__(self, nc):
        self.nc = nc
        self.ops = []
        self.last_w = {}
        self.readers = {}

    def op(self, eng, fn, reads=(), writes=(), dma=None):
        o = _Op()
        o.eng, o.fn, o.idx = eng, fn, len(self.ops)
        o.is_dma, o.dma_sem, o.sig, o.token = dma is not None, dma, False, None
        deps = {}
        psum_reads = [k for k in reads if k.startswith(("mm", "pt", "trb", "ops"))]
        writes = list(writes) + psum_reads
        for k in reads:
            w = self.last_w.get(k)
            if w is not None:
                deps[w] = True
        for k in writes:
            w = self.last_w.get(k)
            if w is not None:
                deps.setdefault(w, False)
            for r in self.readers.get(k, ()):
                deps.setdefault(r, False)
        o.deps = deps
        for k in reads:
            self.readers.setdefault(k, []).append(o.idx)
        for k in writes:
            self.last_w[k] = o.idx
            self.readers[k] = []
        self.ops.append(o)
        return o.idx

    def emit(self, final_wait_keys=()):
        nc, ops = self.nc, self.ops
        fdeps = set()
        for k in final_wait_keys:
            w = self.last_w.get(k)
            if w is not None:
                fdeps.add(w)
        needed = set(fdeps)
        for o in ops:
            best = {}
            for d, raw in o.deps.items():
                p = ops[d]
                if (not o.is_dma) and (not p.is_dma) and p.eng == o.eng and o.eng == "pe":
                    continue
                key = ("d", p.dma_sem) if p.is_dma else ("e", p.eng)
                if key not in best or best[key] < d:
                    best[key] = d
            o.deps = set(best.values())
            needed.update(o.deps)
        for d in needed:
            ops[d].sig = True
        stack = contextlib.ExitStack()
        eng_sem = {e: stack.enter_context(nc.semaphore("s_" + e)) for e in ENGS}
        dma_sem = {}
        for o in ops:
            if o.is_dma and o.dma_sem not in dma_sem:
                dma_sem[o.dma_sem] = stack.enter_context(nc.semaphore("d_" + o.dma_sem))
        cnt = {e: 0 for e in ENGS}
        dcnt = {k: 0 for k in dma_sem}
        for o in ops:
            if o.is_dma:
                dcnt[o.dma_sem] += 16
                o.token = (dma_sem[o.dma_sem], dcnt[o.dma_sem], "d_" + o.dma_sem)
            elif o.sig:
                cnt[o.eng] += 1
                o.token = (eng_sem[o.eng], cnt[o.eng], "s_" + o.eng)
        per_eng = {e: [o for o in ops if o.eng == e] for e in ENGS}
        self.stats = {"ops": len(ops), "per_eng": {e: len(v) for e, v in per_eng.items()}, "sig": dict(cnt)}

        def run(e, h):
            waited = {}
            for o in per_eng[e]:
                for d in sorted(o.deps):
                    sem, val, sname = ops[d].token
                    if waited.get(sname, 0) >= val:
                        continue
                    h.wait_ge(sem, val)
                    waited[sname] = val
                ins = o.fn(h)
                if o.is_dma:
                    ins.then_inc(o.token[0], 16)
                elif o.sig:
                    ins.then_inc(o.token[0], 1)
            if e == "sp":
                fin = {}
                for d in fdeps:
                    sem, val, sname = ops[d].token
                    if sname not in fin or fin[sname][1] < val:
                        fin[sname] = (sem, val)
                for sname, (sem, val) in sorted(fin.items()):
                    if waited.get(sname, 0) >= val:
                        continue
                    h.wait_ge(sem, val)
                    waited[sname] = val

        with stack:
            with nc.Block() as block:
                @block.tensor
                def _(h):
                    run("pe", h)

                @block.scalar
                def _(h):
                    run("act", h)

                @block.vector
                def _(h):
                    run("dve", h)

                @block.gpsimd
                def _(h):
                    run("pool", h)

                @block.sync
                def _(h):
                    run("sp", h)


def build_program(SEQ, with_sample=True, debug=False, stage=99):
    assert SEQ % 512 == 0
    NT = SEQ // 512
    nc = bass.Bass("TRN2", target_bir_lowering=False)

    def din(name, shape):
        return nc.dram_tensor(name, list(shape), F32, kind="ExternalInput").ap()

    def dout(name, shape):
        return nc.dram_tensor(name, list(shape), F32, kind="ExternalOutput").ap()

    x_p = din("x_p", [SEQ, D])
    x_s = din("x_s", [32, D])
    cmat = din("cmat", [3, D])
    st_conv = din("st_conv", [2, 2, D])
    cache_k = din("cache_k", [2, 128, 256])
    cache_v = din("cache_v", [2, 128, 256])
    g_a = din("g_a", [D]); w_mod_a = din("w_mod_a", [D, 3 * D]); b_mod_a = din("b_mod_a", [3 * D])
    w_in = din("w_in", [D, 4 * D]); conv_w = din("conv_w", [3, D]); w_out = din("w_out", [D, D])
    g_kv = din("g_kv", [D]); w_mod_kv = din("w_mod_kv", [D, 2 * D]); b_mod_kv = din("b_mod_kv", [2 * D])
    w_kv = din("w_kv", [D, 512])
    g_b = din("g_b", [D]); w_mod_b = din("w_mod_b", [D, 3 * D]); b_mod_b = din("b_mod_b", [3 * D])
    w_qz = din("w_qz", [D, 2 * D]); w_o = din("w_o", [D, D]); sinks = din("sinks", [16]); g_f = din("g_f", [D])

    y_p = dout("y_p", [SEQ, D]); y_s = dout("y_s", [32, D])
    conv_p = dout("conv_p", [2, D]); conv_s = dout("conv_s", [2, 2, D])
    k_p = dout("k_p", [128, 256]); v_p = dout("v_p", [128, 256])
    k_s = dout("k_s", [2, 128, 256]); v_s = dout("v_s", [2, 128, 256])

    NCH = 17
    scr = nc.dram_tensor("wscr", [NCH, 128, KC, 512], BF16).ap()

    st = contextlib.ExitStack()

    def sb(name, shape, dt):
        return st.enter_context(nc.sbuf_tensor(name, list(shape), dt))

    def ps(name, shape, dt):
        return st.enter_context(nc.psum_tensor(name, list(shape), dt))

    with st:
        st.enter_context(nc.allow_non_contiguous_dma(reason="small strided vector loads/stores"))
        P = Prog(nc)
        outkeys = []

        ring = sb("ring", [128, NSLOT, KC, 512], BF16)
        identf = sb("identf", [128, 128], F32)
        identb = sb("identb", [128, 128], BF16)
        onesm = sb("onesm", [128, 128], BF16)
        ones1 = sb("ones1", [128, 128], BF16)
        iot = sb("iot", [128, 128], I32)
        cosT = sb("cosT", [128, 65, 8], F32)
        sinT = sb("sinT", [128, 65, 8], F32)
        nsinT = sb("nsinT", [128, 65, 8], F32)
        posi = sb("posi", [128, 65], I32)
        posf = sb("posf", [128, 65], F32)
        vecs = sb("vecs", [128, 128], F32)
        vecT = sb("vecT", [128, 128], F32)
        modT = sb("modT", [128, 64, 3], F32)
        cT = sb("cT", [128, KC, 3], F32)
        scT = sb("scT", [128, KC, 3], BF16)
        mrow = sb("mrow", [3, 512], F32)
        gsA = sb("gsA", [128, KC, 3], F32); gsKV = sb("gsKV", [128, KC, 3], F32); gsB = sb("gsB", [128, KC, 3], F32)
        shpad = sb("shpad", [128, 3, KC, 65], BF16)
        bias_in = sb("bias_in", [128, 32, 3], F32)
        bias_z = sb("bias_z", [128, 8, 3], F32)
        brow_hi = sb("brow_hi", [65, 1536], BF16)
        brow_lo = sb("brow_lo", [65, 1536], BF16)
        brow_f = sb("brow_f", [65, 512], F32)
        brow_t = sb("brow_t", [65, 512], F32)
        sink_bc = sb("sink_bc", [128, 16], F32)
        nsink = sb("nsink", [128, 16], F32)
        sink_p = sb("sink_p", [128, 16], F32)
        nsmax = sb("nsmax", [128, 4], F32)
        nsink_p = sb("nsink_p", [128, 16], F32)
        carry = sb("carry", [128, 3, KC, 2], F32)

        xtok = sb("xtok", [128, 2, D], F32)
        ytok = sb("ytok", [128, 2, D], F32)
        xT = sb("xT", [128, KC, 512], F32)
        def _alias(a):
            return xT[:, a:a + 2, :].rearrange("p a b -> p (a b)")[:, 0:520].rearrange("p (b i) -> p b i", i=8)
        rtmp, rtmp2, rki = _alias(0), _alias(2), _alias(4).bitcast(I32)
        sq = sb("sq", [128, 2, 512], BF16)
        rstd = sb("rstd", [128, 512], F32)
        h1 = sb("h1", [128, KC, 512], BF16)
        h2 = sb("h2", [128, KC, 512], BF16)
        mT = sb("mT", [128, KC, 512], BF16)
        c_sb = sb("c_sb", [128, 2, 512], F32)
        vtmp = sb("vtmp", [128, 2, 516], F32)
        ycv = sb("ycv", [128, 2, 512], F32)
        szA = sb("szA", [128, 2, 512], F32)
        szT = sb("szT", [128, KC, 512], BF16)
        ogT = sb("ogT", [128, KC, 512], BF16)
        kf = sb("kf", [128, 2, 256], F32)
        vf = sb("vf", [128, 256], F32)
        kb2 = sb("kb2", [128, 4, 2, 64], BF16)
        KT2 = sb("KT2", [128, 4, 640], BF16)
        Vaug = sb("Vaug", [128, 5, 4, 65], BF16)
        qf = sb("qf", [128, 2, D], F32)
        ra = sb("ra", [128, 16, 16], F32)
        rb = sb("rb", [128, 16, 16], F32)
        qb = sb("qb", [128, 2, D], BF16)
        qT = sb("qT", [128, 4, KC, 128], BF16)
        Pm = sb("Pm", [128, 3, 4, 256], BF16)
        PTs = sb("PTs", [128, 2, 2, 4, 128], BF16)
        stt = sb("stt", [128, 4, 8, 4], F32)
        otok = sb("otok", [128, 2, D], BF16)

        big = ps("big", [128, 4, 512], F32)
        ptp = ps("ptp", [128, 3, 1024], BF16)
        ops_ = ps("ops_", [128, 512], F32)

        mmctr = [0]

        mm_reserved = set()

        attn_mode = [False]

        def mm1():
            if attn_mode[0]:
                mmctr[0] += 1
                return 2 + mmctr[0] % 2
            while True:
                k = mmctr[0] % 4
                mmctr[0] += 1
                if k not in mm_reserved:
                    return k

        def mm2():
            if attn_mode[0]:
                return 0
            while True:
                if mmctr[0] % 2:
                    mmctr[0] += 1
                k = mmctr[0] % 4
                mmctr[0] += 2
                if k not in mm_reserved and (k + 1) not in mm_reserved:
                    return k

        P.op("pool", lambda e: e.iota(iot[:], [[1, 128]], base=0, channel_multiplier=-1), writes=["iot"])
        P.op("dve", lambda e: e.tensor_single_scalar(out=identf[:], in_=iot[:], scalar=0, op=ALU.is_equal),
             reads=["iot"], writes=["identf"])
        P.op("dve", lambda e: e.tensor_copy(out=identb[:], in_=identf[:]), reads=["identf"], writes=["identb"])
        def wview(w):
            return w.rearrange("(kc p) f -> p kc f", p=128)

        w_in_v, w_out_v, w_kv_v, w_qz_v, w_o_v = wview(w_in), wview(w_out), wview(w_kv), wview(w_qz), wview(w_o)
        CH_IN = list(range(8)); CH_OUT = [8, 9]; CH_Z = [10, 11]; CH_KV = 12; CH_Q = [13, 14]; CH_O = [15, 16]
        for i in range(8):
            for sec in range(4):
                P.op("pool", lambda e, i=i, sec=sec: e.dma_start(out=scr[i, :, :, sec * 128:(sec + 1) * 128],
                                                                 in_=w_in_v[:, :, sec * 1024 + i * 128: sec * 1024 + (i + 1) * 128]),
                     writes=["scr%d_%d" % (i, sec)], dma="wc%d" % i)
        scr_keys = {i: ["scr%d_%d" % (i, s) for s in range(4)] for i in range(8)}

        def cast_chunk(ch, src, c0):
            P.op("pool", lambda e: e.dma_start(out=scr[ch], in_=src[:, :, c0:c0 + 512]), writes=["scr%d" % ch], dma="wc%d" % ch)
            scr_keys[ch] = ["scr%d" % ch]

        cast_chunk(8, w_out_v, 0); cast_chunk(9, w_out_v, 512)
        cast_chunk(10, w_qz_v, 1024); cast_chunk(11, w_qz_v, 1536)
        cast_chunk(12, w_kv_v, 0)
        cast_chunk(13, w_qz_v, 0); cast_chunk(14, w_qz_v, 512)
        cast_chunk(15, w_o_v, 0); cast_chunk(16, w_o_v, 512)

        P.op("pool", lambda e: e.memset(onesm[:], 1.0 / 1024.0), writes=["onesm"])
        P.op("pool", lambda e: e.memset(ones1[:], 1.0), writes=["ones1"])
        P.op("pool", lambda e: e.memset(shpad[:], 0.0), writes=["shpad"])
        P.op("pool", lambda e: e.memset(vecs[:], 0.0), writes=["vecs"])
        P.op("pool", lambda e: e.memset(carry[:], 0.0), writes=["carry0", "carry1", "carry2"])
        P.op("pool", lambda e: e.memset(KT2[:], 0.0), writes=["KT2"])
        P.op("pool", lambda e: e.memset(Vaug[:], 0.0), writes=["Vaug%d" % s for s in range(5)])
        P.op("pool", lambda e: e.memset(Vaug[:, :, :, 64:65], 1.0), writes=["Vaug%d" % s for s in range(5)])
        P.op("pool", lambda e: e.memset(Pm[:], 0.0), writes=["Pm0", "Pm1", "Pm2"])
        P.op("pool", lambda e: e.memset(vtmp[:], 0.0), writes=["vtmp0", "vtmp1"])

        P.op("pool", lambda e: e.iota(posi[:, 0:64], [[128, 64]], base=0, channel_multiplier=1), writes=["posi"])
        P.op("pool", lambda e: e.iota(posi[:, 64:65], [[0, 1]], base=PAST_LEN, channel_multiplier=1), writes=["posi"])
        P.op("dve", lambda e: e.tensor_copy(out=posf[:], in_=posi[:]), reads=["posi"], writes=["posf"])
        inv = [float(np.float32(1.0) / (np.float32(ROPE_THETA) ** (np.float32(2 * i) / np.float32(16)))) for i in range(8)]
        TWO_PI = 2.0 * math.pi
        C1 = float(np.float32(6.28125))
        C2 = float(TWO_PI - 6.28125)
        for i in range(8):
            P.op("dve", lambda e, i=i: e.tensor_single_scalar(out=rtmp[:, :, i], in_=posf[:], scalar=inv[i], op=ALU.mult),
                 reads=["posf"], writes=["xT0", "xT1"])

        def sincos(dst, phase, dname):
            P.op("dve", lambda e: e.tensor_scalar(out=rtmp2[:], in0=rtmp[:], scalar1=phase, scalar2=1.0 / TWO_PI, op0=ALU.add, op1=ALU.mult),
                 reads=["xT0", "xT1"], writes=["xT2", "xT3"])
            P.op("dve", lambda e: e.tensor_copy(out=rki[:], in_=rtmp2[:]), reads=["xT2", "xT3"], writes=["xT4", "xT5"])
            P.op("dve", lambda e: e.tensor_copy(out=rtmp2[:], in_=rki[:]), reads=["xT4", "xT5"], writes=["xT2", "xT3"])
            P.op("dve", lambda e: e.scalar_tensor_tensor(out=dst[:], in0=rtmp2[:], scalar=-C1, in1=rtmp[:], op0=ALU.mult, op1=ALU.add),
                 reads=["xT2", "xT3", "xT0", "xT1"], writes=[dname])
            P.op("dve", lambda e: e.tensor_single_scalar(out=dst[:], in_=dst[:], scalar=phase, op=ALU.add),
                 reads=[dname], writes=[dname])
            P.op("dve", lambda e: e.scalar_tensor_tensor(out=dst[:], in0=rtmp2[:], scalar=-C2, in1=dst[:], op0=ALU.mult, op1=ALU.add),
                 reads=["xT2", "xT3", dname], writes=[dname])
            for sgn in (1.0, -1.0):
                cmpop = ALU.is_gt if sgn > 0 else ALU.is_lt
                P.op("dve", lambda e, cmpop=cmpop, sgn=sgn: e.tensor_single_scalar(out=rtmp2[:], in_=dst[:], scalar=sgn * math.pi, op=cmpop),
                     reads=[dname], writes=["xT2", "xT3"])
                P.op("dve", lambda e, sgn=sgn: e.scalar_tensor_tensor(out=dst[:], in0=rtmp2[:], scalar=-sgn * TWO_PI, in1=dst[:], op0=ALU.mult, op1=ALU.add),
                     reads=["xT2", "xT3", dname], writes=[dname])
            P.op("dve", lambda e: e.tensor_scalar(out=dst[:], in0=dst[:], scalar1=-math.pi, scalar2=math.pi, op0=ALU.max, op1=ALU.min),
                 reads=[dname], writes=[dname])
            P.op("act", lambda e: e.activation(out=dst[:], in_=dst[:], func=AF.Sin), reads=[dname], writes=[dname])

        sincos(sinT, 0.0, "sinT")
        sincos(cosT, math.pi / 2.0, "cosT")
        P.op("dve", lambda e: e.tensor_single_scalar(out=nsinT[:], in_=sinT[:], scalar=-1.0, op=ALU.mult),
             reads=["sinT"], writes=["nsinT"])

        ring_state = {"next_slot": 0, "loaded": {}}

        def ring_load(ch):
            s = ring_state["next_slot"] % NSLOT
            ring_state["next_slot"] += 1
            P.op("sp", lambda e: e.dma_start(out=ring[:, s], in_=scr[ch]), reads=scr_keys[ch], writes=["ring%d" % s], dma="ring%d" % s)
            ring_state["loaded"][ch] = s
            return s

        def vload(r0, src, nrows):
            P.op("sp", lambda e: e.dma_start(out=vecs[r0:r0 + nrows, :], in_=src.rearrange("(r c) -> r c", c=128)),
                 reads=["vecs"], writes=["vecs_%d" % r0], dma="vec%d" % r0)
            return "vecs_%d" % r0

        vk = [vload(0, b_mod_a, 24), vload(24, b_mod_kv, 16), vload(40, b_mod_b, 24), vload(64, g_a, 8),
              vload(72, g_kv, 8), vload(80, g_b, 8), vload(88, g_f, 8), vload(96, conv_w.rearrange("t d -> (t d)"), 24)]
        k0 = mm1()
        P.op("pe", lambda e: e.transpose(out=big[:, k0, 0:128], in_=vecs[:, :], identity=identf[:]),
             reads=vk + ["vecs", "identf"], writes=["mm%d" % k0])
        P.op("dve", lambda e: e.tensor_copy(out=vecT[:], in_=big[:, k0, 0:128]), reads=["mm%d" % k0], writes=["vecT"])
        gaT, gkvT, gbT, gfT = vecT[:, 64:72], vecT[:, 72:80], vecT[:, 80:88], vecT[:, 88:96]
        cwT = vecT[:, 96:120]

        P.op("sp", lambda e: e.dma_start(out=sink_bc[:], in_=sinks.partition_broadcast(128)), writes=["sink_bc"], dma="vsink")
        P.op("dve", lambda e: e.tensor_single_scalar(out=nsink[:], in_=sink_bc[:], scalar=-1.0, op=ALU.mult),
             reads=["sink_bc"], writes=["nsink"])
        P.op("dve", lambda e: e.tensor_reduce(out=nsmax[:], in_=sink_bc[:].rearrange("p (q g) -> p q g", g=4), axis=AX.X, op=ALU.max),
             reads=["sink_bc"], writes=["nsmax"])
        P.op("dve", lambda e: e.tensor_single_scalar(out=nsmax[:], in_=nsmax[:], scalar=-1.0, op=ALU.mult), reads=["nsmax"], writes=["nsmax"])
        P.op("dve", lambda e: e.tensor_copy(out=sink_p[:].rearrange("p (q a t) -> p q a t", a=2, t=2),
                                            in_=sink_bc[:].rearrange("p (q t a) -> p q a t", a=2, t=2)), reads=["sink_bc"], writes=["sink_p"])
        P.op("dve", lambda e: e.tensor_copy(out=nsink_p[:].rearrange("p (q a t) -> p q a t", a=2, t=2),
                                            in_=nsink[:].rearrange("p (q t a) -> p q a t", a=2, t=2)), reads=["nsink"], writes=["nsink_p"])

        for n_ in range(3):
            P.op("sp", lambda e, n_=n_: e.dma_start(out=cT[:, :, n_], in_=cmat[n_].rearrange("(kc p) -> p kc", p=128)), writes=["cT"], dma="vcT")
        P.op("act", lambda e: e.activation(out=scT[:], in_=cT[:], func=AF.Silu), reads=["cT"], writes=["scT"])

        mods = [(w_mod_a, 3 * D, 0), (w_mod_kv, 2 * D, 24), (w_mod_b, 3 * D, 40)]
        for wm, width, j0 in mods:
            wmv = wview(wm)
            for blk in range(width // 512):
                s = ring_state["next_slot"] % NSLOT
                ring_state["next_slot"] += 1
                P.op("pool", lambda e, s=s, wmv=wmv, blk=blk: e.dma_start(out=ring[:, s], in_=wmv[:, :, blk * 512:(blk + 1) * 512]),
                     writes=["ring%d" % s], dma="ringp%d" % s)
                k = mm1()
                for kc in range(KC):
                    P.op("pe", lambda e, s=s, kc=kc, k=k: e.matmul(big[0:3, k, :], lhsT=scT[:, kc, :], rhs=ring[:, s, kc, :],
                                                                   start=(kc == 0), stop=(kc == KC - 1)),
                         reads=["scT", "ring%d" % s], writes=["mm%d" % k])
                P.op("dve", lambda e, k=k: e.tensor_copy(out=mrow[:], in_=big[0:3, k, :]), reads=["mm%d" % k], writes=["mrow"])
                k2 = mm1()
                for q4 in range(4):
                    P.op("pe", lambda e, q4=q4, k2=k2: e.transpose(out=big[:, k2, q4 * 4:q4 * 4 + 3], in_=mrow[:, q4 * 128:(q4 + 1) * 128],
                                                                  identity=identf[0:3, 0:3]),
                         reads=["mrow", "identf"], writes=["mm%d" % k2])
                jj = j0 + blk * 4
                P.op("dve", lambda e, k2=k2, jj=jj: e.tensor_tensor(
                    out=modT[:, jj:jj + 4, :], in0=big[:, k2, 0:16].rearrange("p (q n) -> p q n", n=4)[:, :, 0:3],
                    in1=vecT[:, jj:jj + 4].unsqueeze(2).to_broadcast([128, 4, 3]), op=ALU.add),
                    reads=["mm%d" % k2, "vecT"], writes=["modT"])
        gateA = modT[:, 16:24, :]
        gateB = modT[:, 56:64, :]
        for dst, sc0, gT, dname in ((gsA, 8, gaT, "gsA"), (gsKV, 32, gkvT, "gsKV"), (gsB, 48, gbT, "gsB")):
            P.op("dve", lambda e, dst=dst, sc0=sc0, gT=gT: e.scalar_tensor_tensor(
                out=dst[:], in0=modT[:, sc0:sc0 + 8, :], scalar=1.0, in1=gT.unsqueeze(2).to_broadcast([128, 8, 3]),
                op0=ALU.add, op1=ALU.mult), reads=["modT", "vecT"], writes=[dname])
        for si, sh0 in enumerate((0, 24, 40)):
            for n in range(3):
                P.op("dve", lambda e, si=si, sh0=sh0, n=n: e.tensor_copy(out=shpad[:, si, :, 32 * n], in_=modT[:, sh0:sh0 + 8, n]),
                     reads=["modT", "shpad"], writes=["shpad"])

        for ch in CH_IN + CH_Z:
            s = ring_load(ch)
            for blk in range(4):
                k = mm1()
                si = 0 if ch < 8 else 2
                for kc in range(KC):
                    P.op("pe", lambda e, s=s, kc=kc, k=k, blk=blk, si=si: e.matmul(
                        big[:, k, 0:65], lhsT=ring[:, s, kc, blk * 128:(blk + 1) * 128], rhs=shpad[:, si, kc, :],
                        start=(kc == 0), stop=(kc == KC - 1)), reads=["ring%d" % s, "shpad"], writes=["mm%d" % k])
                if ch < 8:
                    dstap = bias_in[:, blk * 8 + ch, :]
                    dk = "bias_in"
                else:
                    dstap = bias_z[:, (ch - CH_Z[0]) * 4 + blk, :]
                    dk = "bias_z"
                P.op("dve", lambda e, k=k, dstap=dstap: e.tensor_copy(out=dstap, in_=big[:, k, 0:65:32]),
                     reads=["mm%d" % k], writes=[dk])
        for ci, (ch, si) in enumerate(((CH_KV, 1), (CH_Q[0], 2), (CH_Q[1], 2))):
            s = ring_load(ch)
            k = mm1()
            for kc in range(KC):
                P.op("pe", lambda e, s=s, kc=kc, k=k, si=si: e.matmul(big[0:65, k, :], lhsT=shpad[:, si, kc, :], rhs=ring[:, s, kc, :],
                                                                        start=(kc == 0), stop=(kc == KC - 1)),
                     reads=["ring%d" % s, "shpad"], writes=["mm%d" % k])
            P.op("dve", lambda e, k=k, ci=ci: e.tensor_copy(out=brow_hi[:, ci * 512:(ci + 1) * 512], in_=big[0:65, k, :]),
                 reads=["mm%d" % k], writes=["brow_hi%d" % ci])
            P.op("dve", lambda e, ci=ci: e.tensor_copy(out=brow_f[:], in_=brow_hi[:, ci * 512:(ci + 1) * 512]),
                 reads=["brow_hi%d" % ci], writes=["brow_f"])
            P.op("dve", lambda e, k=k: e.tensor_tensor(out=brow_t[:], in0=big[0:65, k, :], in1=brow_f[:], op=ALU.subtract),
                 reads=["mm%d" % k, "brow_f"], writes=["brow_t"])
            P.op("dve", lambda e, ci=ci: e.tensor_copy(out=brow_lo[:, ci * 512:(ci + 1) * 512], in_=brow_t[:]),
                 reads=["brow_t"], writes=["brow_lo%d" % ci])

        if with_sample:
            for s_ in range(2):
                for t_ in range(2):
                    P.op("sp", lambda e, s_=s_, t_=t_: e.dma_start(out=carry[:, 1 + s_, :, t_], in_=st_conv[s_, t_].rearrange("(c p) -> p c", p=128)),
                         reads=[], writes=["carry%d" % (1 + s_)], dma="vcar%d" % s_)
                P.op("sp", lambda e, s_=s_: e.dma_start(out=k_s[s_, 0:112, :], in_=cache_k[s_, 16:128, :]), writes=["o_ksc%d" % s_], dma="outm")
                P.op("sp", lambda e, s_=s_: e.dma_start(out=v_s[s_, 0:112, :], in_=cache_v[s_, 16:128, :]), writes=["o_vsc%d" % s_], dma="outm")
                outkeys.extend(["o_ksc%d" % s_, "o_vsc%d" % s_])

        xin_ctr = [0]
        xpre = {}
        yout_ctr = [0]
        ktr = [0]

        class NormAcc:
            def __init__(self, T):
                self.T, self.k, self.n, self.pend = T, mm1(), 0, None
                mm_reserved.add(self.k)

            def add(self, kc, xkeys):
                T, k = self.T, self.k
                sl = ktr[0] % 2
                ktr[0] += 1
                P.op("act", lambda e: e.activation(out=sq[:, sl, 0:T], in_=xT[:, kc, 0:T], func=AF.Square), reads=xkeys, writes=["sq%d" % sl])
                self.flush()
                self.pend = sl

            def flush(self):
                if self.pend is None:
                    return
                T, k, sl, n = self.T, self.k, self.pend, self.n
                P.op("pe", lambda e: e.matmul(big[:, k, 0:T], lhsT=onesm[:], rhs=sq[:, sl, 0:T], start=(n == 0), stop=(n == KC - 1)),
                     reads=["sq%d" % sl, "onesm"], writes=["mm%d" % k])
                self.n += 1
                self.pend = None

            def finish(self):
                T, k = self.T, self.k
                self.flush()
                assert self.n == KC
                P.op("act", lambda e: e.activation(out=rstd[:, 0:T], in_=big[:, k, 0:T], func=AF.Ln, bias=EPS, scale=1.0),
                     reads=["mm%d" % k], writes=["rstd"])
                P.op("act", lambda e: e.activation(out=rstd[:, 0:T], in_=rstd[:, 0:T], func=AF.Exp, scale=-0.5),
                     reads=["rstd"], writes=["rstd"])
                mm_reserved.discard(k)

        def norm_stats(T, xkeys_of_kc):
            na = NormAcc(T)
            for kc in range(KC):
                na.add(kc, xkeys_of_kc(kc))
            na.finish()

        def run_tile(T, segs, ioblocks, blocks, tile_idx, next_iob, last_tile_out):
            nblk_keys = [b_["c0"] for b_ in blocks]

            def xk(kc):
                return ["xT%d" % kc]

            def xload(ioe, key):
                (c0_, rows_, src_, dst_) = ioe
                sl_ = xin_ctr[0] % 2
                xin_ctr[0] += 1
                P.op("sp", lambda e, sl_=sl_, rows_=rows_, src_=src_: e.dma_start(out=xtok[0:rows_, sl_, :], in_=src_),
                     writes=["xtok%d" % sl_], dma="xin%d" % sl_)
                xpre[key] = sl_

            for bidx, (c0, rows, src, dst) in enumerate(ioblocks):
                if (tile_idx, bidx) not in xpre:
                    xload((c0, rows, src, dst), (tile_idx, bidx))
                sl = xpre.pop((tile_idx, bidx))
                for half in range(2):
                    k = mm1()
                    for q4 in range(4):
                        kc = half * 4 + q4
                        P.op("pe", lambda e, sl=sl, rows=rows, kc=kc, q4=q4, k=k: e.transpose(
                            out=big[:, k, q4 * 128:q4 * 128 + rows], in_=xtok[0:rows, sl, kc * 128:(kc + 1) * 128],
                            identity=identf[0:rows, 0:rows]), reads=["xtok%d" % sl, "identf"], writes=["mm%d" % k])
                    P.op("act", lambda e, half=half, k=k, c0=c0, rows=rows: e.activation(
                        out=xT[:, half * 4:half * 4 + 4, c0:c0 + rows],
                        in_=big[:, k, :].rearrange("p (q t) -> p q t", t=128)[:, :, 0:rows], func=AF.Copy),
                        reads=["mm%d" % k], writes=["xT%d" % kc_ for kc_ in range(half * 4, half * 4 + 4)])

            if next_iob is not None:
                for bidx in range(min(2, len(next_iob))):
                    xload(next_iob[bidx], (tile_idx + 1, bidx))
            if stage < 1:
                return
            norm_stats(T, xk)
            for kc in range(KC):
                for (c0, n, bt, ci) in segs:
                    P.op("dve", lambda e, kc=kc, c0=c0, n=n, bt=bt: e.scalar_tensor_tensor(
                        out=h1[:, kc, c0:c0 + n], in0=xT[:, kc, c0:c0 + n], scalar=gsA[:, kc, bt:bt + 1], in1=rstd[:, c0:c0 + n],
                        op0=ALU.mult, op1=ALU.mult), reads=xk(kc) + ["rstd", "gsA"], writes=["h1_%d" % kc])
            h1keys = ["h1_%d" % kc for kc in range(KC)]
            h2keys = ["h2_%d" % kc for kc in range(KC)]

            if stage < 2:
                return
            pending = None
            for i in range(8):
                s = ring_state["loaded"].pop(i) if i in ring_state["loaded"] else ring_load(i)
                sl = i % 2
                accs = {}
                for sec in (1, 2, 3, 0):
                    k = mm1()
                    accs[sec] = k
                    for kc in range(KC):
                        P.op("pe", lambda e, s=s, kc=kc, k=k, sec=sec: e.matmul(
                            big[:, k, 0:T], lhsT=ring[:, s, kc, sec * 128:(sec + 1) * 128], rhs=h1[:, kc, 0:T],
                            start=(kc == 0), stop=(kc == KC - 1)), reads=["ring%d" % s, "h1_%d" % kc], writes=["mm%d" % k])
                    for si_, (c0, n, bt, ci) in enumerate(segs):
                        if sec == 1:
                            P.op("act", lambda e, k=k, c0=c0, n=n, bt=bt, sl=sl, i=i: e.activation(
                                out=c_sb[:, sl, c0:c0 + n], in_=big[:, k, c0:c0 + n], func=AF.Identity,
                                bias=bias_in[:, 8 + i, bt:bt + 1], scale=1.0), reads=["mm%d" % k, "bias_in"], writes=["c_sb%d" % sl])
                        elif sec == 2:
                            P.op("pool", lambda e, sl=sl, si_=si_, ci=ci, i=i: e.tensor_copy(out=vtmp[:, sl, si_ * 40:si_ * 40 + 2], in_=carry[:, ci, i, :]),
                                 reads=["carry%d" % ci], writes=["vtmp%d" % sl])
                            P.op("dve", lambda e, k=k, c0=c0, n=n, bt=bt, sl=sl, si_=si_, i=i: e.scalar_tensor_tensor(
                                out=vtmp[:, sl, si_ * 40 + 2:si_ * 40 + 2 + n], in0=big[:, k, c0:c0 + n], scalar=bias_in[:, 16 + i, bt:bt + 1],
                                in1=c_sb[:, sl, c0:c0 + n], op0=ALU.add, op1=ALU.mult),
                                reads=["mm%d" % k, "bias_in", "c_sb%d" % sl], writes=["vtmp%d" % sl])
                            P.op("pool", lambda e, sl=sl, si_=si_, ci=ci, i=i, n=n: e.tensor_copy(out=carry[:, ci, i, :], in_=vtmp[:, sl, si_ * 40 + n:si_ * 40 + n + 2]),
                                 reads=["vtmp%d" % sl], writes=["carry%d" % ci])
                            P.op("act", lambda e, sl=sl, si_=si_, c0=c0, n=n, i=i: e.activation(
                                out=ycv[:, sl, c0:c0 + n], in_=vtmp[:, sl, si_ * 40:si_ * 40 + n], func=AF.Copy, scale=cwT[:, i:i + 1]),
                                reads=["vtmp%d" % sl, "vecT"], writes=["ycv%d" % sl])
                            for tap in (1, 2):
                                P.op("dve", lambda e, sl=sl, si_=si_, c0=c0, n=n, i=i, tap=tap: e.scalar_tensor_tensor(
                                    out=ycv[:, sl, c0:c0 + n], in0=vtmp[:, sl, si_ * 40 + tap:si_ * 40 + tap + n], scalar=cwT[:, tap * 8 + i:tap * 8 + i + 1],
                                    in1=ycv[:, sl, c0:c0 + n], op0=ALU.mult, op1=ALU.add),
                                    reads=["vtmp%d" % sl, "vecT", "ycv%d" % sl], writes=["ycv%d" % sl])
                        elif sec == 3:
                            P.op("act", lambda e, k=k, c0=c0, n=n, bt=bt, sl=sl, i=i: e.activation(
                                out=szA[:, sl, c0:c0 + n], in_=big[:, k, c0:c0 + n], func=AF.Silu,
                                bias=bias_in[:, 24 + i, bt:bt + 1], scale=1.0), reads=["mm%d" % k, "bias_in"], writes=["szA%d" % sl])
                            P.op("pool", lambda e, sl=sl, c0=c0, n=n: e.tensor_tensor(
                                out=ycv[:, sl, c0:c0 + n], in0=ycv[:, sl, c0:c0 + n], in1=szA[:, sl, c0:c0 + n], op=ALU.mult),
                                reads=["ycv%d" % sl, "szA%d" % sl], writes=["ycv%d" % sl])
                        else:
                            P.op("dve", lambda e, k=k, c0=c0, n=n, bt=bt, sl=sl, i=i: e.scalar_tensor_tensor(
                                out=mT[:, i, c0:c0 + n], in0=big[:, k, c0:c0 + n], scalar=bias_in[:, i, bt:bt + 1],
                                in1=ycv[:, sl, c0:c0 + n], op0=ALU.add, op1=ALU.mult),
                                reads=["mm%d" % k, "bias_in", "ycv%d" % sl], writes=["mT%d" % i])
                nxt = {0: 5, 1: 6, 2: 7, 3: 8, 4: 9, 5: 10, 6: 11, 7: 12}[i]
                ring_load(nxt)

            if stage < 3:
                return
            na1 = NormAcc(T)
            for oc in range(8):
                ch = CH_OUT[oc // 4]
                s = ring_state["loaded"][ch]
                k = mm1()
                for ic in range(KC):
                    P.op("pe", lambda e, s=s, ic=ic, k=k, oc=oc: e.matmul(
                        big[:, k, 0:T], lhsT=ring[:, s, ic, (oc % 4) * 128:(oc % 4 + 1) * 128], rhs=mT[:, ic, 0:T],
                        start=(ic == 0), stop=(ic == KC - 1)), reads=["ring%d" % s, "mT%d" % ic], writes=["mm%d" % k])
                for (c0, n, bt, ci) in segs:
                    P.op("dve", lambda e, k=k, oc=oc, c0=c0, n=n, bt=bt: e.scalar_tensor_tensor(
                        out=xT[:, oc, c0:c0 + n], in0=big[:, k, c0:c0 + n], scalar=gateA[:, oc, bt:bt + 1], in1=xT[:, oc, c0:c0 + n],
                        op0=ALU.mult, op1=ALU.add), reads=["mm%d" % k, "modT"] + xk(oc), writes=xk(oc))
                na1.add(oc, xk(oc))
                if oc == 3:
                    ring_state["loaded"].pop(CH_OUT[0]); ring_load(13)
                if oc == 7:
                    ring_state["loaded"].pop(CH_OUT[1]); ring_load(14)

            if stage < 4:
                return
            na1.finish()
            for hbuf, gsv, hname, gname in ((h2, gsB, "h2_%d", "gsB"), (h1, gsKV, "h1_%d", "gsKV")):
                for kc in range(KC):
                    for (c0, n, bt, ci) in segs:
                        P.op("dve", lambda e, kc=kc, c0=c0, n=n, bt=bt, hbuf=hbuf, gsv=gsv: e.scalar_tensor_tensor(
                            out=hbuf[:, kc, c0:c0 + n], in0=xT[:, kc, c0:c0 + n], scalar=gsv[:, kc, bt:bt + 1], in1=rstd[:, c0:c0 + n],
                            op0=ALU.mult, op1=ALU.mult), reads=xk(kc) + ["rstd", gname], writes=[hname % kc])

            if stage < 5:
                return
            s_kv = ring_state["loaded"][CH_KV]
            s_q = [ring_state["loaded"][CH_Q[0]], ring_state["loaded"][CH_Q[1]]]

            if stage < 6:
                return
            def pjkv(bi):
                B = blocks[bi]
                c0, R, bt, osl, sl = B["c0"], B["R"], B["n"], B["oslot"], bi % 2
                k = mm1()
                P.op("pe", lambda e, k=k, R=R, bt=bt: e.matmul(
                    big[0:R, k, :], lhsT=ones1[32 * bt:32 * bt + 1, 0:R], rhs=brow_hi[32 * bt:32 * bt + 1, 0:512],
                    start=True, stop=False), reads=["ones1", "brow_hi0"], writes=["mm%d" % k])
                for kc in range(KC):
                    P.op("pe", lambda e, kc=kc, k=k, c0=c0, R=R: e.matmul(
                        big[0:R, k, :], lhsT=h1[:, kc, c0:c0 + R], rhs=ring[:, s_kv, kc, :], start=False, stop=(kc == KC - 1)),
                        reads=["h1_%d" % kc, "ring%d" % s_kv], writes=["mm%d" % k])
                P.op("act", lambda e, k=k, R=R, sl=sl: e.activation(out=kf[0:R, sl, :], in_=big[0:R, k, 0:256], func=AF.Copy),
                     reads=["mm%d" % k], writes=["kf%d" % sl])
                P.op("dve", lambda e, k=k, R=R, osl=osl: e.tensor_copy(
                    out=Vaug[0:R, osl, :, 0:64], in_=big[0:R, k, 256:512].rearrange("p (h d) -> p h d", d=64)),
                    reads=["mm%d" % k], writes=["Vaug%d" % osl])
                if B["vout"] is not None:
                    P.op("act", lambda e, k=k, R=R: e.activation(out=vf[0:R, :], in_=big[0:R, k, 256:512], func=AF.Copy),
                         reads=["mm%d" % k], writes=["vf"])
                    ok = "o_v_%d_%d" % (bt, bi)
                    P.op("sp", lambda e, R=R, dst=B["vout"]: e.dma_start(out=dst, in_=vf[0:R, :]), reads=["vf"], writes=[ok], dma="outv%d" % bi)
                    outkeys.append(ok)

            def pjq(bi, hf):
                B = blocks[bi]
                c0, R, bt, sl = B["c0"], B["R"], B["n"], bi % 2
                k = mm1()
                P.op("pe", lambda e, k=k, R=R, bt=bt: e.matmul(
                    big[0:R, k, :], lhsT=ones1[32 * bt:32 * bt + 1, 0:R], rhs=brow_hi[32 * bt:32 * bt + 1, 512 * (1 + hf):512 * (2 + hf)],
                    start=True, stop=False), reads=["ones1", "brow_hi%d" % (1 + hf)], writes=["mm%d" % k])
                for kc in range(KC):
                    P.op("pe", lambda e, kc=kc, k=k, c0=c0, R=R: e.matmul(
                        big[0:R, k, :], lhsT=h2[:, kc, c0:c0 + R], rhs=ring[:, s_q[hf], kc, :], start=False, stop=(kc == KC - 1)),
                        reads=["h2_%d" % kc, "ring%d" % s_q[hf]], writes=["mm%d" % k])
                if hf == 0:
                    P.op("act", lambda e, k=k, R=R, sl=sl: e.activation(out=qf[0:R, sl, 0:512], in_=big[0:R, k, :], func=AF.Copy),
                         reads=["mm%d" % k], writes=["qf%d_0" % sl])
                else:
                    P.op("dve", lambda e, k=k, R=R, sl=sl: e.tensor_copy(out=qf[0:R, sl, 512:1024], in_=big[0:R, k, :]),
                         reads=["mm%d" % k], writes=["qf%d_1" % sl])

            def pj0(bi):
                pjkv(bi); pjq(bi, 0); pjq(bi, 1)

            def pj1(bi):
                B = blocks[bi]
                R, bt, pb, sl = B["R"], B["n"], B["posblk"], bi % 2
                cosb = cosT[0:R, pb, :]
                for (src, nh, keys) in ((kf[0:R, sl, :].rearrange("p (h d) -> p h d", d=64), 4, ["kf%d" % sl]),
                                        (qf[0:R, sl, :].rearrange("p (h d) -> p h d", d=64), 16, ["qf%d_0" % sl, "qf%d_1" % sl])):
                    P.op("pool", lambda e, R=R, src=src, nh=nh, cosb=cosb: e.tensor_tensor(
                        out=ra[0:R, 0:nh, :].rearrange("p h (t i) -> p h t i", i=8), in0=src[:, :, 0:16].rearrange("p h (t i) -> p h t i", i=8),
                        in1=cosb.unsqueeze(1).unsqueeze(1).to_broadcast([R, nh, 2, 8]), op=ALU.mult),
                        reads=keys + ["cosT"], writes=["ra"])
                    P.op("pool", lambda e, R=R, src=src, nh=nh, pb=pb: e.tensor_tensor(
                        out=rb[0:R, 0:nh, 0:8], in0=src[:, :, 8:16], in1=nsinT[0:R, pb, :].unsqueeze(1).to_broadcast([R, nh, 8]), op=ALU.mult),
                        reads=keys + ["nsinT"], writes=["rb"])
                    P.op("pool", lambda e, R=R, src=src, nh=nh, pb=pb: e.tensor_tensor(
                        out=rb[0:R, 0:nh, 8:16], in0=src[:, :, 0:8], in1=sinT[0:R, pb, :].unsqueeze(1).to_broadcast([R, nh, 8]), op=ALU.mult),
                        reads=keys + ["sinT"], writes=["rb"])
                    P.op("pool", lambda e, R=R, src=src, nh=nh: e.tensor_tensor(out=src[:, :, 0:16], in0=ra[0:R, 0:nh, :], in1=rb[0:R, 0:nh, :], op=ALU.add),
                         reads=["ra", "rb"], writes=keys)
                if B["kout"] is not None:
                    ok = "o_k_%d_%d" % (bt, bi)
                    P.op("sp", lambda e, R=R, sl=sl, dst=B["kout"]: e.dma_start(out=dst, in_=kf[0:R, sl, :]), reads=["kf%d" % sl], writes=[ok], dma="outk%d" % bi)
                    outkeys.append(ok)

            def pj2(bi):
                B = blocks[bi]
                R, kown, sl = B["R"], B["kbase"] + 128, bi % 2
                P.op("act", lambda e, R=R, sl=sl: e.activation(
                    out=kb2[0:R], in_=kf[0:R, sl, :].rearrange("p (h d) -> p h d", d=64).unsqueeze(2).to_broadcast([R, 4, 2, 64]), func=AF.Copy),
                    reads=["kf%d" % sl], writes=["kb2"])
                P.op("act", lambda e, R=R, sl=sl: e.activation(out=qb[0:R, sl, 0:512], in_=qf[0:R, sl, 0:512], func=AF.Copy, scale=0.125),
                     reads=["qf%d_0" % sl], writes=["qb%d_0" % sl])
                P.op("dve", lambda e, R=R, sl=sl: e.tensor_single_scalar(out=qb[0:R, sl, 512:1024], in_=qf[0:R, sl, 512:1024], scalar=0.125, op=ALU.mult),
                     reads=["qf%d_1" % sl], writes=["qb%d_1" % sl])
                for h in range(4):
                    P.op("pe", lambda e, R=R, h=h: e.transpose(out=ptp[:, 2, h * 128:h * 128 + R],
                                                                in_=kb2[0:R, h].rearrange("p t d -> p (t d)"), identity=identb[0:R, 0:R]),
                         reads=["kb2", "identb"], writes=["trb"])
                P.op("dve", lambda e, R=R, kown=kown: e.tensor_copy(
                    out=KT2[:, :, kown:kown + R], in_=ptp[:, 2, 0:512].rearrange("p (h t) -> p h t", t=128)[:, :, 0:R]),
                    reads=["trb"], writes=[B["kok"]])
                for c in range(8):
                    P.op("pe", lambda e, R=R, c=c, sl=sl: e.transpose(out=ptp[:, 2, c * 128:c * 128 + R], in_=qb[0:R, sl, c * 128:(c + 1) * 128],
                                                                       identity=identb[0:R, 0:R]),
                         reads=["qb%d_%d" % (sl, c // 4), "identb"], writes=["trb"])
                P.op("act", lambda e, R=R, bi=bi: e.activation(
                    out=qT[:, bi, :, 0:R], in_=ptp[:, 2, :].rearrange("p (c t) -> p c t", t=128)[:, :, 0:R], func=AF.Copy),
                    reads=["trb"], writes=["qT%d" % bi])

            units = [(bi, Q) for bi in range(len(blocks)) for Q in range(4)]
            ustate = {}

            def stA(u):
                bi, Q = units[u]
                B = blocks[bi]
                R, kbase = B["R"], B["kbase"]
                N = 128 + R
                ps_, st3 = u % 3, u % 4
                k = mm2()
                S = big[:, k:k + 2, :].rearrange("p a (g n) -> p (a g) n", n=256)
                kkeys = [B["khk"], B["kok"]]
                for g in (0, 2, 1, 3):
                    hq = 4 * Q + 2 * (g % 2) + g // 2
                    c, j = hq // 2, hq % 2
                    P.op("pe", lambda e, R=R, N=N, g=g, c=c, j=j, Q=Q, bi=bi, S=S, kbase=kbase: e.matmul(
                        S[0:R, g, 0:N], lhsT=qT[64 * j:64 * j + 64, bi, c, 0:R], rhs=KT2[64 * j:64 * j + 64, Q, kbase:kbase + N],
                        start=True, stop=True), reads=["qT%d" % bi] + kkeys, writes=["mm%d" % (k + g // 2)])
                skeys = ["mm%d" % k, "mm%d" % (k + 1)]
                sk = "stt%d" % st3
                P.op("dve", lambda e, R=R, N=N, S=S, st3=st3: e.tensor_reduce(
                    out=stt[0:R, st3, 0, 0:1], in_=S[0:R, :, 0:N], axis=AX.XY, op=ALU.max), reads=skeys, writes=[sk + "mx"])
                P.op("dve", lambda e, R=R, st3=st3, Q=Q: e.scalar_tensor_tensor(
                    out=stt[0:R, st3, 1, 0:1], in0=stt[0:R, st3, 0, 0:1], scalar=-1.0, in1=nsmax[0:R, Q:Q + 1], op0=ALU.mult, op1=ALU.min),
                    reads=[sk + "mx", "nsmax"], writes=[sk + "nb"])
                P.op("act", lambda e, R=R, N=N, S=S, ps_=ps_, st3=st3: e.activation(
                    out=Pm[0:R, ps_, :, 0:N], in_=S[0:R, :, 0:N], func=AF.Exp, bias=stt[0:R, st3, 1, 0:1], scale=1.0),
                    reads=skeys + [sk + "nb"], writes=["Pm%d" % ps_])
                if B["mask"]:
                    P.op("pool", lambda e, ps_=ps_: e.memset(Pm[0:64, ps_, :, 192:256], 0.0), writes=["Pm%d" % ps_])
                    if B["first"]:
                        P.op("pool", lambda e, ps_=ps_: e.memset(Pm[0:64, ps_, :, 0:128], 0.0), writes=["Pm%d" % ps_])
                        P.op("pool", lambda e, ps_=ps_: e.memset(Pm[64:128, ps_, :, 0:128], 0.0), writes=["Pm%d" % ps_])
                    else:
                        P.op("pool", lambda e, ps_=ps_: e.memset(Pm[64:128, ps_, :, 0:64], 0.0), writes=["Pm%d" % ps_])
                P.op("act", lambda e, R=R, st3=st3, Q=Q: e.activation(
                    out=stt[0:R, st3, 3, :], in_=sink_p[0:R, 4 * Q:4 * Q + 4], func=AF.Exp, bias=stt[0:R, st3, 1, 0:1], scale=1.0),
                    reads=[sk + "nb", "sink_p"], writes=[sk + "es"])

            def stB(u):
                bi, Q = units[u]
                B = blocks[bi]
                R = B["R"]
                ps_, pm_ = u % 2, u % 3
                for g in range(4):
                    for kb in range(2):
                        nk = 128 if kb == 0 else R
                        P.op("pe", lambda e, R=R, g=g, kb=kb, nk=nk, ps_=ps_, pm_=pm_: e.transpose(
                            out=ptp[0:nk, ps_, (kb * 4 + g) * 128:(kb * 4 + g) * 128 + R], in_=Pm[0:R, pm_, g, kb * 128:kb * 128 + nk],
                            identity=identb[0:R, 0:R]), reads=["Pm%d" % pm_, "identb"], writes=["pt%d" % ps_])
                if R == 128:
                    P.op("act", lambda e, ps_=ps_: e.activation(
                        out=PTs[:, ps_].rearrange("p k g t -> p (k g) t"), in_=ptp[:, ps_, :].rearrange("p (kg t) -> p kg t", t=128), func=AF.Copy),
                        reads=["pt%d" % ps_], writes=["PTs%d_0" % ps_, "PTs%d_1" % ps_])
                else:
                    for kb in range(2):
                        nk = 128 if kb == 0 else R
                        src = ptp[0:nk, ps_, kb * 512:(kb + 1) * 512].rearrange("p (g t) -> p g t", t=128)[:, :, 0:R]
                        P.op("dve", lambda e, nk=nk, R=R, kb=kb, ps_=ps_, src=src: e.tensor_copy(out=PTs[0:nk, ps_, kb, :, 0:R], in_=src),
                             reads=["pt%d" % ps_], writes=["PTs%d_%d" % (ps_, kb)])

            def stC(u):
                bi, Q = units[u]
                B = blocks[bi]
                R, hsl, osl = B["R"], B["hslot"], B["oslot"]
                ps_, st3, sl = u % 2, u % 4, bi % 2
                sk = "stt%d" % st3
                O3 = ops_[:, 0:260].rearrange("p (g d) -> p g d", d=65)
                for g in range(4):
                    for ki, kb in enumerate((1, 0)):
                        nk = 128 if kb == 0 else R
                        vs = hsl if kb == 0 else osl
                        P.op("pe", lambda e, R=R, g=g, kb=kb, ki=ki, nk=nk, vs=vs, Q=Q, ps_=ps_, O3=O3: e.matmul(
                            O3[0:R, g, :], lhsT=PTs[0:nk, ps_, kb, g, 0:R], rhs=Vaug[0:nk, vs, Q, :], start=(ki == 0), stop=(ki == 1)),
                            reads=["PTs%d_%d" % (ps_, kb), "Vaug%d" % vs], writes=["ops"])
                P.op("dve", lambda e, R=R, st3=st3, O3=O3: e.tensor_tensor(
                    out=stt[0:R, st3, 4, :], in0=O3[0:R, :, 64], in1=stt[0:R, st3, 3, :], op=ALU.add),
                    reads=["ops", sk + "es"], writes=[sk + "den"])
                P.op("dve", lambda e, R=R, st3=st3: e.reciprocal(out=stt[0:R, st3, 5, :], in_=stt[0:R, st3, 4, :]),
                     reads=[sk + "den"], writes=[sk + "ri"])
                P.op("dve", lambda e, R=R, st3=st3, Q=Q, sl=sl, O3=O3: e.tensor_tensor(
                    out=otok[0:R, sl, 256 * Q:256 * (Q + 1)].rearrange("p (t a d) -> p a t d", t=2, a=2, d=64),
                    in0=O3[0:R, :, 0:64].rearrange("p (a t) d -> p a t d", a=2),
                    in1=stt[0:R, st3, 5, :].rearrange("p (a t) -> p a t", a=2).unsqueeze(3).to_broadcast([R, 2, 2, 64]), op=ALU.mult),
                    reads=["ops", sk + "ri"], writes=["otok%d" % sl])
                if Q == 3:
                    c0 = B["c0"]
                    for c in range(8):
                        P.op("pe", lambda e, R=R, c=c, sl=sl: e.transpose(out=ptp[:, 2, c * 128:c * 128 + R], in_=otok[0:R, sl, c * 128:(c + 1) * 128],
                                                                           identity=identb[0:R, 0:R]),
                             reads=["otok%d" % sl, "identb"], writes=["trb"])
                    P.op("dve", lambda e, R=R, c0=c0: e.tensor_tensor(
                        out=ogT[:, :, c0:c0 + R], in0=ptp[:, 2, :].rearrange("p (c t) -> p c t", t=128)[:, :, 0:R], in1=szT[:, :, c0:c0 + R], op=ALU.mult),
                        reads=["trb"] + ["szT%d" % z for z in range(8)], writes=["ogT"])

            def zgate(zlist):
                for zc in zlist:
                    ch = CH_Z[zc // 4]
                    s = ring_state["loaded"][ch]
                    k = mm1()
                    for kc in range(KC):
                        P.op("pe", lambda e, s=s, kc=kc, k=k, zc=zc: e.matmul(
                            big[:, k, 0:T], lhsT=ring[:, s, kc, (zc % 4) * 128:(zc % 4 + 1) * 128], rhs=h2[:, kc, 0:T],
                            start=(kc == 0), stop=(kc == KC - 1)), reads=["ring%d" % s, "h2_%d" % kc], writes=["mm%d" % k])
                    for (c0, n, bt, ci) in segs:
                        P.op("act", lambda e, k=k, zc=zc, c0=c0, n=n, bt=bt: e.activation(
                            out=szT[:, zc, c0:c0 + n], in_=big[:, k, c0:c0 + n], func=AF.Silu, bias=bias_z[:, zc, bt:bt + 1], scale=1.0),
                            reads=["mm%d" % k, "bias_z"], writes=["szT%d" % zc])
                    if zc == 3:
                        ring_state["loaded"].pop(CH_Z[0]); ring_load(15)
                    if zc == 7:
                        ring_state["loaded"].pop(CH_Z[1]); ring_load(16)
            zgate(range(0, 4))
            pj0(0)
            pj1(0)
            zgate(range(4, 8))
            pj2(0)
            nu = len(units)
            DB, DC = 2, 3

            def proj_pieces(step):
                for nb in range(1, len(blocks)):
                    d = step - (4 * nb - 5)
                    if d == 0:
                        pjkv(nb)
                    elif d == 1:
                        pjq(nb, 0)
                    elif d == 2:
                        pjq(nb, 1)
                        pj1(nb)
                    elif d == 3:
                        pj2(nb)

            proj_pieces(-1)
            attn_mode[0] = True
            for s_ in range(nu + DC):
                if 0 <= s_ - DB < nu:
                    stB(s_ - DB)
                if s_ < nu:
                    stA(s_)
                if 0 <= s_ - DC < nu:
                    stC(s_ - DC)
                proj_pieces(s_)
            attn_mode[0] = False

            if blocks[0]["mask"]:
                P.op("pool", lambda e: e.tensor_copy(out=KT2[:, :, 0:128], in_=KT2[:, :, 512:640]), reads=["KT2_3"], writes=["KT2"])
                P.op("pool", lambda e: e.tensor_copy(out=Vaug[:, 0, :, 0:64], in_=Vaug[:, 4, :, 0:64]), reads=["Vaug4"], writes=["Vaug0"])
            ring_state["loaded"].pop(CH_KV); ring_state["loaded"].pop(CH_Q[0]); ring_state["loaded"].pop(CH_Q[1])
            if not last_tile_out:
                ring_load(0); ring_load(1); ring_load(2)

            if stage < 8:
                return
            na2 = NormAcc(T)
            for oc in range(8):
                ch = CH_O[oc // 4]
                s = ring_state["loaded"][ch]
                k = mm1()
                for ic in range(KC):
                    P.op("pe", lambda e, s=s, ic=ic, k=k, oc=oc: e.matmul(
                        big[:, k, 0:T], lhsT=ring[:, s, ic, (oc % 4) * 128:(oc % 4 + 1) * 128], rhs=ogT[:, ic, 0:T],
                        start=(ic == 0), stop=(ic == KC - 1)), reads=["ring%d" % s, "ogT"], writes=["mm%d" % k])
                for (c0, n, bt, ci) in segs:
                    P.op("dve", lambda e, k=k, oc=oc, c0=c0, n=n, bt=bt: e.scalar_tensor_tensor(
                        out=xT[:, oc, c0:c0 + n], in0=big[:, k, c0:c0 + n], scalar=gateB[:, oc, bt:bt + 1], in1=xT[:, oc, c0:c0 + n],
                        op0=ALU.mult, op1=ALU.add), reads=["mm%d" % k, "modT"] + xk(oc), writes=xk(oc))
                na2.add(oc, xk(oc))
                if oc == 3:
                    ring_state["loaded"].pop(CH_O[0])
                    if not last_tile_out:
                        ring_load(3)
                if oc == 7:
                    ring_state["loaded"].pop(CH_O[1])
                    if not last_tile_out:
                        ring_load(4)

            if stage < 9:
                return
            na2.finish()
            for kc in range(KC):
                P.op("dve", lambda e, kc=kc: e.scalar_tensor_tensor(
                    out=xT[:, kc, 0:T], in0=xT[:, kc, 0:T], scalar=gfT[:, kc:kc + 1], in1=rstd[:, 0:T], op0=ALU.mult, op1=ALU.mult),
                    reads=xk(kc) + ["rstd", "vecT"], writes=xk(kc))
            for (c0, rows, src, dst) in ioblocks:
                sl = yout_ctr[0] % 2
                yout_ctr[0] += 1
                for half in range(2):
                    k = mm1()
                    for q4 in range(4):
                        kc = half * 4 + q4
                        P.op("pe", lambda e, rows=rows, kc=kc, q4=q4, k=k, c0=c0: e.transpose(
                            out=big[0:rows, k, q4 * 128:(q4 + 1) * 128], in_=xT[:, kc, c0:c0 + rows], identity=identf[:]),
                            reads=xk(kc) + ["identf"], writes=["mm%d" % k])
                    P.op("act", lambda e, half=half, k=k, rows=rows, sl=sl: e.activation(
                        out=ytok[0:rows, sl, half * 512:(half + 1) * 512], in_=big[0:rows, k, :], func=AF.Copy),
                        reads=["mm%d" % k], writes=["ytok%d_%d" % (sl, half)])
                ok = "o_y_%d" % yout_ctr[0]
                P.op("sp", lambda e, rows=rows, sl=sl, dst=dst: e.dma_start(out=dst, in_=ytok[0:rows, sl, :]),
                     reads=["ytok%d_0" % sl, "ytok%d_1" % sl], writes=[ok], dma="yout%d" % sl)
                outkeys.append(ok)

        ring_state["loaded"].clear()
        for ch in (0, 1, 2, 3, 4):
            ring_load(ch)
        tiles = []
        for t in range(NT):
            segs = [(0, 512, 0, 0)]
            iob = [(b * 128, 128, x_p[t * 512 + b * 128: t * 512 + (b + 1) * 128, :], y_p[t * 512 + b * 128: t * 512 + (b + 1) * 128, :]) for b in range(4)]
            blocks = []
            for b in range(4):
                last = (t == NT - 1 and b == 3)
                blocks.append(dict(c0=b * 128, R=128, n=0, kbase=b * 128, hslot=b, oslot=b + 1, posblk=t * 4 + b, mask=True,
                                   khk=("KT2" if b == 0 else "KT2_%d" % (b - 1)), kok="KT2_%d" % b,
                                   first=(t == 0 and b == 0), kout=(k_p[:, :] if last else None), vout=(v_p[:, :] if last else None)))
            tiles.append((512, segs, iob, blocks))
        if with_sample:
            segs = [(0, 16, 1, 1), (16, 16, 2, 2)]
            iob = [(0, 32, x_s[:, :], y_s[:, :])]
            blocks = []
            for s_ in range(2):
                blocks.append(dict(c0=16 * s_, R=16, n=1 + s_, kbase=256 * s_, hslot=2 * s_, oslot=2 * s_ + 1, posblk=64, mask=False,
                                   khk=("KT2" if s_ == 0 else "KT2_1"), kok=("KT2_0" if s_ == 0 else "KT2_2"),
                                   first=False, kout=k_s[s_, 112:128, :], vout=v_s[s_, 112:128, :]))
            tiles.append((32, segs, iob, blocks))

        if stage < 1:
            tiles = []
        for ti, (T, segs, iob, blocks) in enumerate(tiles):
            is_sample = with_sample and ti == len(tiles) - 1
            if is_sample:
                for s_ in range(2):
                    P.op("sp", lambda e, s_=s_: e.dma_start(out=kf[:, s_, :], in_=cache_k[s_]), writes=["kf%d" % s_], dma="cink%d" % s_)
                    P.op("sp", lambda e, s_=s_: e.dma_start(out=vf[:, :], in_=cache_v[s_]), writes=["vf"], dma="cinv%d" % s_)
                    P.op("dve", lambda e, s_=s_: e.tensor_copy(out=Vaug[:, 2 * s_, :, 0:64], in_=vf[:, :].rearrange("p (h d) -> p h d", d=64)),
                         reads=["vf"], writes=["Vaug%d" % (2 * s_)])
                    P.op("act", lambda e, s_=s_: e.activation(
                        out=kb2[:], in_=kf[:, s_, :].rearrange("p (h d) -> p h d", d=64).unsqueeze(2).to_broadcast([128, 4, 2, 64]), func=AF.Copy),
                        reads=["kf%d" % s_], writes=["kb2"])
                    for h in range(4):
                        P.op("pe", lambda e, h=h: e.transpose(out=ptp[:, 2, h * 128:(h + 1) * 128], in_=kb2[:, h].rearrange("p t d -> p (t d)"),
                                                               identity=identb[:]), reads=["kb2", "identb"], writes=["trb"])
                    P.op("dve", lambda e, s_=s_: e.tensor_copy(out=KT2[:, :, 256 * s_:256 * s_ + 128],
                                                               in_=ptp[:, 2, 0:512].rearrange("p (h t) -> p h t", t=128)),
                         reads=["trb"], writes=["KT2"] if s_ == 0 else ["KT2_1"])
            run_tile(T, segs, iob, blocks, ti, (tiles[ti + 1][2] if ti + 1 < len(tiles) else None), last_tile_out=(ti == len(tiles) - 1))
            if ti == NT - 1:
                for t_ in range(2):
                    P.op("sp", lambda e, t_=t_: e.dma_start(out=conv_p[t_].rearrange("(c p) -> p c", p=128), in_=carry[:, 0, :, t_]),
                         reads=["carry0"], writes=["o_convp%d" % t_], dma="outm")
                    outkeys.append("o_convp%d" % t_)
            if is_sample:
                for s_ in range(2):
                    for t_ in range(2):
                        P.op("sp", lambda e, s_=s_, t_=t_: e.dma_start(out=conv_s[s_, t_].rearrange("(c p) -> p c", p=128), in_=carry[:, 1 + s_, :, t_]),
                             reads=["carry%d" % (1 + s_)], writes=["o_convs%d_%d" % (s_, t_)], dma="outm")
                        outkeys.append("o_convs%d_%d" % (s_, t_))

        P.emit(final_wait_keys=outkeys)
        build_program.stats = P.stats
    return nc


_CACHE = {}


def _get_program(SEQ):
    if SEQ not in _CACHE:
        _CACHE[SEQ] = build_program(SEQ)
    return _CACHE[SEQ]


def kernel(x_prompt, x_sample, c_prompt, c_sample, state_conv, cache_k, cache_v,
           g_a, w_mod_a, b_mod_a, w_in_a, conv_w_a, w_out_a,
           g_kv, w_mod_kv, b_mod_kv, w_kv,
           g_b, w_mod_b, b_mod_b, w_qz_b, w_o_b, sinks_b, g_final):
    f = lambda a: np.ascontiguousarray(np.asarray(a, dtype=np.float32))
    x_prompt, x_sample = f(x_prompt), f(x_sample)
    B, SEQ, _ = x_prompt.shape
    nc = _get_program(SEQ)
    shared = {
        "g_a": f(g_a)[0], "w_mod_a": f(w_mod_a)[0], "b_mod_a": f(b_mod_a)[0], "w_in": f(w_in_a)[0],
        "conv_w": f(conv_w_a)[0], "w_out": f(w_out_a)[0], "g_kv": f(g_kv), "w_mod_kv": f(w_mod_kv),
        "b_mod_kv": f(b_mod_kv), "w_kv": f(w_kv), "g_b": f(g_b)[0], "w_mod_b": f(w_mod_b)[0],
        "b_mod_b": f(b_mod_b)[0], "w_qz": f(w_qz_b)[0], "w_o": f(w_o_b)[0], "sinks": f(sinks_b)[0], "g_f": f(g_final),
    }
    c_prompt, c_sample, state_conv = f(c_prompt), f(c_sample), f(state_conv)
    cache_k, cache_v = f(cache_k), f(cache_v)
    in_maps = []
    for i in range(NCORES):
        m = dict(shared)
        m["x_p"] = x_prompt[i]
        m["x_s"] = x_sample[2 * i:2 * i + 2].reshape(32, D)
        m["cmat"] = np.ascontiguousarray(np.stack([c_prompt[i], c_sample[2 * i], c_sample[2 * i + 1]], axis=0))
        m["st_conv"] = np.ascontiguousarray(state_conv[0, 2 * i:2 * i + 2])
        m["cache_k"] = np.ascontiguousarray(cache_k[2 * i:2 * i + 2].reshape(2, 128, 256))
        m["cache_v"] = np.ascontiguousarray(cache_v[2 * i:2 * i + 2].reshape(2, 128, 256))
        in_maps.append(m)
    res = run_bass_kernel_spmd(nc, in_maps, core_ids=list(range(NCORES)))
    r = res.results
    y_prompt = np.stack([r[i]["y_p"] for i in range(NCORES)], axis=0)
    y_sample = np.concatenate([r[i]["y_s"].reshape(2, 16, D) for i in range(NCORES)], axis=0)
    conv_p = np.stack([r[i]["conv_p"] for i in range(NCORES)], axis=0)[None]
    conv_s = np.concatenate([r[i]["conv_s"] for i in range(NCORES)], axis=0)[None]
    k_p = np.stack([r[i]["k_p"].reshape(128, 4, 64) for i in range(NCORES)], axis=0)
    v_p = np.stack([r[i]["v_p"].reshape(128, 4, 64) for i in range(NCORES)], axis=0)
    k_s = np.concatenate([r[i]["k_s"].reshape(2, 128, 4, 64) for i in range(NCORES)], axis=0)
    v_s = np.concatenate([r[i]["v_s"].reshape(2, 128, 4, 64) for i in range(NCORES)], axis=0)
    return (y_prompt, y_sample, conv_p, conv_s, k_p, v_p, k_s, v_s)
```
